# Optimizing a Trainium2 kernel written in Bass

```python
import jax
import jax.numpy as jnp
from jax import lax
import numpy as np

D_MODEL = 1024
BATCH = 4
SEQ = 4096
DEPTH = 2

CHUNK = 64
D_PLE = 256
N_MIXERS = 4
D_GROUP = D_MODEL // N_MIXERS
D_MIX = N_MIXERS * D_GROUP
HEAD_DIM = 64
N_GROUP_HEADS = D_GROUP // HEAD_DIM
GMLP_BLOCK = 128
DN_CONV = 4
RW_DECAY_LORA = 64
RW_AAA_LORA = 64
RW_GATE_LORA = 128
RW_LNX_EPS = 64e-5
CONF_CONV = 31
D_FF = 256 * round(8 * D_MODEL / (3 * 256))
FFN_CONV = 3
NORM_EPS = 1e-6
LN_EPS = 1e-5

A_COLS = 2 * D_GROUP
B_COLS = 4 * D_GROUP + 2 * N_GROUP_HEADS
C_COLS = 3 * D_GROUP + RW_DECAY_LORA + RW_AAA_LORA + RW_GATE_LORA
D_COLS = 2 * D_GROUP
D_IN = A_COLS + B_COLS + C_COLS + D_COLS

kernel_name = 'hybrid_parallel_group_streaming_encoder'


def _split(x, sizes):
    idx = [int(i) for i in np.cumsum(sizes)[:-1]]
    return jnp.split(x, idx, axis=-1)


def rms_norm(x, g):
    xf = x.astype(jnp.float32)
    y = xf * lax.rsqrt(jnp.mean(xf * xf, axis=-1, keepdims=True) + NORM_EPS)
    return (y * g.astype(jnp.float32)).astype(x.dtype)


def layer_norm(x, g, b, eps):
    xf = x.astype(jnp.float32)
    mu = jnp.mean(xf, axis=-1, keepdims=True)
    var = jnp.mean(jnp.square(xf - mu), axis=-1, keepdims=True)
    y = (xf - mu) * lax.rsqrt(var + eps) * g.astype(jnp.float32) + b.astype(jnp.float32)
    return y.astype(x.dtype)


def l2_normalize(x):
    xf = x.astype(jnp.float32)
    return xf * lax.rsqrt(jnp.sum(xf * xf, axis=-1, keepdims=True) + 1e-6)


def causal_depthwise_conv(x, w):
    K, C = w.shape
    return lax.conv_general_dilated(
        x, w[:, None, :].astype(x.dtype), window_strides=(1,), padding=[(K - 1, 0)],
        dimension_numbers=('NWC', 'WIO', 'NWC'), feature_group_count=C)


def _heads(t):
    Bn, S, _ = t.shape
    return t.reshape(Bn, S, N_GROUP_HEADS, HEAD_DIM)


def gmlp_spatial_gating(z, v_g, v_b, w_s, b_s):
    Bn, S, _ = z.shape
    u, v = jnp.split(jax.nn.gelu(z), 2, axis=-1)
    v = layer_norm(v, v_g, v_b, LN_EPS)
    v = v.reshape(Bn, S // GMLP_BLOCK, GMLP_BLOCK, N_GROUP_HEADS, HEAD_DIM)
    pos = jnp.arange(GMLP_BLOCK) // CHUNK
    mask = (pos[None, :] <= pos[:, None]).astype(w_s.dtype)
    sv = jnp.einsum('hij,bnjhd->bnihd', w_s * mask, v) + b_s.T[None, None, :, :, None]
    return u * sv.reshape(Bn, S, D_GROUP)


def chunk_gated_delta_rule(q, k, v, g, beta):
    Bn, S, H, Dk = q.shape
    Dv = v.shape[-1]
    N = S // CHUNK

    def chunks(t):
        t = t.reshape((Bn, N, CHUNK, H) + t.shape[3:])
        return jnp.moveaxis(t, 3, 1)

    q, k, v, beta = chunks(q), chunks(k), chunks(v), chunks(beta)
    g = jnp.cumsum(chunks(g), axis=-1)
    incl = jnp.tril(jnp.ones((CHUNK, CHUNK), dtype=bool))
    strict = jnp.tril(jnp.ones((CHUNK, CHUNK), dtype=bool), -1)
    diff = g[..., :, None] - g[..., None, :]
    decay = jnp.where(incl, jnp.exp(jnp.where(incl, diff, 0.0)), 0.0)
    kb = k * beta[..., None]
    L = jnp.where(strict, jnp.einsum('bhnid,bhnjd->bhnij', kb, k) * decay, 0.0)
    eye = jnp.eye(CHUNK, dtype=jnp.float32)
    rhs = jnp.concatenate([v * beta[..., None], kb * jnp.exp(g)[..., None]], axis=-1)
    sol = lax.linalg.triangular_solve(eye + L, rhs, left_side=True, lower=True)
    u, w = sol[..., :Dv], sol[..., Dv:]
    attn = jnp.einsum('bhnid,bhnjd->bhnij', q, k) * decay
    g_last = g[..., -1]
    k_dec = k * jnp.exp(g_last[..., None] - g)[..., None]
    q_dec = q * jnp.exp(g)[..., None]

    def step(state, inp):
        q_c, k_c, u_c, w_c, a_c, gl = inp
        v_new = u_c - jnp.einsum('bhck,bhkv->bhcv', w_c, state)
        o = jnp.einsum('bhck,bhkv->bhcv', q_c, state) + jnp.einsum('bhij,bhjv->bhiv', a_c, v_new)
        state = state * jnp.exp(gl)[..., None, None] + jnp.einsum('bhck,bhcv->bhkv', k_c, v_new)
        return state, o

    xs = tuple(jnp.moveaxis(t, 2, 0) for t in (q_dec, k_dec, u, w, attn, g_last))
    state0 = jnp.zeros((Bn, H, Dk, Dv), jnp.float32)
    _, o = lax.scan(step, state0, xs)
    o = jnp.moveaxis(jnp.moveaxis(o, 0, 2), 1, 3)
    return o.reshape(Bn, S, H, Dv)


def gated_deltanet(z, conv_w, a_log, dt_bias, o_g):
    q, k, v, gate, beta_raw, alpha_raw = _split(z, [D_GROUP] * 4 + [N_GROUP_HEADS] * 2)
    qkv = jax.nn.silu(causal_depthwise_conv(jnp.concatenate([q, k, v], axis=-1), conv_w))
    q, k, v = jnp.split(qkv, 3, axis=-1)
    q = l2_normalize(_heads(q)) * (HEAD_DIM ** -0.5)
    k = l2_normalize(_heads(k))
    v = _heads(v).astype(jnp.float32)
    beta = jax.nn.sigmoid(beta_raw.astype(jnp.float32))
    g = -jnp.exp(a_log.astype(jnp.float32)) * jax.nn.softplus(
        alpha_raw.astype(jnp.float32) + dt_bias.astype(jnp.float32))
    o = chunk_gated_delta_rule(q, k, v, g, beta)
    o = rms_norm(o, o_g) * jax.nn.silu(_heads(gate).astype(jnp.float32))
    Bn, S = z.shape[:2]
    return o.reshape(Bn, S, D_GROUP).astype(z.dtype)


def rwkv7_recurrence(r, w, k, v, a, b):
    Bn, S, H, D = r.shape

    def step(state, inp):
        r_t, w_t, k_t, v_t, a_t, b_t = inp
        sa = jnp.einsum('bhij,bhj->bhi', state, a_t)
        state = (state * w_t[:, :, None, :] + sa[..., None] * b_t[:, :, None, :]
                 + v_t[..., None] * k_t[:, :, None, :])
        return state, jnp.einsum('bhij,bhj->bhi', state, r_t)

    xs = tuple(jnp.moveaxis(t, 1, 0) for t in (r, w, k, v, a, b))
    _, y = lax.scan(step, jnp.zeros((Bn, H, D, D), jnp.float32), xs)
    return jnp.moveaxis(y, 0, 1)


def rwkv7_time_mix(P, mu, w0, w2, a0, a2, g2, k_k, k_a, r_k, lnx_g, lnx_b):
    Bn, S, _ = P.shape
    P_prev = jnp.pad(P, ((0, 0), (1, 0), (0, 0)))[:, :-1]
    P = P + (P_prev - P) * mu
    r, k, v, xw, xa, xg = _split(P, [D_GROUP] * 3 + [RW_DECAY_LORA, RW_AAA_LORA, RW_GATE_LORA])
    w = -jax.nn.softplus(-(w0 + jnp.tanh(xw) @ w2)) - 0.5
    a = jax.nn.sigmoid(a0 + xa @ a2)
    gate = jax.nn.sigmoid(xg) @ g2
    kk = l2_normalize(_heads(k * k_k))
    k = k * (1.0 + (a - 1.0) * k_a)
    r_h = _heads(r).astype(jnp.float32)
    k_h = _heads(k).astype(jnp.float32)
    v_h = _heads(v).astype(jnp.float32)
    a_h = _heads(a).astype(jnp.float32)
    decay = jnp.exp(-jnp.exp(_heads(w).astype(jnp.float32)))
    y = rwkv7_recurrence(r_h, decay, k_h, v_h, -kk, kk * a_h)
    y = layer_norm(y, lnx_g.reshape(N_GROUP_HEADS, HEAD_DIM), lnx_b.reshape(N_GROUP_HEADS, HEAD_DIM), RW_LNX_EPS)
    y = y + jnp.sum(r_h * k_h * r_k.astype(jnp.float32), axis=-1, keepdims=True) * v_h
    return (y.reshape(Bn, S, D_GROUP).astype(P.dtype) * gate)


def conformer_conv(z, conv_w, conv_b, ln_g, ln_b):
    z1, z2 = jnp.split(z, 2, axis=-1)
    h = z1 * jax.nn.sigmoid(z2)
    h = causal_depthwise_conv(h, conv_w) + conv_b
    h = layer_norm(h, ln_g, ln_b, LN_EPS)
    return jax.nn.silu(h)


def setup_inputs(seed: int = 0) -> dict:
    key = jax.random.key(seed)
    ks = iter(jax.random.split(key, 40))
    f32 = jnp.float32

    def nrm(shape, scale):
        return jax.random.normal(next(ks), shape, f32) * scale

    def gain(shape):
        return 1.0 + nrm(shape, 0.02)

    L, H = DEPTH, N_GROUP_HEADS
    dt = jnp.exp(jax.random.uniform(next(ks), (L, H), f32, float(np.log(1e-3)), float(np.log(1e-1))))
    return {
        'x': nrm((BATCH, SEQ, D_MODEL), 1.0),
        'p': nrm((DEPTH, BATCH, SEQ, D_PLE), 1.0),
        'norm_mix_g': gain((L, D_MODEL)),
        'w_in': nrm((L, D_MODEL, D_IN), D_MODEL ** -0.5),
        'gmlp_v_g': gain((L, D_GROUP)),
        'gmlp_v_b': nrm((L, D_GROUP), 0.02),
        'gmlp_w_s': nrm((L, H, GMLP_BLOCK, GMLP_BLOCK), GMLP_BLOCK ** -0.5),
        'gmlp_b_s': 1.0 + nrm((L, H, GMLP_BLOCK), 0.1),
        'dn_conv_w': nrm((L, DN_CONV, 3 * D_GROUP), DN_CONV ** -0.5),
        'dn_a_log': jnp.log(jax.random.uniform(next(ks), (L, H), f32, 1.0, 16.0)),
        'dn_dt_bias': dt + jnp.log(-jnp.expm1(-dt)),
        'dn_o_g': gain((L, HEAD_DIM)),
        'rw_mu': jax.random.uniform(next(ks), (L, C_COLS), f32, 0.0, 1.0),
        'rw_w0': jax.random.uniform(next(ks), (L, D_GROUP), f32, -6.0, -1.0),
        'rw_w2': nrm((L, RW_DECAY_LORA, D_GROUP), 0.5 * RW_DECAY_LORA ** -0.5),
        'rw_a0': nrm((L, D_GROUP), 0.1),
        'rw_a2': nrm((L, RW_AAA_LORA, D_GROUP), RW_AAA_LORA ** -0.5),
        'rw_g2': nrm((L, RW_GATE_LORA, D_GROUP), RW_GATE_LORA ** -0.5),
        'rw_k_k': 0.85 + nrm((L, D_GROUP), 0.05),
        'rw_k_a': 1.0 + nrm((L, D_GROUP), 0.05),
        'rw_r_k': nrm((L, H, HEAD_DIM), 0.1),
        'rw_lnx_g': gain((L, D_GROUP)),
        'rw_lnx_b': nrm((L, D_GROUP), 0.02),
        'cf_conv_w': nrm((L, CONF_CONV, D_GROUP), CONF_CONV ** -0.5),
        'cf_conv_b': nrm((L, D_GROUP), 0.02),
        'cf_ln_g': gain((L, D_GROUP)),
        'cf_ln_b': nrm((L, D_GROUP), 0.02),
        'w_out': nrm((L, D_MIX, D_MODEL), D_MIX ** -0.5),
        'norm_ffn_g': gain((L, D_MODEL)),
        'w_ffn_gate': nrm((L, D_MODEL, D_FF), D_MODEL ** -0.5),
        'w_ffn_up': nrm((L, D_MODEL, D_FF), D_MODEL ** -0.5),
        'ffn_conv_w': nrm((L, FFN_CONV, D_FF), FFN_CONV ** -0.5),
        'w_ffn_down': nrm((L, D_FF, D_MODEL), D_FF ** -0.5),
        'norm_ple_g': gain((L, D_MODEL)),
        'w_ple_gate': nrm((L, D_MODEL, D_MODEL), D_MODEL ** -0.5),
        'w_ple_proj': nrm((L, D_PLE, D_MODEL), D_PLE ** -0.5),
        'final_norm_g': gain((D_MODEL,)),
    }


def reference(x, p, norm_mix_g, w_in, gmlp_v_g, gmlp_v_b, gmlp_w_s, gmlp_b_s,
              dn_conv_w, dn_a_log, dn_dt_bias, dn_o_g,
              rw_mu, rw_w0, rw_w2, rw_a0, rw_a2, rw_g2, rw_k_k, rw_k_a, rw_r_k, rw_lnx_g, rw_lnx_b,
              cf_conv_w, cf_conv_b, cf_ln_g, cf_ln_b, w_out,
              norm_ffn_g, w_ffn_gate, w_ffn_up, ffn_conv_w, w_ffn_down,
              norm_ple_g, w_ple_gate, w_ple_proj, final_norm_g):
    h = x
    for i in range(DEPTH):
        hn = rms_norm(h, norm_mix_g[i])
        z_a, z_b, z_c, z_d = _split(hn @ w_in[i], [A_COLS, B_COLS, C_COLS, D_COLS])
        o_a = gmlp_spatial_gating(z_a, gmlp_v_g[i], gmlp_v_b[i], gmlp_w_s[i], gmlp_b_s[i])
        o_b = gated_deltanet(z_b, dn_conv_w[i], dn_a_log[i], dn_dt_bias[i], dn_o_g[i])
        o_c = rwkv7_time_mix(z_c, rw_mu[i], rw_w0[i], rw_w2[i], rw_a0[i], rw_a2[i], rw_g2[i],
                             rw_k_k[i], rw_k_a[i], rw_r_k[i], rw_lnx_g[i], rw_lnx_b[i])
        o_d = conformer_conv(z_d, cf_conv_w[i], cf_conv_b[i], cf_ln_g[i], cf_ln_b[i])
        h = h + jnp.concatenate([o_a, o_b, o_c, o_d], axis=-1) @ w_out[i]
        hn = rms_norm(h, norm_ffn_g[i])
        gate = causal_depthwise_conv(hn @ w_ffn_gate[i], ffn_conv_w[i])
        h = h + (jax.nn.silu(gate) * (hn @ w_ffn_up[i])) @ w_ffn_down[i]
        hn = rms_norm(h, norm_ple_g[i])
        h = h + (p[i] @ w_ple_proj[i]) * jax.nn.sigmoid(hn @ w_ple_gate[i])
    return rms_norm(h, final_norm_g)
```

```python
import numpy as np
import concourse.bass as bass
import concourse.mybir as mybir

F32 = mybir.dt.float32
BF16 = mybir.dt.bfloat16
ALU = mybir.AluOpType
AF = mybir.ActivationFunctionType
AX = mybir.AxisListType

ENGS = ("pe", "act", "dve", "pool", "sp")
N_DMA_SEM = 24


class Prog:
    def __init__(self, nc):
        self.nc = nc
        self.q = {e: [] for e in ENGS}
        self.cnt = {e: 0 for e in ENGS}
        self.seen = {e: {} for e in ENGS}
        self.acc = {}
        self.esem = {}
        self.dsem = {"sp": [], "pool": [], "act": []}
        self.dma_i = 0
        self.dma_k = {"sp": 0, "pool": 0, "act": 0}
        self.dma_cnt = {}
        self.out_tokens = []
        self.n_wait = 0
        self.track_dram = set()

    def setup_sems(self, stack):
        for e in ("pe", "act", "dve", "pool"):
            self.esem[e] = stack.enter_context(self.nc.semaphore("s_" + e))
        for q, n in (("sp", 16), ("pool", 8)):
            for i in range(n):
                self.dsem[q].append(stack.enter_context(self.nc.semaphore("s_dma_%s%d" % (q, i))))

    @staticmethod
    def _region(ap):
        t = ap.tensor
        shp = list(t.shape)
        rowlen = 1
        for s in shp[1:]:
            rowlen *= s
        off = int(ap.offset)
        apl = ap.ap
        p0 = off // rowlen
        f0 = off % rowlen
        pstep, pcnt = apl[0]
        if pstep == 0:
            p1 = p0 + 1
        else:
            p1 = p0 + (pcnt - 1) * (pstep // rowlen) + 1
        ext = 0
        for st, c in apl[1:]:
            ext += (c - 1) * abs(st)
        if str(ap.space) == "PSUM":
            return (ap.name, p0, p1, 0, rowlen)
        return (ap.name, p0, p1, f0, f0 + ext + 1)

    def _reg2(self, ap):
        if str(ap.space) == "DRAM":
            if ap.name in self.track_dram:
                return (ap.name, 0, 1, 0, 1)
            return None
        return self._region(ap)

    def _need(self, eng, tok, needs):
        if tok is None:
            return
        sem, val, teng, tidx = tok
        if teng == eng:
            if eng == "pe":
                return
        key = id(sem)
        if self.seen[eng].get(key, 0) >= val:
            return
        cur = needs.get(key)
        if cur is None or cur[1] < val:
            needs[key] = (sem, val)

    def emit(self, eng, outs, ins, fn, is_dma=False, is_output=False):
        needs = {}
        regs_in = [r for r in (self._reg2(a) for a in ins if a is not None) if r is not None]
        regs_out = [r for r in (self._reg2(a) for a in outs if a is not None) if r is not None]
        psum_in = [r for r, a in zip(regs_in, [a for a in ins if a is not None and self._reg2(a) is not None])
                   if str(a.space) == "PSUM"]
        if psum_in:
            regs_in = [r for r in regs_in if r not in psum_in]
            regs_out = regs_out + psum_in
        for (name, p0, p1, f0, f1) in regs_in:
            for r in self.acc.get(name, ()):
                if r[5] and r[0] < p1 and p0 < r[1] and r[2] < f1 and f0 < r[3]:
                    self._need(eng, r[4], needs)
        for (name, p0, p1, f0, f1) in regs_out:
            for r in self.acc.get(name, ()):
                if r[0] < p1 and p0 < r[1] and r[2] < f1 and f0 < r[3]:
                    self._need(eng, r[4], needs)
        if is_dma:
            j = self.dma_k[eng] % len(self.dsem[eng])
            self.dma_k[eng] += 1
            self.dma_i += 1
            sem = self.dsem[eng][j]
            prev = self.dma_cnt.get((eng, j), 0)
            if prev > 0:
                if self.seen[eng].get(id(sem), 0) < 16 * prev:
                    needs[id(sem)] = (sem, 16 * prev)
            self.dma_cnt[(eng, j)] = prev + 1
            tok = (sem, 16 * (prev + 1), "dma%s%d" % (eng, j), 0)
            inc = (sem, 16)
        else:
            idx = self.cnt[eng]
            sem = self.esem[eng]
            tok = (sem, idx + 1, eng, idx)
            inc = (sem, 1)
            self.cnt[eng] = idx + 1
        for key, (s, v) in needs.items():
            self.q[eng].append(("w", s, v))
            self.seen[eng][key] = v
            self.n_wait += 1
        self.q[eng].append(("op", fn, inc))
        if is_output:
            self.out_tokens.append(tok)
        for (name, p0, p1, f0, f1) in regs_out:
            lst = self.acc.setdefault(name, [])
            lst[:] = [r for r in lst if not (p0 <= r[0] and r[1] <= p1 and f0 <= r[2] and r[3] <= f1)]
            lst.append((p0, p1, f0, f1, tok, True, eng))
        for (name, p0, p1, f0, f1) in regs_in:
            lst = self.acc.setdefault(name, [])
            if not is_dma:
                lst[:] = [r for r in lst if not ((not r[5]) and r[6] == eng and p0 <= r[0] and r[1] <= p1
                                                 and f0 <= r[2] and r[3] <= f1)]
            lst.append((p0, p1, f0, f1, tok, False, eng if not is_dma else "dma"))

    def mm(self, out, lhsT, rhs, start=True, stop=True):
        self.emit("pe", [out], [lhsT, rhs],
                  lambda e: e.matmul(out, lhsT, rhs, start=start, stop=stop))

    def tr(self, out, in_, ident):
        self.emit("pe", [out], [in_, ident], lambda e: e.transpose(out, in_, ident))

    def act(self, out, in_, func, bias=None, scale=None, accum_out=None, eng="act"):
        kw = {}
        ins = [in_]
        if bias is not None:
            kw["bias"] = bias
            if not isinstance(bias, (int, float)):
                ins.append(bias)
        if scale is not None:
            kw["scale"] = scale
            if not isinstance(scale, (int, float)):
                ins.append(scale)
        outs = [out]
        if accum_out is not None:
            kw["accum_out"] = accum_out
            outs.append(accum_out)
        self.emit("act", outs, ins, lambda e: e.activation(out, in_, func, **kw))

    def tt(self, out, in0, in1, op, eng="dve"):
        self.emit(eng, [out], [in0, in1], lambda e: e.tensor_tensor(out, in0, in1, op))

    def ts(self, out, in0, s1, op0, s2=None, op1=None, eng="dve", accum_out=None):
        ins = [in0] + [s for s in (s1, s2) if s is not None and not isinstance(s, (int, float))]
        outs = [out] + ([accum_out] if accum_out is not None else [])
        kw = {}
        if op1 is not None:
            kw["op1"] = op1
        if accum_out is not None:
            kw["accum_out"] = accum_out
        self.emit(eng, outs, ins, lambda e: e.tensor_scalar(out, in0, s1, s2, op0, **kw))

    def stt(self, out, in0, scalar, in1, op0, op1, eng="dve"):
        eng = "dve"
        ins = [in0, in1] + ([scalar] if not isinstance(scalar, (int, float)) else [])
        self.emit(eng, [out], ins, lambda e: e.scalar_tensor_tensor(out, in0, scalar, in1, op0, op1))

    def copy(self, out, in_, eng="dve"):
        if eng == "act":
            self.emit("act", [out], [in_], lambda e: e.copy(out, in_))
        else:
            self.emit(eng, [out], [in_], lambda e: e.tensor_copy(out, in_))

    def memset(self, ap, val, eng="dve"):
        self.emit(eng, [ap], [], lambda e: e.memset(ap, val))

    def reduce(self, out, in_, op, axis=AX.X, eng="dve"):
        self.emit(eng, [out], [in_], lambda e: e.tensor_reduce(out, in_, axis, op))

    def bn_stats(self, out, in_):
        self.emit("dve", [out], [in_], lambda e: e.bn_stats(out, in_))

    def bn_aggr(self, out, in_):
        self.emit("dve", [out], [in_], lambda e: e.bn_aggr(out, in_))

    def dma(self, out, in_, eng="sp", is_output=False):
        self.emit(eng, [out], [in_], lambda e: e.dma_start(out=out, in_=in_), is_dma=True, is_output=is_output)

    def dma_cast(self, out, in_):
        self.emit("pool", [out], [in_], lambda e: e.dma_start(out=out, in_=in_), is_dma=True)

    def emit_cc(self, outs, ins, fn, sem, inc):
        eng = "pool"
        needs = {}
        regs_in = [r for r in (self._reg2(a) for a in ins) if r is not None]
        regs_out = [r for r in (self._reg2(a) for a in outs) if r is not None]
        for (name, p0, p1, f0, f1) in regs_in:
            for r in self.acc.get(name, ()):
                if r[5] and r[0] < p1 and p0 < r[1] and r[2] < f1 and f0 < r[3]:
                    self._need(eng, r[4], needs)
        for (name, p0, p1, f0, f1) in regs_out:
            for r in self.acc.get(name, ()):
                if r[0] < p1 and p0 < r[1] and r[2] < f1 and f0 < r[3]:
                    self._need(eng, r[4], needs)
        self.cc_n = getattr(self, "cc_n", 0) + 1
        tok = (sem, inc * self.cc_n, "cc", 0)
        for key, (s_, v) in needs.items():
            self.q[eng].append(("w", s_, v))
            self.seen[eng][key] = v
        self.q[eng].append(("op", fn, (sem, inc)))
        for (name, p0, p1, f0, f1) in regs_out:
            lst = self.acc.setdefault(name, [])
            lst[:] = [r for r in lst if not (p0 <= r[0] and r[1] <= p1 and f0 <= r[2] and r[3] <= f1)]
            lst.append((p0, p1, f0, f1, tok, True, "cc"))
        for (name, p0, p1, f0, f1) in regs_in:
            self.acc.setdefault(name, []).append((p0, p1, f0, f1, tok, False, "cc"))

    def wait_token(self, eng, tok):
        needs = {}
        self._need(eng, tok, needs)
        for key, (s, v) in needs.items():
            self.q[eng].append(("w", s, v))
            self.seen[eng][key] = v

    def finish(self):
        for tok in self.out_tokens:
            self.wait_token("sp", tok)
        for e in ("pe", "act", "dve", "pool"):
            if self.cnt[e] > 0:
                self.q["sp"].append(("w", self.esem[e], self.cnt[e]))
        nc = self.nc
        q = self.q

        def replay(name, eng):
            for it in q[name]:
                if it[0] == "w":
                    eng.wait_ge(it[1], it[2])
                else:
                    ins = it[1](eng)
                    ins.then_inc(it[2][0], it[2][1])

        with nc.Block() as block:
            @block.tensor
            def _(e):
                replay("pe", e)

            @block.scalar
            def _(e):
                replay("act", e)

            @block.vector
            def _(e):
                replay("dve", e)

            @block.gpsimd
            def _(e):
                replay("pool", e)

            @block.sync
            def _(e):
                replay("sp", e)

from contextlib import ExitStack
from concourse.bass_utils import run_bass_kernel_spmd

D = 1024
S = 4096
TT = 256
NSUB = TT // 128
DIN = 3080
DFF = 2816
NKF = 22
NL = 2
LC = 202
LR = 1352
C_NMIX, C_NFFN, C_NPLE, C_DNW, C_MU, C_A0, C_KK, C_KA, C_RK = 0, 8, 16, 24, 48, 56, 58, 60, 62
C_CFW, C_CFB, C_CFG, C_CFLB, C_FFW, C_BST = 64, 126, 128, 130, 132, 198
C_FINAL = NL * LC
NCOL = C_FINAL + 8
R_VG, R_VB, R_ALOG, R_DTB, R_OG, R_W0, R_LNG, R_LNB = 0, 256, 512, 516, 520, 584, 840, 1096
NROW = NL * LR
M_WA2, M_G2, M_WST = 0, 256, 512
K_ID, K_MSL, K_MSU, K_MUI, K_BLK, K_GMT = 0, 128, 256, 384, 512, 640
EXPM05 = 0.6065306597126334


class _Stop(Exception):
    pass


def build_program(n_tiles=S // TT, n_layers=1, dbg=None, stop=None, prologue=True, n_cores=8):
    nc = bass.Bass("TRN2", target_bir_lowering=False)
    dt = nc.dram_tensor
    n_layers = 1
    x_d = dt("x", [S + TT, D], F32, kind="ExternalInput").ap()
    p_d = dt("p", [1, S + TT, 256], F32, kind="ExternalInput").ap()
    flag_d = dt("flag_d", [128, 1], F32, kind="ExternalInput").ap()
    send_d = dt("send_d", [128, 8 * TT], F32, kind="Internal").ap()
    recv_d = dt("recv_d", [256, 8 * TT], F32, kind="Internal").ap()
    cols_d = dt("cols_d", [128, NCOL], F32, kind="ExternalInput").ap()
    rows_d = dt("rows_d", [1, NROW], F32, kind="ExternalInput").ap()
    mats_d = dt("mats_d", [NL, 128, 1024], F32, kind="ExternalInput").ap()
    cst_d = dt("cst_d", [128, 768], F32, kind="ExternalInput").ap()
    wshapes = {"w_in": [D, DIN], "w_out": [D, D], "w_fg": [D, DFF], "w_fu": [D, DFF],
               "w_fd": [DFF, D], "w_pg": [D, D], "w_pp": [256, D]}
    w32 = {k: dt(k, [1] + v, F32, kind="ExternalInput").ap() for k, v in wshapes.items()}
    wbf = {k: dt(k + "_bf", [1] + v, BF16, kind="Internal").ap() for k, v in wshapes.items()}
    out_d = dt("out", [S, D], F32, kind="ExternalOutput").ap()
    dbg_d = None
    if dbg is not None:
        dbg_d = dt("dbg", list(dbg), F32, kind="ExternalOutput").ap()

    P = Prog(nc)
    for k in wshapes:
        P.track_dram.add(k + "_bf")
    P.track_dram.update(["send_d", "recv_d"])

    with ExitStack() as st:
        P.setup_sems(st)
        cc_sem = st.enter_context(nc.semaphore("s_cc"))
        sb = lambda n, s, d=F32: st.enter_context(nc.sbuf_tensor(n, s, d))
        banks = [st.enter_context(nc.psum_tensor("ps%d" % i, [128, 512], F32)) for i in range(8)]
        bank_i = [0]

        def pb():
            b = banks[bank_i[0] % 8]
            bank_i[0] += 1
            return b

        class Ring:
            def __init__(self, name, n, shape, d=F32):
                self.t = [sb("%s%d" % (name, i), shape, d) for i in range(n)]
                self.i = 0

            def next(self):
                t = self.t[self.i % len(self.t)]
                self.i += 1
                return t

        cols = sb("cols", [128, NCOL + 8])
        rows = sb("rows", [128, NROW])
        mats = sb("mats", [128, NL, 1024])
        cst = sb("cst", [128, 768])
        cc = sb("cc", [128, 8])
        ones_bf = sb("ones_bf", [128, 128], BF16)
        ones_f = sb("ones_f", [128, 128])
        o256_f = sb("o256_f", [128, 128])
        triS = sb("triS", [128, 256])
        wmT = sb("wmT", [128, NL, 512], BF16)
        nexpA = sb("nexpA", [128, NL, 4])
        hTs = [sb("hT%d" % i, [128, 8, TT]) for i in range(1)]
        hns = [sb("hn%d" % i, [128, 8, TT], BF16) for i in range(1)]
        hT = hTs[0]
        abuf = sb("abuf", [128, 11, TT], BF16)
        o_all = sb("o_all", [128, 8, TT], BF16)
        Sblk = sb("Sblk", [128, NL, 2, 128])
        Zblk = sb("Zblk", [128, NL, 2, 128])
        qkvh = sb("qkvh", [128, NL, 6, 3])
        zch = sb("zch", [128, NL, 8, 1])
        cfh = sb("cfh", [128, NL, 2, 30])
        ffh = sb("ffh", [128, NL, NKF, 2])
        qkvs = sb("qkvs", [128, 6, TT])
        gateS = sb("gateS", [128, NSUB, 256])
        betaT = sb("betaT", [128, NSUB, 4])
        nbetaT = sb("nbetaT", [128, NSUB, 4])
        gT = sb("gT", [128, NSUB, 4])
        zcm = sb("zcm", [128, 8, TT])
        asig = sb("asig", [128, 2, TT])
        gateC = sb("gateC", [128, 2, TT])
        thx = sb("thx", [128, TT])
        hbuf = sb("hbuf", [128, 2, 30 + TT])
        KbeP = sb("KbeP", [128, 4, 128])
        AtP = sb("AtP", [128, 4, 128])
        tinv_d = Ring("tinvd", 6, [128, 512])
        tinv_r = Ring("tinvr", 6, [128, 512])
        X512d = [sb("X512d_%d" % i, [128, 512]) for i in range(8)]
        X512r = [sb("X512r_%d" % i, [128, 512]) for i in range(6)]
        X256d = [sb("X256d_%d" % i, [128, 256]) for i in range(12)]
        X256r = [sb("X256r_%d" % i, [128, 256]) for i in range(22)]
        X512, X256 = X512d, X256d
        tsm = Ring("tsm", 8, [128, 8])
        tsm_d = Ring("tsmd", 10, [128, 8])
        tsm_r = Ring("tsmr", 10, [128, 8])
        tb256 = Ring("tb256", 2, [128, 256], BF16)

        ID = cst[:, K_ID:K_ID + 128]
        MSL = cst[:, K_MSL:K_MSL + 128]
        MSU = cst[:, K_MSU:K_MSU + 128]
        MUI = cst[:, K_MUI:K_MUI + 128]
        BLK = cst[:, K_BLK:K_BLK + 128]
        GMT = cst[:, K_GMT:K_GMT + 128]

        def b4(m):
            return m.unsqueeze(1).to_broadcast([128, 4, 128])

        def v4(t):
            return t.rearrange("p (h f) -> p h f", h=4)

        def vpar(t, par):
            return t.rearrange("p (a b f) -> p a b f", a=2, b=2)[:, :, par, :]

        def b2(m):
            return m.unsqueeze(1).to_broadcast([128, 2, 128])

        def v2(t):
            return t.rearrange("p (h f) -> p h f", h=2)

        C_ONE, C_E6, C_E5, C_EX, C_ZERO = 0, 1, 2, 3, 4

        def ccol(i):
            return cc[:, i:i + 1]

        def col(l, off, i=0, n=1):
            b = l * LC + off + i
            return cols[:, b:b + n]

        def row(l, off, n):
            b = l * LR + off
            return rows[:, b:b + n]

        def recip(out, in_):
            P.emit("dve", [out], [in_], lambda e: e.reciprocal(out, in_))

        def rsqrt(out, in_, eps_col, scale=1.0):
            P.act(out, in_, AF.Sqrt, bias=ccol(eps_col), scale=scale)
            recip(out, out)

        P.dma(cols[:, 0:NCOL], cols_d)
        P.dma(rows[:], rows_d.to_broadcast([128, NROW]))
        P.dma(mats[:], mats_d.rearrange("l p n -> p l n"))
        P.dma(cst[:], cst_d)
        P.dma(cc[:, 5:6], flag_d)
        P.ts(cc[:, 6:7], cc[:, 5:6], -1.0, ALU.mult, 1.0, ALU.add)
        P.memset(cc[:, 0:1], 1.0)
        P.memset(cc[:, 1:2], 1e-6)
        P.memset(cc[:, 2:3], 1e-5)
        P.memset(cc[:, 3:4], 64e-5)
        P.memset(cc[:, 4:5], 0.0)
        P.memset(ones_bf[:], 1.0)
        P.memset(ones_f[:], 1.0)
        P.memset(o256_f[:], 1.0 / 256.0)
        P.ts(triS[:, 0:128], MUI, -EXPM05, ALU.mult)
        P.ts(triS[:, 128:256], MSU, -EXPM05, ALU.mult)
        for l in range(NL):
            P.tt(wmT[:, l, :].rearrange("p (h f) -> p h f", h=4),
                 mats[:, l, M_WST:M_WST + 512].rearrange("p (h f) -> p h f", h=4), b4(GMT), ALU.mult)
            P.act(nexpA[:, l, :], row(l, R_ALOG, 4), AF.Exp)
            P.ts(nexpA[:, l, :], nexpA[:, l, :], -1.0, ALU.mult)
            P.ts(cols[:, NCOL + 2 * l:NCOL + 2 * l + 2], col(l, C_KA, 0, 2), -1.0, ALU.mult, 1.0, ALU.add)
        for t_ in (Sblk, Zblk, qkvh, zch, cfh, ffh, KbeP, AtP):
            P.memset(t_[:], 0.0)

        hT_flat = hT[:].rearrange("p c t -> p (c t)")
        abuf_flat = abuf[:].rearrange("p c t -> p (c t)")
        ci = 0
        for l in range(n_layers if prologue else 0):
            for k, shp in wshapes.items():
                n_el = shp[0] * shp[1] // 128
                src = w32[k][l].rearrange("(p a) n -> p (a n)", p=128)
                dst = wbf[k][l].rearrange("(p a) n -> p (a n)", p=128)
                off = 0
                while off < n_el:
                    w_ = min(4 * TT, n_el - off)
                    stg = hT_flat[:, (ci % 2) * 4 * TT:(ci % 2) * 4 * TT + w_]
                    P.dma(stg, src[:, off:off + w_])
                    ob = abuf_flat[:, (ci % 2) * 4 * TT:(ci % 2) * 4 * TT + 4 * TT]
                    eng = ("dve", "act", "pool")[ci % 3]
                    P.copy(ob[:, 0:w_], stg, eng=eng)
                    P.dma(dst[:, off:off + w_], ob[:, 0:w_], eng="pool" if ci % 2 else "sp")
                    off += w_
                    ci += 1

        class Stream:
            def __init__(self, name, bank_list, n_g, n_w, n_x):
                self.banks = bank_list
                self.bi = 0
                self.g512 = Ring("g512" + name, n_g, [128, TT])
                self.w515 = Ring("w515" + name, 2, [128, 516])
                self.wring = Ring("wb" + name, n_w, [128, 4096], BF16)
                self.xio = Ring("xio" + name, n_x, [128, 1024])
                self.tr_i = 0

            def pb(self):
                b = self.banks[self.bi % len(self.banks)]
                self.bi += 1
                return b

        SA = Stream("A", banks[0:8], 4, 3, 1)
        SB = SA

        class Pool:
            def __init__(self, bl):
                self.banks = bl
                self.bi = 0
                self.tr_i = 0

            def pb(self):
                b = self.banks[self.bi % len(self.banks)]
                self.bi += 1
                return b

        PD = Pool(banks[0:4])
        PR = Pool(banks[4:8])
        PRest = Pool(banks[0:6])
        PConf = Pool(banks[6:8])

        def transpose_pool(S_, dst_fn, src_aps):
            i = 0
            while i < len(src_aps):
                ps = S_.pb()
                n = min(4, len(src_aps) - i)
                for j in range(n):
                    P.tr(ps[:, j * 128:(j + 1) * 128], src_aps[i + j], ID)
                eng_ = ("act", "dve")[S_.tr_i % 2]
                S_.tr_i += 1
                for j in range(n):
                    P.copy(dst_fn(i + j), ps[:, j * 128:(j + 1) * 128], eng=eng_)
                i += n

        def make_helpers(S_):
            def load_w(ap_dram, kc, ncol):
                wt = S_.wring.next()
                v = wt[:, 0:kc * ncol].rearrange("p (k n) -> p k n", k=kc)
                P.dma(v, ap_dram.rearrange("(k p) n -> p k n", p=128))
                return v

            def rmsnorm(hT, gcol_off_layer, sq, out_bf=None, out_f32=None):
                for c in range(8):
                    P.act(sq[:, c, :], hT[:, c, :], AF.Square)
                ps = S_.pb()
                for c in range(8):
                    P.mm(ps[:, 0:TT], ones_bf[:], sq[:, c, :], start=(c == 0), stop=(c == 7))
                rstd = S_.g512.next()
                rsqrt(rstd[:], ps[:, 0:TT], C_E6, scale=1.0 / D)
                for c in range(8):
                    g = cols[:, gcol_off_layer + c:gcol_off_layer + c + 1]
                    o = out_bf[:, c, :] if out_bf is not None else out_f32[:, c, :]
                    P.stt(o, hT[:, c, :], g, rstd[:], ALU.mult, ALU.mult)

            def transpose_to(dst_fn, src_aps):
                i = 0
                while i < len(src_aps):
                    ps = S_.pb()
                    n = min(4, len(src_aps) - i)
                    for j in range(n):
                        P.tr(ps[:, j * 128:(j + 1) * 128], src_aps[i + j], ID)
                    eng_ = ("act", "dve")[S_.tr_i % 2]
                    S_.tr_i += 1
                    for j in range(n):
                        P.copy(dst_fn(i + j), ps[:, j * 128:(j + 1) * 128], eng=eng_)
                    i += n
            return load_w, rmsnorm, transpose_to

        def tri_inv(Nn, NT, tinv, pb):
            Tt = tinv.next()
            P.tt(v4(Tt[:]), v4(NT[:]), b4(ID), ALU.add)
            cN, cNT = Nn, NT
            for k in range(1, 7):
                yield
                psA = pb()
                for h in range(4):
                    hs = slice(h * 128, (h + 1) * 128)
                    P.mm(psA[:, hs], cNT[:, hs], cN[:, hs])
                nN = tinv.next()
                P.copy(nN[:], psA[:, :], eng="act")
                nNT = None
                if k < 6:
                    yield
                    psB = pb()
                    for h in range(4):
                        hs = slice(h * 128, (h + 1) * 128)
                        P.mm(psB[:, hs], cN[:, hs], cNT[:, hs])
                    nNT = tinv.next()
                    P.copy(nNT[:], psB[:, :], eng="dve")
                yield
                psC = pb()
                for h in range(4):
                    hs = slice(h * 128, (h + 1) * 128)
                    P.mm(psC[:, hs], nN[:, hs], Tt[:, hs])
                nT = tinv.next()
                P.tt(nT[:], Tt[:], psC[:, :], ALU.add)
                Tt = nT
                cN, cNT = nN, nNT
            return Tt

        def chk(name):
            if stop == name:
                raise _Stop()

        def genA(ti, l, par):
            tok0 = ti * TT
            hT = hTs[par]
            hn = hns[par]
            pb = SA.pb
            g512, w515, wring, xio = SA.g512, SA.w515, SA.wring, SA.xio
            load_w, rmsnorm, transpose_to = make_helpers(SA)
            SA.mode = "dense"
            if l == 0:

                if ti == 0:
                    load_x_tile(0)
                for s in range(NSUB):
                    yield
                    transpose_to(lambda i, s=s: hT[:, i, s * 128:(s + 1) * 128],
                                 [X512r[2 * s + c // 4][:, (c % 4) * 128:(c % 4 + 1) * 128] for c in range(8)])

            w_in = wbf["w_in"][l]
            if ti >= 1:
                rcv = X512[0:4]
                for q in range(4):
                    P.dma(rcv[q][:], recv_d[0:128, q * 512:(q + 1) * 512])
                hflat = hT[:].rearrange("p c t -> p (c t)")
                for q in range(4):
                    yield
                    P.stt(hflat[:, q * 512:(q + 1) * 512], rcv[q][:], cc[:, 5:6], hflat[:, q * 512:(q + 1) * 512],
                          ALU.mult, ALU.add)
            chk('x')
            rmsnorm(hT, l * LC + C_NMIX, o_all, out_bf=hn)
            store_out(SA)

            wD = load_w(w_in[:, 2568:3080], 8, 512)
            for c in range(2):
                yield
                ps1 = pb()
                for kc in range(8):
                    yield
                    P.mm(ps1[:, 0:TT], wD[:, kc, c * 128:(c + 1) * 128], hn[:, kc, :], start=(kc == 0), stop=(kc == 7))
                ps2 = pb()
                for kc in range(8):
                    yield
                    P.mm(ps2[:, 0:TT], wD[:, kc, 256 + c * 128:256 + (c + 1) * 128], hn[:, kc, :],
                         start=(kc == 0), stop=(kc == 7))
                sg = g512.next()
                P.act(sg[:], ps2[:, 0:TT], AF.Sigmoid)
                P.copy(hbuf[:, c, 0:30], cfh[:, l, c, :], eng="pool")
                P.tt(hbuf[:, c, 30:30 + TT], ps1[:, 0:TT], sg[:], ALU.mult)
                P.copy(cfh[:, l, c, :], hbuf[:, c, TT:TT + 30], eng="pool")

            def rest_gen():
                pb = PRest.pb
                transpose_to = lambda d_, s_: transpose_pool(PRest, d_, s_)
                wA = load_w(w_in[:, 0:512], 8, 512)
                for s in range(NSUB):
                    yield
                    ss = slice(s * 128, (s + 1) * 128)
                    ps = pb()
                    for kc in range(8):
                        yield
                        P.mm(ps[:, :], hn[:, kc, ss], wA[:, kc, :], start=(kc == 0), stop=(kc == 7))
                    xz = X512r[3]
                    P.copy(xz[:], ps[:, :], eng="act")
                    x2 = X512r[4]
                    P.tt(x2[:], xz[:], xz[:], ALU.mult)
                    P.ts(x2[:], x2[:], 0.044715, ALU.mult, 1.0, ALU.add)
                    P.tt(x2[:], x2[:], xz[:], ALU.mult)
                    P.act(x2[:], x2[:], AF.Sigmoid, scale=1.5957691216057308)
                    gl = X512r[5]
                    P.tt(gl[:], xz[:], x2[:], ALU.mult)
                    st6 = tsm.next()
                    P.bn_stats(st6[:, 0:6], gl[:, 256:512])
                    mv = tsm.next()
                    P.bn_aggr(mv[:, 0:2], st6[:, 0:6])
                    rs = tsm.next()
                    rsqrt(rs[:, 0:1], mv[:, 1:2], C_E5)
                    vn = X256[0]
                    P.ts(vn[:], gl[:, 256:512], mv[:, 0:1], ALU.subtract, rs[:, 0:1], ALU.mult)
                    P.tt(vn[:], vn[:], row(l, R_VG, 256), ALU.mult)
                    vnb = tb256.next()
                    P.tt(vnb[:], vn[:], row(l, R_VB, 256), ALU.add)
                    ps2 = pb()
                    for h in range(4):
                        yield
                        P.mm(ps2[:, h * 64:(h + 1) * 64], wmT[:, l, h * 128:(h + 1) * 128], vnb[:, h * 64:(h + 1) * 64])
                    oa = X256[1]
                    for h in range(4):
                        yield
                        hs = slice(h * 64, (h + 1) * 64)
                        P.stt(oa[:, hs], ps2[:, hs], col(l, C_BST, h), gl[:, hs], ALU.add, ALU.mult)
                    transpose_to(lambda i, s=s: o_all[:, i, s * 128:(s + 1) * 128],
                                 [oa[:, 0:128], oa[:, 128:256]])

                chk('A')
                for half in range(2):
                    yield
                    wB = load_w(w_in[:, 512 + half * 384: 512 + (half + 1) * 384], 8, 384)
                    for cc_ in range(3):
                        yield
                        c = half * 3 + cc_
                        ps = pb()
                        for kc in range(8):
                            yield
                            P.mm(ps[:, 0:TT], wB[:, kc, cc_ * 128:(cc_ + 1) * 128], hn[:, kc, :],
                                 start=(kc == 0), stop=(kc == 7))
                        wk = w515.next()
                        P.copy(wk[:, 3:3 + TT], ps[:, 0:TT], eng="act")
                        P.copy(wk[:, 0:3], qkvh[:, l, c, :], eng="pool")
                        acc = g512.next()
                        P.ts(acc[:], wk[:, 3:3 + TT], col(l, C_DNW, c * 4 + 3), ALU.mult)
                        for k in (2, 1, 0):
                            yield
                            P.stt(acc[:], wk[:, k:k + TT], col(l, C_DNW, c * 4 + k), acc[:], ALU.mult, ALU.add)
                        P.copy(qkvh[:, l, c, :], wk[:, TT:TT + 3], eng="pool")
                        P.act(qkvs[:, c, :], acc[:], AF.Silu)

                chk('B')
                wG = load_w(w_in[:, 1280:1544], 8, 264)
                for s in range(NSUB):
                    yield
                    ss = slice(s * 128, (s + 1) * 128)
                    ps = pb()
                    for kc in range(8):
                        yield
                        P.mm(ps[:, 0:264], hn[:, kc, ss], wG[:, kc, :], start=(kc == 0), stop=(kc == 7))
                    P.act(gateS[:, s, :], ps[:, 0:256], AF.Silu)
                    P.act(betaT[:, s, :], ps[:, 256:260], AF.Sigmoid)
                    P.ts(nbetaT[:, s, :], betaT[:, s, :], -1.0, ALU.mult)
                    sp_ = tsm.next()
                    P.tt(sp_[:, 0:4], ps[:, 260:264], row(l, R_DTB, 4), ALU.add)
                    P.act(sp_[:, 0:4], sp_[:, 0:4], AF.Exp)
                    P.act(sp_[:, 0:4], sp_[:, 0:4], AF.Ln, bias=ccol(C_ONE), scale=1.0)
                    P.tt(gT[:, s, :], sp_[:, 0:4], nexpA[:, l, :], ALU.mult)

                chk('B2')
                for half in range(2):
                    yield
                    wC = load_w(w_in[:, 1544 + half * 512: 1544 + (half + 1) * 512], 8, 512)
                    for cc_ in range(4):
                        yield
                        c = half * 4 + cc_
                        ps = pb()
                        for kc in range(8):
                            yield
                            P.mm(ps[:, 0:TT], wC[:, kc, cc_ * 128:(cc_ + 1) * 128], hn[:, kc, :],
                                 start=(kc == 0), stop=(kc == 7))
                        wk = w515.next()
                        P.copy(wk[:, 1:1 + TT], ps[:, 0:TT], eng="act")
                        P.copy(wk[:, 0:1], zch[:, l, c, :], eng="pool")
                        d_ = g512.next()
                        P.tt(d_[:], wk[:, 0:TT], wk[:, 1:1 + TT], ALU.subtract)
                        P.stt(zcm[:, c, :], d_[:], col(l, C_MU, c), wk[:, 1:1 + TT], ALU.mult, ALU.add)
                        P.copy(zch[:, l, c, :], wk[:, TT:TT + 1], eng="pool")


            def conf_gen():
                pb = PConf.pb
                accs = []
                for c in range(2):
                    yield
                    accA = X512[2 * c][:, 0:TT]
                    P.ts(accA[:], hbuf[:, c, 0:TT], col(l, C_CFW, c * 31 + 0), ALU.mult, col(l, C_CFB, c), ALU.add)
                    for k in range(1, 31):
                        yield
                        P.stt(accA[:], hbuf[:, c, k:k + TT], col(l, C_CFW, c * 31 + k), accA[:], ALU.mult, ALU.add)
                    accs.append(accA)
                psm = pb()
                pss = pb()
                for c in range(2):
                    yield
                    P.mm(psm[:, 0:TT], o256_f[:], accs[c][:], start=(c == 0), stop=(c == 1))
                sqs = []
                for c in range(2):
                    yield
                    sq = X512[2 * c + 1][:, 0:TT]
                    P.act(sq[:], accs[c][:], AF.Square)
                    sqs.append(sq)
                for c in range(2):
                    yield
                    P.mm(pss[:, 0:TT], o256_f[:], sqs[c][:], start=(c == 0), stop=(c == 1))
                mean = X512[4][:, 0:TT]
                P.copy(mean[:], psm[:, 0:TT], eng="act")
                var = X512[5][:, 0:TT]
                P.tt(var[:], mean[:], mean[:], ALU.mult)
                P.tt(var[:], pss[:, 0:TT], var[:], ALU.subtract)
                rsqrt(var[:], var[:], C_E5)
                for c in range(2):
                    yield
                    t1 = X512[6 + c][:, 0:TT]
                    P.tt(t1[:], accs[c][:], mean[:], ALU.subtract)
                    P.tt(t1[:], t1[:], var[:], ALU.mult)
                    P.ts(t1[:], t1[:], col(l, C_CFG, c), ALU.mult, col(l, C_CFLB, c), ALU.add)
                    P.act(o_all[:, 6 + c, :], t1[:], AF.Silu)


            gc_, gr_ = conf_gen(), rest_gen()
            c_alive = r_alive = True
            while c_alive or r_alive:
                for _ in range(4):
                    if r_alive:
                        try:
                            next(gr_)
                        except StopIteration:
                            r_alive = False
                if c_alive:
                    try:
                        next(gc_)
                    except StopIteration:
                        c_alive = False
                yield
            SA.mode = "chain"
            chk('conf')
            def dn_gen():
                pb = PD.pb
                X256, X512, tsm = X256d, X512d, tsm_d
                transpose_to = lambda d_, s_: transpose_pool(PD, d_, s_)
                for c in range(4):
                    yield
                    sq = g512.next()
                    P.act(sq[:], qkvs[:, c, :], AF.Square)
                    ps = pb()
                    P.mm(ps[:, 0:TT], BLK, sq[:])
                    rn = g512.next()
                    rsqrt(rn[:], ps[:, 0:TT], C_E6)
                    if c < 2:
                        P.stt(qkvs[:, c, :], qkvs[:, c, :], 0.125, rn[:], ALU.mult, ALU.mult)
                    else:
                        P.tt(qkvs[:, c, :], qkvs[:, c, :], rn[:], ALU.mult)
                for j in range(NSUB):
                    yield
                    sl = slice(j * 128, (j + 1) * 128)
                    KV = X512[0]
                    transpose_to(lambda i: KV[:, i * 128:(i + 1) * 128], [qkvs[:, 2 + i, sl] for i in range(4)])
                    Ktm = KV[:, 0:256]
                    Vtm = KV[:, 256:512]
                    psg = pb()
                    P.mm(psg[:, 0:4], MUI, gT[:, j, :])
                    gcum = tsm.next()
                    P.copy(gcum[:, 0:4], psg[:, 0:4])
                    dGg, dGb = X512[6], X512[7]
                    for h in range(4):
                        yield
                        P.ts(dGg[:, h * 128:(h + 1) * 128], ID, gcum[:, h:h + 1], ALU.mult)
                        P.ts(dGb[:, h * 128:(h + 1) * 128], ID, betaT[:, j, h:h + 1], ALU.mult)
                    psG = pb()
                    P.mm(psG[:, :], ones_f[:], dGg[:])
                    psB = pb()
                    P.mm(psB[:, :], ones_f[:], dGb[:])
                    Grow = X512[1]
                    P.copy(Grow[:], psG[:, :], eng="act")
                    D1 = X512[2]
                    for h in range(4):
                        yield
                        hs = slice(h * 128, (h + 1) * 128)
                        P.ts(D1[:, hs], Grow[:, hs], -1.0, ALU.mult, gcum[:, h:h + 1], ALU.add)
                    Esl = X512[3]
                    P.tt(v4(Esl[:]), v4(D1[:]), b4(MSL), ALU.mult)
                    P.act(Esl[:], Esl[:], AF.Exp)
                    P.tt(v4(Esl[:]), v4(Esl[:]), b4(MSL), ALU.mult)
                    Eui = X512[4]
                    P.tt(v4(Eui[:]), v4(D1[:]), b4(MUI), ALU.mult)
                    P.act(Eui[:], Eui[:], AF.Exp, scale=-1.0)
                    P.tt(v4(Eui[:]), v4(Eui[:]), b4(MUI), ALU.mult)
                    EB = X512[2]
                    P.tt(v4(EB[:]), v4(Eui[:]), b4(MSU), ALU.mult)
                    P.stt(EB[:], EB[:], -1.0, psB[:, :], ALU.mult, ALU.mult)
                    Erow = X512[5]
                    P.act(Erow[:], Grow[:], AF.Exp)
                    psKK = [pb(), pb()]
                    psKQ = [pb(), pb()]
                    for h in range(4):
                        yield
                        c_, b_ = h // 2, (h % 2) * 64
                        P.mm(psKK[h % 2][:, c_ * 128:(c_ + 1) * 128], qkvs[b_:b_ + 64, 2 + c_, sl], qkvs[b_:b_ + 64, 2 + c_, sl])
                    for h in range(4):
                        yield
                        c_, b_ = h // 2, (h % 2) * 64
                        P.mm(psKQ[h % 2][:, c_ * 128:(c_ + 1) * 128], qkvs[b_:b_ + 64, 2 + c_, sl], qkvs[b_:b_ + 64, c_, sl])
                    Nn = X512[6]
                    for h in range(4):
                        yield
                        hs = slice(h * 128, (h + 1) * 128)
                        P.stt(Nn[:, hs], psKK[h % 2][:, (h // 2) * 128:(h // 2 + 1) * 128], nbetaT[:, j, h:h + 1],
                              Esl[:, hs], ALU.mult, ALU.mult)
                    NT = X512[7]
                    for par in range(2):
                        yield
                        P.tt(vpar(NT[:], par), v2(psKK[par][:, 0:256]), vpar(EB[:], par), ALU.mult)
                    attnT = X512[3]
                    for par in range(2):
                        yield
                        P.tt(vpar(attnT[:], par), v2(psKQ[par][:, 0:256]), vpar(Eui[:], par), ALU.mult)
                    Tt = yield from tri_inv(Nn, NT, tinv_d, pb)
                    eg = tsm.next()
                    P.act(eg[:, 0:4], gcum[:, 0:4], AF.Exp)
                    P.tt(eg[:, 0:4], eg[:, 0:4], betaT[:, j, :], ALU.mult)
                    Vb = X256[0]
                    for h in range(4):
                        yield
                        hs = slice(h * 64, (h + 1) * 64)
                        P.ts(Vb[:, hs], Vtm[:, hs], betaT[:, j, h:h + 1], ALU.mult)
                        P.ts(KbeP[:, h, (h % 2) * 64:(h % 2) * 64 + 64], Ktm[:, hs], eg[:, h:h + 1], ALU.mult)
                    psU = pb()
                    for h in range(4):
                        yield
                        hs = slice(h * 128, (h + 1) * 128)
                        P.mm(psU[:, h * 64:(h + 1) * 64], Tt[:, hs], Vb[:, h * 64:(h + 1) * 64])
                    Usb = X256[1]
                    P.copy(Usb[:], psU[:, 0:256], eng="act")
                    psW = pb()
                    for pr in range(2):
                        yield
                        for hh in range(2):
                            yield
                            h = pr * 2 + hh
                            P.mm(psW[:, pr * 128:(pr + 1) * 128], KbeP[:, h, :], Tt[:, h * 128:(h + 1) * 128],
                                 start=(hh == 0), stop=(hh == 1))
                    WT = X256[2]
                    P.copy(WT[:], psW[:, 0:256])
                    Qd = X256[3]
                    for pr in range(2):
                        yield
                        for hh in range(2):
                            yield
                            h = pr * 2 + hh
                            b_ = hh * 64
                            P.tt(Qd[b_:b_ + 64, pr * 128:(pr + 1) * 128], qkvs[b_:b_ + 64, pr, sl],
                                 Erow[b_:b_ + 64, h * 128:(h + 1) * 128], ALU.mult)
                    glast = Grow[:].rearrange("p (h f) -> p h f", h=4)[:, :, 127]
                    kds = tsm.next()
                    P.tt(kds[:, 0:4], glast, gcum[:, 0:4], ALU.subtract)
                    P.act(kds[:, 0:4], kds[:, 0:4], AF.Exp)
                    egl = tsm.next()
                    P.act(egl[:, 0:4], glast, AF.Exp)
                    Kd = X256[4]
                    for h in range(4):
                        yield
                        hs = slice(h * 64, (h + 1) * 64)
                        P.ts(Kd[:, hs], Ktm[:, hs], kds[:, h:h + 1], ALU.mult)
                    otm = X256[5]
                    for pr in range(2):
                        yield
                        prs = slice(pr * 128, (pr + 1) * 128)
                        ps1 = pb()
                        P.mm(ps1[:, 0:128], WT[:, prs], Sblk[:, l, pr, :])
                        vnew = X256[6 + pr]
                        P.tt(vnew[:, 0:128], Usb[:, prs], ps1[:, 0:128], ALU.subtract)
                        ps2 = pb()
                        P.mm(ps2[:, 0:128], Qd[:, prs], Sblk[:, l, pr, :], start=True, stop=False)
                        for hh in range(2):
                            yield
                            h = pr * 2 + hh
                            P.mm(ps2[:, hh * 64:(hh + 1) * 64], attnT[:, h * 128:(h + 1) * 128],
                                 vnew[:, hh * 64:(hh + 1) * 64], start=False, stop=(hh == 1))
                        P.copy(otm[:, prs], ps2[:, 0:128], eng="act")
                        ps3 = pb()
                        P.mm(ps3[:, 0:128], Kd[:, prs], vnew[:, 0:128])
                        tm = X256[8 + pr]
                        P.tt(tm[:, 0:128], ps3[:, 0:128], BLK, ALU.mult)
                        for hh in range(2):
                            yield
                            h = pr * 2 + hh
                            b_ = hh * 64
                            P.stt(Sblk[b_:b_ + 64, l, pr, :], Sblk[b_:b_ + 64, l, pr, :], egl[b_:b_ + 64, h:h + 1],
                                  tm[b_:b_ + 64, 0:128], ALU.mult, ALU.add)
                    sq = X256[10]
                    P.tt(sq[:], otm[:], otm[:], ALU.mult)
                    ssq = tsm.next()
                    P.reduce(ssq[:, 0:4], sq[:].rearrange("p (h d) -> p h d", h=4), ALU.add)
                    rsqrt(ssq[:, 0:4], ssq[:, 0:4], C_E6, scale=1.0 / 64)
                    ob = X256[11]
                    for h in range(4):
                        yield
                        hs = slice(h * 64, (h + 1) * 64)
                        P.stt(ob[:, hs], otm[:, hs], ssq[:, h:h + 1], row(l, R_OG, 64), ALU.mult, ALU.mult)
                    P.tt(ob[:], ob[:], gateS[:, j, :], ALU.mult)
                    transpose_to(lambda i, j=j: o_all[:, 2 + i, j * 128:(j + 1) * 128],
                                 [ob[:, 0:128], ob[:, 128:256]])


            def rw_gen():
                pb = PR.pb
                X256, X512, tsm = X256r, X512r, tsm_r
                transpose_to = lambda d_, s_: transpose_pool(PR, d_, s_)
                wa2 = mats[:, l, M_WA2:M_WA2 + 256]
                g2 = mats[:, l, M_G2:M_G2 + 256]
                P.act(thx[0:64, :], zcm[0:64, 6, :], AF.Tanh)
                sgx = X512r[0][:, 0:TT]
                P.act(sgx[:], zcm[:, 7, :], AF.Sigmoid)
                for c in range(2):
                    yield
                    ps = pb()
                    P.mm(ps[:, 0:TT], wa2[64:128, c * 128:(c + 1) * 128], zcm[64:128, 6, :])
                    P.act(asig[:, c, :], ps[:, 0:TT], AF.Sigmoid, bias=col(l, C_A0, c), scale=1.0)
                    ps = pb()
                    P.mm(ps[:, 0:TT], g2[:, c * 128:(c + 1) * 128], sgx[:])
                    P.copy(gateC[:, c, :], ps[:, 0:TT])
                for j in range(NSUB):
                    yield
                    sl = slice(j * 128, (j + 1) * 128)
                    def fm(cbase):
                        return zcm[:, cbase:cbase + 2, sl]

                    def t2(i):
                        t = X256[i]
                        return t, t[:].rearrange("p (c t) -> p c t", c=2)
                    kkt, kk3 = t2(0)
                    for c in range(2):
                        yield
                        P.ts(kk3[:, c, :], zcm[:, 2 + c, sl], col(l, C_KK, c), ALU.mult)
                    sq = X256[1]
                    P.act(sq[:], kkt[:], AF.Square)
                    ps = pb()
                    P.mm(ps[:, 0:256], BLK, sq[:])
                    rn = X256[2]
                    rsqrt(rn[:], ps[:, 0:256], C_E6)
                    P.tt(kkt[:], kkt[:], rn[:], ALU.mult)
                    k2t, k23 = t2(3)
                    for c in range(2):
                        yield
                        P.ts(k23[:, c, :], asig[:, c, sl], col(l, C_KA, c), ALU.mult,
                             cols[:, NCOL + 2 * l + c:NCOL + 2 * l + c + 1], ALU.add)
                    P.tt(k23, k23, fm(2), ALU.mult)
                    bvt, bv3 = t2(4)
                    P.tt(bv3, kk3, asig[:, :, sl], ALU.mult)
                    rkt, rk3 = t2(5)
                    for c in range(2):
                        yield
                        P.stt(rk3[:, c, :], zcm[:, c, sl], col(l, C_RK, c), k23[:, c, :], ALU.mult, ALU.mult)
                    psb = pb()
                    P.mm(psb[:, 0:256], BLK, rkt[:])
                    bon, bon3 = t2(11)
                    P.tt(bon3, psb[:, 0:256].rearrange("p (c t) -> p c t", c=2), fm(4), ALU.mult)
                    psl = pb()
                    P.mm(psl[:, 0:256], thx[0:64, sl], wa2[0:64, :])
                    sgT = X256[6]
                    P.tt(sgT[:], psl[:, 0:256], row(l, R_W0, 256), ALU.add)
                    P.act(sgT[:], sgT[:], AF.Sigmoid)
                    psc = pb()
                    for c in range(2):
                        yield
                        P.mm(psc[:, c * 128:(c + 1) * 128], sgT[:, c * 128:(c + 1) * 128], triS[:, 0:128])
                    for c in range(2):
                        yield
                        P.mm(psc[:, 256 + c * 128:256 + (c + 1) * 128], sgT[:, c * 128:(c + 1) * 128], triS[:, 128:256])
                    cum = X512[0]
                    P.copy(cum[:], psc[:, :], eng="act")
                    cum3 = cum[:, 0:256].rearrange("p (c t) -> p c t", c=2)
                    tot = cum3[:, :, 127]
                    gam, gam3 = t2(7)
                    P.act(gam[:], cum[:, 0:256], AF.Exp)
                    igam, igam3 = t2(8)
                    P.act(igam[:], cum[:, 0:256], AF.Exp, scale=-1.0)
                    gamx, gamx3 = t2(9)
                    P.act(gamx[:], cum[:, 256:512], AF.Exp)
                    ghat, ghat3 = t2(10)
                    for c in range(2):
                        yield
                        P.act(ghat3[:, c, :], cum3[:, c, :], AF.Exp, bias=cum3[:, c, 127:128], scale=-1.0)
                    etot = tsm.next()
                    P.act(etot[:, 0:2], tot, AF.Exp)
                    At, At3 = t2(12)
                    P.stt(At[:], kkt[:], -1.0, gamx[:], ALU.mult, ALU.mult)
                    Bt, Bt3 = t2(13)
                    P.tt(Bt[:], bvt[:], igam[:], ALU.mult)
                    Kt, Kt3 = t2(14)
                    P.tt(Kt[:], k2t[:], igam[:], ALU.mult)
                    Rt, Rt3 = t2(15)
                    P.tt(Rt3, fm(0), gam3, ALU.mult)
                    Bh, Bh3 = t2(16)
                    P.tt(Bh[:], bvt[:], ghat[:], ALU.mult)
                    Kh, Kh3 = t2(17)
                    P.tt(Kh[:], k2t[:], ghat[:], ALU.mult)
                    BhT = X256[0]
                    KhT = X256[1]
                    VT = X256[2]
                    transpose_to(lambda i: (BhT, BhT, KhT, KhT)[i][:, (i % 2) * 128:(i % 2) * 128 + 128],
                                 [Bh3[:, 0, :], Bh3[:, 1, :], Kh3[:, 0, :], Kh3[:, 1, :]])
                    AtT = X256[3]
                    transpose_to(lambda i: (VT, VT, AtT, AtT)[i][:, (i % 2) * 128:(i % 2) * 128 + 128],
                                 [zcm[:, 4, sl], zcm[:, 5, sl], At3[:, 0, :], At3[:, 1, :]])
                    for h in range(4):
                        yield
                        b_ = (h % 2) * 64
                        P.copy(AtP[:, h, b_:b_ + 64], AtT[:, h * 64:(h + 1) * 64])
                    def hm(lhs3, rhs3, mask, slot):
                        ps = [pb(), pb()]
                        for h in range(4):
                            c_, b_ = h // 2, (h % 2) * 64
                            P.mm(ps[h % 2][:, c_ * 128:(c_ + 1) * 128], lhs3[b_:b_ + 64, c_, :], rhs3[b_:b_ + 64, c_, :])
                        o = X512[slot]
                        for par in range(2):
                            P.tt(vpar(o[:], par), v2(ps[par][:, 0:256]), b2(mask), ALU.mult)
                        return o
                    Nn = hm(At3, Bt3, MSL, 1)
                    NT = hm(Bt3, At3, MSU, 2)
                    AakT = hm(Kt3, At3, MSU, 3)
                    ArbT = hm(Bt3, Rt3, MUI, 4)
                    ArkT = hm(Kt3, Rt3, MUI, 5)
                    Tt = yield from tri_inv(Nn, NT, tinv_r, pb)
                    psM = pb()
                    for h in range(4):
                        yield
                        P.mm(psM[:, h * 64:(h + 1) * 64], AakT[:, h * 128:(h + 1) * 128], VT[:, h * 64:(h + 1) * 64])
                    M1 = X256[4]
                    P.copy(M1[:], psM[:, 0:256], eng="act")
                    psU = pb()
                    for h in range(4):
                        yield
                        P.mm(psU[:, h * 64:(h + 1) * 64], Tt[:, h * 128:(h + 1) * 128], M1[:, h * 64:(h + 1) * 64])
                    Usb = X256[5]
                    P.copy(Usb[:], psU[:, 0:256])
                    psW = pb()
                    for pr in range(2):
                        yield
                        for hh in range(2):
                            yield
                            h = pr * 2 + hh
                            P.mm(psW[:, pr * 128:(pr + 1) * 128], AtP[:, h, :], Tt[:, h * 128:(h + 1) * 128],
                                 start=(hh == 0), stop=(hh == 1))
                    WT = X256[6]
                    P.copy(WT[:], psW[:, 0:256], eng="act")
                    ytm = X256[7]
                    for pr in range(2):
                        yield
                        prs = slice(pr * 128, (pr + 1) * 128)
                        ps1 = pb()
                        P.mm(ps1[:, 0:128], WT[:, prs], Zblk[:, l, pr, :])
                        Pm = X256[8 + pr]
                        P.tt(Pm[:, 0:128], Usb[:, prs], ps1[:, 0:128], ALU.add)
                        ps2 = pb()
                        P.mm(ps2[:, 0:128], Rt3[:, pr, :], Zblk[:, l, pr, :], start=True, stop=False)
                        for hh in range(2):
                            yield
                            h = pr * 2 + hh
                            P.mm(ps2[:, hh * 64:(hh + 1) * 64], ArbT[:, h * 128:(h + 1) * 128],
                                 Pm[:, hh * 64:(hh + 1) * 64], start=False, stop=False)
                            P.mm(ps2[:, hh * 64:(hh + 1) * 64], ArkT[:, h * 128:(h + 1) * 128],
                                 VT[:, h * 64:(h + 1) * 64], start=False, stop=(hh == 1))
                        P.copy(ytm[:, prs], ps2[:, 0:128], eng="act")
                        ps3 = pb()
                        P.mm(ps3[:, 0:128], BhT[:, prs], Pm[:, 0:128], start=True, stop=False)
                        P.mm(ps3[:, 0:128], KhT[:, prs], VT[:, prs], start=False, stop=True)
                        tm = X256[(10, 18)[pr]]
                        P.tt(tm[:, 0:128], ps3[:, 0:128], BLK, ALU.mult)
                        P.stt(Zblk[:, l, pr, :], Zblk[:, l, pr, :], etot[:, pr:pr + 1], tm[:, 0:128], ALU.mult, ALU.add)
                    y4 = ytm[:].rearrange("p (h d) -> p h d", h=4)
                    s1 = tsm.next()
                    P.reduce(s1[:, 0:4], y4, ALU.add)
                    sq = X256[19]
                    P.tt(sq[:], ytm[:], ytm[:], ALU.mult)
                    s2 = tsm.next()
                    P.reduce(s2[:, 0:4], sq[:].rearrange("p (h d) -> p h d", h=4), ALU.add)
                    mean = tsm.next()
                    P.ts(mean[:, 0:4], s1[:, 0:4], 1.0 / 64, ALU.mult)
                    m2 = tsm.next()
                    P.tt(m2[:, 0:4], mean[:, 0:4], mean[:, 0:4], ALU.mult)
                    var = tsm.next()
                    P.stt(var[:, 0:4], s2[:, 0:4], 1.0 / 64, m2[:, 0:4], ALU.mult, ALU.subtract)
                    rsqrt(var[:, 0:4], var[:, 0:4], C_EX)
                    yn = X256[20]
                    for h in range(4):
                        yield
                        hs = slice(h * 64, (h + 1) * 64)
                        P.ts(yn[:, hs], ytm[:, hs], mean[:, h:h + 1], ALU.subtract, var[:, h:h + 1], ALU.mult)
                    P.tt(yn[:], yn[:], row(l, R_LNG, 256), ALU.mult)
                    P.tt(yn[:], yn[:], row(l, R_LNB, 256), ALU.add)
                    psT = pb()
                    for c in range(2):
                        yield
                        P.tr(psT[:, c * 128:(c + 1) * 128], yn[:, c * 128:(c + 1) * 128], ID)
                    yo = X256[21]
                    P.tt(yo[:], psT[:, 0:256], bon[:], ALU.add)
                    P.tt(o_all[:, 4:6, sl], yo[:].rearrange("p (c t) -> p c t", c=2), gateC[:, :, sl], ALU.mult)


            gd, gr = dn_gen(), rw_gen()
            alive = [gd, gr]
            while alive:
                for g_ in list(alive):
                    try:
                        next(g_)
                    except StopIteration:
                        alive.remove(g_)
                yield
            chk('dn')
            chk('rw')
            if dbg is not None and dbg_d is not None and ti == 0 and l == 0 and dbg[0] == 1024:
                for c in range(8):
                    yield
                    tmp = g512.next()
                    P.copy(tmp[:], o_all[:, c, :])
                    P.dma(dbg_d[c * 128:(c + 1) * 128, :], tmp[:], is_output=True)

            SA.mode = "dense"
            for half in range(2):
                yield
                wO = load_w(wbf["w_out"][l][:, half * 512:(half + 1) * 512], 8, 512)
                for dc in range(4):
                    yield
                    ps = pb()
                    for kc in range(8):
                        yield
                        P.mm(ps[:, 0:TT], wO[:, kc, dc * 128:(dc + 1) * 128], o_all[:, kc, :],
                             start=(kc == 0), stop=(kc == 7))
                    d = half * 4 + dc
                    P.tt(hT[:, d, :], hT[:, d, :], ps[:, 0:TT], ALU.add)


        groups = [[2 * i, 2 * i + 1] for i in range(n_cores // 2)]

        def ccfn(e):
            return e.collective_compute("AllGather", ALU.bypass, replica_groups=groups, ins=[send_d], outs=[recv_d])

        def load_x_tile(step_):
            for s in range(NSUB):
                r0 = step_ * TT + s * 128
                for hf in range(2):
                    P.dma(X512r[2 * s + hf][:], x_d[r0:r0 + 128, hf * 512:(hf + 1) * 512])

        def genB(ti, l, par):
            tok0 = ti * TT
            hT = hTs[par]
            hn = hns[par]
            pb = SB.pb
            g512, w515, wring, xio = SB.g512, SB.w515, SB.wring, SB.xio
            load_w, rmsnorm, transpose_to = make_helpers(SB)

            chk('oproj')
            if ti + 1 <= n_tiles:
                load_x_tile(ti + 1)
            rmsnorm(hT, l * LC + C_NFFN, abuf, out_bf=hn)
            for part in range(2):
                for g in range(6):
                    yield
                    ncl = 2 if g < 5 else 1
                    c0 = part * 11 + g * 2
                    wt = wring.next()
                    vG = wt[:, 0:8 * 128 * ncl].rearrange("p (k n) -> p k n", k=8)
                    vU = wt[:, 2048:2048 + 8 * 128 * ncl].rearrange("p (k n) -> p k n", k=8)
                    P.dma(vG, wbf["w_fg"][l][:, c0 * 128:(c0 + ncl) * 128].rearrange("(k p) n -> p k n", p=128))
                    P.dma(vU, wbf["w_fu"][l][:, c0 * 128:(c0 + ncl) * 128].rearrange("(k p) n -> p k n", p=128))
                    for ci_ in range(ncl):
                        yield
                        c = c0 + ci_
                        psg = pb()
                        for kc in range(8):
                            yield
                            P.mm(psg[:, 0:TT], vG[:, kc, ci_ * 128:(ci_ + 1) * 128], hn[:, kc, :],
                                 start=(kc == 0), stop=(kc == 7))
                        psu = pb()
                        for kc in range(8):
                            yield
                            P.mm(psu[:, 0:TT], vU[:, kc, ci_ * 128:(ci_ + 1) * 128], hn[:, kc, :],
                                 start=(kc == 0), stop=(kc == 7))
                        wk = w515.next()
                        P.copy(wk[:, 2:2 + TT], psg[:, 0:TT], eng="act")
                        P.copy(wk[:, 0:2], ffh[:, l, c, :], eng="pool")
                        acc = g512.next()
                        P.ts(acc[:], wk[:, 2:2 + TT], col(l, C_FFW, c * 3 + 2), ALU.mult)
                        P.stt(acc[:], wk[:, 1:1 + TT], col(l, C_FFW, c * 3 + 1), acc[:], ALU.mult, ALU.add)
                        P.stt(acc[:], wk[:, 0:TT], col(l, C_FFW, c * 3 + 0), acc[:], ALU.mult, ALU.add)
                        P.copy(ffh[:, l, c, :], wk[:, TT:TT + 2], eng="pool")
                        P.act(acc[:], acc[:], AF.Silu)
                        P.tt(abuf[:, c - part * 11, :], acc[:], psu[:, 0:TT], ALU.mult)
                for dg in range(4):
                    yield
                    wd = wring.next()
                    vD = wd[:, 0:11 * 256].rearrange("p (k n) -> p k n", k=11)
                    P.dma(vD, wbf["w_fd"][l][part * 1408:(part + 1) * 1408, dg * 256:(dg + 1) * 256]
                          .rearrange("(k p) n -> p k n", p=128))
                    for dc in range(2):
                        yield
                        ps = pb()
                        for kc in range(11):
                            yield
                            P.mm(ps[:, 0:TT], vD[:, kc, dc * 128:(dc + 1) * 128], abuf[:, kc, :],
                                 start=(kc == 0), stop=(kc == 10))
                        d = dg * 2 + dc
                        P.tt(hT[:, d, :], hT[:, d, :], ps[:, 0:TT], ALU.add)

            chk('ffn')
            rmsnorm(hT, l * LC + C_NPLE, abuf, out_bf=hn)
            pT = abuf
            for s in range(NSUB):
                yield
                pt = xio.next()
                P.dma(pt[:, 0:256], p_d[l, tok0 + s * 128: tok0 + (s + 1) * 128, :])
                transpose_to(lambda i, s=s: pT[:, i, s * 128:(s + 1) * 128], [pt[:, 0:128], pt[:, 128:256]])
            for half in range(2):
                yield
                wGt = load_w(wbf["w_pg"][l][:, half * 512:(half + 1) * 512], 8, 512)
                wP = load_w(wbf["w_pp"][l][:, half * 512:(half + 1) * 512], 2, 512)
                for dc in range(4):
                    yield
                    d = half * 4 + dc
                    psg = pb()
                    for kc in range(8):
                        yield
                        P.mm(psg[:, 0:TT], wGt[:, kc, dc * 128:(dc + 1) * 128], hn[:, kc, :],
                             start=(kc == 0), stop=(kc == 7))
                    psp = pb()
                    for kc in range(2):
                        yield
                        P.mm(psp[:, 0:TT], wP[:, kc, dc * 128:(dc + 1) * 128], pT[:, kc, :], start=(kc == 0), stop=(kc == 1))
                    sg = g512.next()
                    P.act(sg[:], psg[:, 0:TT], AF.Sigmoid)
                    P.tt(sg[:], sg[:], psp[:, 0:TT], ALU.mult)
                    P.tt(hT[:, d, :], hT[:, d, :], sg[:], ALU.add)

            if ti < n_tiles:
                P.dma(send_d, hT[:].rearrange("p c t -> p (c t)"))
                P.emit_cc([recv_d], [send_d], ccfn, cc_sem, 1)
            if ti >= 1:
                rmsnorm(hT, C_FINAL, abuf, out_f32=zcm)
                pending_store.append(ti)

        pending_store = []

        def store_out(S_):
            while pending_store:
                ti_ = pending_store.pop(0)
                otok = (ti_ - 1) * TT
                for s in range(NSUB):
                    ot = S_.xio.next()
                    transpose_pool(S_, lambda i: ot[:, i * 128:(i + 1) * 128], [zcm[:, c, s * 128:(s + 1) * 128] for c in range(8)])
                    P.dma(out_d[otok + s * 128: otok + (s + 1) * 128, :], ot[:], eng="pool", is_output=True)

        try:
            for step in range(n_tiles + 1):
                for _ in genA(step, 0, 0):
                    pass
                for _ in genB(step, 0, 0):
                    pass
                if step == n_tiles:
                    store_out(SA)
                if step == 0:
                    P.ts(ffh[:, 0, :, :], ffh[:, 0, :, :], cc[:, 6:7], ALU.mult)
        except _Stop:
            pass
        P.finish()
    return nc, P


def _colvec(v):
    v = np.asarray(v, np.float32).reshape(-1)
    return v.reshape(-1, 128).T


def pack_shared(inp, role):
    cols = np.zeros((128, NCOL), np.float32)
    rows = np.zeros((1, NROW), np.float32)
    mats = np.zeros((NL, 128, 1024), np.float32)
    l = role
    b = 0
    cols[:, b + C_NMIX:b + C_NMIX + 8] = _colvec(inp["norm_mix_g"][l])
    cols[:, b + C_NFFN:b + C_NFFN + 8] = _colvec(inp["norm_ffn_g"][l])
    cols[:, b + C_NPLE:b + C_NPLE + 8] = _colvec(inp["norm_ple_g"][l])
    cols[:, b + C_DNW:b + C_DNW + 24] = np.asarray(inp["dn_conv_w"][l]).reshape(4, 6, 128).transpose(2, 1, 0).reshape(128, 24)
    cols[:, b + C_MU:b + C_MU + 8] = _colvec(inp["rw_mu"][l])
    cols[:, b + C_A0:b + C_A0 + 2] = _colvec(inp["rw_a0"][l])
    cols[:, b + C_KK:b + C_KK + 2] = _colvec(inp["rw_k_k"][l])
    cols[:, b + C_KA:b + C_KA + 2] = _colvec(inp["rw_k_a"][l])
    cols[:, b + C_RK:b + C_RK + 2] = _colvec(inp["rw_r_k"][l])
    cols[:, b + C_CFW:b + C_CFW + 62] = np.asarray(inp["cf_conv_w"][l]).reshape(31, 2, 128).transpose(2, 1, 0).reshape(128, 62)
    cols[:, b + C_CFB:b + C_CFB + 2] = _colvec(inp["cf_conv_b"][l])
    cols[:, b + C_CFG:b + C_CFG + 2] = _colvec(inp["cf_ln_g"][l])
    cols[:, b + C_CFLB:b + C_CFLB + 2] = _colvec(inp["cf_ln_b"][l])
    cols[:, b + C_FFW:b + C_FFW + 66] = np.asarray(inp["ffn_conv_w"][l]).reshape(3, 22, 128).transpose(2, 1, 0).reshape(128, 66)
    cols[:, b + C_BST:b + C_BST + 4] = np.asarray(inp["gmlp_b_s"][l]).T
    r = 0
    rows[0, r + R_VG:r + R_VG + 256] = inp["gmlp_v_g"][l]
    rows[0, r + R_VB:r + R_VB + 256] = inp["gmlp_v_b"][l]
    rows[0, r + R_ALOG:r + R_ALOG + 4] = inp["dn_a_log"][l]
    rows[0, r + R_DTB:r + R_DTB + 4] = inp["dn_dt_bias"][l]
    rows[0, r + R_OG:r + R_OG + 64] = inp["dn_o_g"][l]
    rows[0, r + R_W0:r + R_W0 + 256] = inp["rw_w0"][l]
    rows[0, r + R_LNG:r + R_LNG + 256] = inp["rw_lnx_g"][l]
    rows[0, r + R_LNB:r + R_LNB + 256] = inp["rw_lnx_b"][l]
    mats[0, 0:64, M_WA2:M_WA2 + 256] = inp["rw_w2"][l]
    mats[0, 64:128, M_WA2:M_WA2 + 256] = inp["rw_a2"][l]
    mats[0, :, M_G2:M_G2 + 256] = inp["rw_g2"][l]
    mats[0, :, M_WST:M_WST + 512] = np.asarray(inp["gmlp_w_s"][l]).transpose(2, 0, 1).reshape(128, 512)
    cols[:, C_FINAL:C_FINAL + 8] = _colvec(inp["final_norm_g"])
    pi = np.arange(128)[:, None]
    fi = np.arange(128)[None, :]
    cst = np.concatenate([(pi == fi), (pi > fi), (pi < fi), (pi <= fi), (pi // 64 == fi // 64),
                          (pi // 64 <= fi // 64)], axis=1).astype(np.float32)
    sh = {"cols_d": cols, "rows_d": rows, "mats_d": mats, "cst_d": np.ascontiguousarray(cst),
          "flag_d": np.full((128, 1), float(role), np.float32)}
    names = {"w_in": "w_in", "w_out": "w_out", "w_fg": "w_ffn_gate", "w_fu": "w_ffn_up", "w_fd": "w_ffn_down",
             "w_pg": "w_ple_gate", "w_pp": "w_ple_proj"}
    for k, src in names.items():
        sh[k] = np.ascontiguousarray(np.asarray(inp[src], np.float32)[l:l + 1])
    return sh


def core_inputs(inputs, shs, c):
    b, role = c // 2, c % 2
    x = np.asarray(inputs["x"], np.float32)
    p = np.asarray(inputs["p"], np.float32)
    m = dict(shs[role])
    if role == 0:
        m["x"] = np.concatenate([x[b], np.zeros((TT, D), np.float32)], axis=0)
        m["p"] = np.concatenate([p[0, b], np.zeros((TT, 256), np.float32)], axis=0)[None]
    else:
        m["x"] = np.zeros((S + TT, D), np.float32)
        m["p"] = np.concatenate([np.zeros((TT, 256), np.float32), p[1, b]], axis=0)[None]
    return m


_CACHE = {}


def kernel(**inputs):
    inputs = {k: np.asarray(v) for k, v in inputs.items()}
    shs = [pack_shared(inputs, 0), pack_shared(inputs, 1)]
    if "nc" not in _CACHE:
        _CACHE["nc"] = build_program()[0]
    nc = _CACHE["nc"]
    in_maps = [core_inputs(inputs, shs, c) for c in range(8)]
    res = run_bass_kernel_spmd(nc, in_maps, core_ids=list(range(8)))
    out = np.stack([res.results[2 * b + 1]["out"] for b in range(4)], axis=0)
    return out.astype(np.float32)
```

```python
import numpy as np
import concourse.bass as bass
import concourse.mybir as mybir

F32 = mybir.dt.float32
BF16 = mybir.dt.bfloat16
ALU = mybir.AluOpType
AF = mybir.ActivationFunctionType
AX = mybir.AxisListType

ENGS = ("pe", "act", "dve", "pool", "sp")
N_DMA_SEM = 24


class Prog:
    def __init__(self, nc):
        self.nc = nc
        self.q = {e: [] for e in ENGS}
        self.cnt = {e: 0 for e in ENGS}
        self.seen = {e: {} for e in ENGS}
        self.acc = {}
        self.esem = {}
        self.dsem = {"sp": [], "pool": [], "act": []}
        self.dma_i = 0
        self.dma_k = {"sp": 0, "pool": 0, "act": 0}
        self.dma_cnt = {}
        self.out_tokens = []
        self.n_wait = 0
        self.track_dram = set()

    def setup_sems(self, stack):
        for e in ("pe", "act", "dve", "pool"):
            self.esem[e] = stack.enter_context(self.nc.semaphore("s_" + e))
        for q, n in (("sp", 16), ("pool", 8)):
            for i in range(n):
                self.dsem[q].append(stack.enter_context(self.nc.semaphore("s_dma_%s%d" % (q, i))))

    @staticmethod
    def _region(ap):
        t = ap.tensor
        shp = list(t.shape)
        rowlen = 1
        for s in shp[1:]:
            rowlen *= s
        off = int(ap.offset)
        apl = ap.ap
        p0 = off // rowlen
        f0 = off % rowlen
        pstep, pcnt = apl[0]
        if pstep == 0:
            p1 = p0 + 1
        else:
            p1 = p0 + (pcnt - 1) * (pstep // rowlen) + 1
        ext = 0
        for st, c in apl[1:]:
            ext += (c - 1) * abs(st)
        if str(ap.space) == "PSUM":
            return (ap.name, p0, p1, 0, rowlen)
        return (ap.name, p0, p1, f0, f0 + ext + 1)

    def _reg2(self, ap):
        if str(ap.space) == "DRAM":
            if ap.name in self.track_dram:
                return (ap.name, 0, 1, 0, 1)
            return None
        return self._region(ap)

    def _need(self, eng, tok, needs):
        if tok is None:
            return
        sem, val, teng, tidx = tok
        if teng == eng:
            if eng == "pe":
                return
        key = id(sem)
        if self.seen[eng].get(key, 0) >= val:
            return
        cur = needs.get(key)
        if cur is None or cur[1] < val:
            needs[key] = (sem, val)

    def emit(self, eng, outs, ins, fn, is_dma=False, is_output=False):
        needs = {}
        regs_in = [r for r in (self._reg2(a) for a in ins if a is not None) if r is not None]
        regs_out = [r for r in (self._reg2(a) for a in outs if a is not None) if r is not None]
        psum_in = [r for r, a in zip(regs_in, [a for a in ins if a is not None and self._reg2(a) is not None])
                   if str(a.space) == "PSUM"]
        if psum_in:
            regs_in = [r for r in regs_in if r not in psum_in]
            regs_out = regs_out + psum_in
        for (name, p0, p1, f0, f1) in regs_in:
            for r in self.acc.get(name, ()):
                if r[5] and r[0] < p1 and p0 < r[1] and r[2] < f1 and f0 < r[3]:
                    self._need(eng, r[4], needs)
        for (name, p0, p1, f0, f1) in regs_out:
            for r in self.acc.get(name, ()):
                if r[0] < p1 and p0 < r[1] and r[2] < f1 and f0 < r[3]:
                    self._need(eng, r[4], needs)
        if is_dma:
            j = self.dma_k[eng] % len(self.dsem[eng])
            self.dma_k[eng] += 1
            self.dma_i += 1
            sem = self.dsem[eng][j]
            prev = self.dma_cnt.get((eng, j), 0)
            if prev > 0:
                if self.seen[eng].get(id(sem), 0) < 16 * prev:
                    needs[id(sem)] = (sem, 16 * prev)
            self.dma_cnt[(eng, j)] = prev + 1
            tok = (sem, 16 * (prev + 1), "dma%s%d" % (eng, j), 0)
            inc = (sem, 16)
        else:
            idx = self.cnt[eng]
            sem = self.esem[eng]
            tok = (sem, idx + 1, eng, idx)
            inc = (sem, 1)
            self.cnt[eng] = idx + 1
        for key, (s, v) in needs.items():
            self.q[eng].append(("w", s, v))
            self.seen[eng][key] = v
            self.n_wait += 1
        self.q[eng].append(("op", fn, inc))
        if is_output:
            self.out_tokens.append(tok)
        for (name, p0, p1, f0, f1) in regs_out:
            lst = self.acc.setdefault(name, [])
            lst[:] = [r for r in lst if not (p0 <= r[0] and r[1] <= p1 and f0 <= r[2] and r[3] <= f1)]
            lst.append((p0, p1, f0, f1, tok, True, eng))
        for (name, p0, p1, f0, f1) in regs_in:
            lst = self.acc.setdefault(name, [])
            if not is_dma:
                lst[:] = [r for r in lst if not ((not r[5]) and r[6] == eng and p0 <= r[0] and r[1] <= p1
                                                 and f0 <= r[2] and r[3] <= f1)]
            lst.append((p0, p1, f0, f1, tok, False, eng if not is_dma else "dma"))

    def mm(self, out, lhsT, rhs, start=True, stop=True):
        self.emit("pe", [out], [lhsT, rhs],
                  lambda e: e.matmul(out, lhsT, rhs, start=start, stop=stop))

    def tr(self, out, in_, ident):
        self.emit("pe", [out], [in_, ident], lambda e: e.transpose(out, in_, ident))

    def act(self, out, in_, func, bias=None, scale=None, accum_out=None, eng="act"):
        kw = {}
        ins = [in_]
        if bias is not None:
            kw["bias"] = bias
            if not isinstance(bias, (int, float)):
                ins.append(bias)
        if scale is not None:
            kw["scale"] = scale
            if not isinstance(scale, (int, float)):
                ins.append(scale)
        outs = [out]
        if accum_out is not None:
            kw["accum_out"] = accum_out
            outs.append(accum_out)
        self.emit("act", outs, ins, lambda e: e.activation(out, in_, func, **kw))

    def tt(self, out, in0, in1, op, eng="dve"):
        self.emit(eng, [out], [in0, in1], lambda e: e.tensor_tensor(out, in0, in1, op))

    def ts(self, out, in0, s1, op0, s2=None, op1=None, eng="dve", accum_out=None):
        ins = [in0] + [s for s in (s1, s2) if s is not None and not isinstance(s, (int, float))]
        outs = [out] + ([accum_out] if accum_out is not None else [])
        kw = {}
        if op1 is not None:
            kw["op1"] = op1
        if accum_out is not None:
            kw["accum_out"] = accum_out
        self.emit(eng, outs, ins, lambda e: e.tensor_scalar(out, in0, s1, s2, op0, **kw))

    def stt(self, out, in0, scalar, in1, op0, op1, eng="dve"):
        eng = "dve"
        ins = [in0, in1] + ([scalar] if not isinstance(scalar, (int, float)) else [])
        self.emit(eng, [out], ins, lambda e: e.scalar_tensor_tensor(out, in0, scalar, in1, op0, op1))

    def copy(self, out, in_, eng="dve"):
        if eng == "act":
            self.emit("act", [out], [in_], lambda e: e.copy(out, in_))
        else:
            self.emit(eng, [out], [in_], lambda e: e.tensor_copy(out, in_))

    def memset(self, ap, val, eng="dve"):
        self.emit(eng, [ap], [], lambda e: e.memset(ap, val))

    def reduce(self, out, in_, op, axis=AX.X, eng="dve"):
        self.emit(eng, [out], [in_], lambda e: e.tensor_reduce(out, in_, axis, op))

    def bn_stats(self, out, in_):
        self.emit("dve", [out], [in_], lambda e: e.bn_stats(out, in_))

    def bn_aggr(self, out, in_):
        self.emit("dve", [out], [in_], lambda e: e.bn_aggr(out, in_))

    def dma(self, out, in_, eng="sp", is_output=False):
        self.emit(eng, [out], [in_], lambda e: e.dma_start(out=out, in_=in_), is_dma=True, is_output=is_output)

    def dma_cast(self, out, in_):
        self.emit("pool", [out], [in_], lambda e: e.dma_start(out=out, in_=in_), is_dma=True)

    def emit_cc(self, outs, ins, fn, sem, inc):
        eng = "pool"
        needs = {}
        regs_in = [r for r in (self._reg2(a) for a in ins) if r is not None]
        regs_out = [r for r in (self._reg2(a) for a in outs) if r is not None]
        for (name, p0, p1, f0, f1) in regs_in:
            for r in self.acc.get(name, ()):
                if r[5] and r[0] < p1 and p0 < r[1] and r[2] < f1 and f0 < r[3]:
                    self._need(eng, r[4], needs)
        for (name, p0, p1, f0, f1) in regs_out:
            for r in self.acc.get(name, ()):
                if r[0] < p1 and p0 < r[1] and r[2] < f1 and f0 < r[3]:
                    self._need(eng, r[4], needs)
        self.cc_n = getattr(self, "cc_n", 0) + 1
        tok = (sem, inc * self.cc_n, "cc", 0)
        for key, (s_, v) in needs.items():
            self.q[eng].append(("w", s_, v))
            self.seen[eng][key] = v
        self.q[eng].append(("op", fn, (sem, inc)))
        for (name, p0, p1, f0, f1) in regs_out:
            lst = self.acc.setdefault(name, [])
            lst[:] = [r for r in lst if not (p0 <= r[0] and r[1] <= p1 and f0 <= r[2] and r[3] <= f1)]
            lst.append((p0, p1, f0, f1, tok, True, "cc"))
        for (name, p0, p1, f0, f1) in regs_in:
            self.acc.setdefault(name, []).append((p0, p1, f0, f1, tok, False, "cc"))

    def wait_token(self, eng, tok):
        needs = {}
        self._need(eng, tok, needs)
        for key, (s, v) in needs.items():
            self.q[eng].append(("w", s, v))
            self.seen[eng][key] = v

    def finish(self):
        for tok in self.out_tokens:
            self.wait_token("sp", tok)
        for e in ("pe", "act", "dve", "pool"):
            if self.cnt[e] > 0:
                self.q["sp"].append(("w", self.esem[e], self.cnt[e]))
        nc = self.nc
        q = self.q

        def replay(name, eng):
            for it in q[name]:
                if it[0] == "w":
                    eng.wait_ge(it[1], it[2])
                else:
                    ins = it[1](eng)
                    ins.then_inc(it[2][0], it[2][1])

        with nc.Block() as block:
            @block.tensor
            def _(e):
                replay("pe", e)

            @block.scalar
            def _(e):
                replay("act", e)

            @block.vector
            def _(e):
                replay("dve", e)

            @block.gpsimd
            def _(e):
                replay("pool", e)

            @block.sync
            def _(e):
                replay("sp", e)

from contextlib import ExitStack
from concourse.bass_utils import run_bass_kernel_spmd

D = 1024
S = 4096
TT = 256
NSUB = TT // 128
DIN = 3080
DFF = 2816
NKF = 22
NL = 2
LC = 202
LR = 1352
C_NMIX, C_NFFN, C_NPLE, C_DNW, C_MU, C_A0, C_KK, C_KA, C_RK = 0, 8, 16, 24, 48, 56, 58, 60, 62
C_CFW, C_CFB, C_CFG, C_CFLB, C_FFW, C_BST = 64, 126, 128, 130, 132, 198
C_FINAL = NL * LC
NCOL = C_FINAL + 8
R_VG, R_VB, R_ALOG, R_DTB, R_OG, R_W0, R_LNG, R_LNB = 0, 256, 512, 516, 520, 584, 840, 1096
NROW = NL * LR
M_WA2, M_G2, M_WST = 0, 256, 512
K_ID, K_MSL, K_MSU, K_MUI, K_BLK, K_GMT = 0, 128, 256, 384, 512, 640
EXPM05 = 0.6065306597126334


class _Stop(Exception):
    pass


def build_program(n_tiles=S // TT, n_layers=1, dbg=None, stop=None, prologue=True, n_cores=8):
    nc = bass.Bass("TRN2", target_bir_lowering=False)
    dt = nc.dram_tensor
    n_layers = 1
    x_d = dt("x", [S + TT, D], F32, kind="ExternalInput").ap()
    p_d = dt("p", [1, S + TT, 256], F32, kind="ExternalInput").ap()
    flag_d = dt("flag_d", [128, 1], F32, kind="ExternalInput").ap()
    send_d = dt("send_d", [128, 8 * TT], F32, kind="Internal").ap()
    recv_d = dt("recv_d", [256, 8 * TT], F32, kind="Internal").ap()
    cols_d = dt("cols_d", [128, NCOL], F32, kind="ExternalInput").ap()
    rows_d = dt("rows_d", [1, NROW], F32, kind="ExternalInput").ap()
    mats_d = dt("mats_d", [NL, 128, 1024], F32, kind="ExternalInput").ap()
    cst_d = dt("cst_d", [128, 768], F32, kind="ExternalInput").ap()
    wshapes = {"w_in": [D, DIN], "w_out": [D, D], "w_fg": [D, DFF], "w_fu": [D, DFF],
               "w_fd": [DFF, D], "w_pg": [D, D], "w_pp": [256, D]}
    w32 = {k: dt(k, [1] + v, F32, kind="ExternalInput").ap() for k, v in wshapes.items()}
    wbf = {k: dt(k + "_bf", [1] + v, BF16, kind="Internal").ap() for k, v in wshapes.items()}
    out_d = dt("out", [S, D], F32, kind="ExternalOutput").ap()
    dbg_d = None
    if dbg is not None:
        dbg_d = dt("dbg", list(dbg), F32, kind="ExternalOutput").ap()

    P = Prog(nc)
    for k in wshapes:
        P.track_dram.add(k + "_bf")
    P.track_dram.update(["send_d", "recv_d"])

    with ExitStack() as st:
        P.setup_sems(st)
        cc_sem = st.enter_context(nc.semaphore("s_cc"))
        sb = lambda n, s, d=F32: st.enter_context(nc.sbuf_tensor(n, s, d))
        banks = [st.enter_context(nc.psum_tensor("ps%d" % i, [128, 512], F32)) for i in range(8)]
        bank_i = [0]

        def pb():
            b = banks[bank_i[0] % 8]
            bank_i[0] += 1
            return b

        class Ring:
            def __init__(self, name, n, shape, d=F32):
                self.t = [sb("%s%d" % (name, i), shape, d) for i in range(n)]
                self.i = 0

            def next(self):
                t = self.t[self.i % len(self.t)]
                self.i += 1
                return t

        cols = sb("cols", [128, NCOL + 8])
        rows = sb("rows", [128, NROW])
        mats = sb("mats", [128, NL, 1024])
        cst = sb("cst", [128, 768])
        cc = sb("cc", [128, 8])
        ones_bf = sb("ones_bf", [128, 128], BF16)
        ones_f = sb("ones_f", [128, 128])
        o256_f = sb("o256_f", [128, 128])
        triS = sb("triS", [128, 256])
        wmT = sb("wmT", [128, NL, 512], BF16)
        nexpA = sb("nexpA", [128, NL, 4])
        hTs = [sb("hT%d" % i, [128, 8, TT]) for i in range(1)]
        hns = [sb("hn%d" % i, [128, 8, TT], BF16) for i in range(1)]
        hT = hTs[0]
        abuf = sb("abuf", [128, 11, TT], BF16)
        o_all = sb("o_all", [128, 8, TT], BF16)
        Sblk = sb("Sblk", [128, NL, 2, 128])
        Zblk = sb("Zblk", [128, NL, 2, 128])
        qkvh = sb("qkvh", [128, NL, 6, 3])
        zch = sb("zch", [128, NL, 8, 1])
        cfh = sb("cfh", [128, NL, 2, 30])
        ffh = sb("ffh", [128, NL, NKF, 2])
        qkvs = sb("qkvs", [128, 6, TT])
        gateS = sb("gateS", [128, NSUB, 256])
        betaT = sb("betaT", [128, NSUB, 4])
        nbetaT = sb("nbetaT", [128, NSUB, 4])
        gT = sb("gT", [128, NSUB, 4])
        zcm = sb("zcm", [128, 8, TT])
        asig = sb("asig", [128, 2, TT])
        gateC = sb("gateC", [128, 2, TT])
        thx = sb("thx", [128, TT])
        hbuf = sb("hbuf", [128, 2, 30 + TT])
        KbeP = sb("KbeP", [128, 4, 128])
        AtP = sb("AtP", [128, 4, 128])
        tinv_d = Ring("tinvd", 6, [128, 512])
        tinv_r = Ring("tinvr", 6, [128, 512])
        X512d = [sb("X512d_%d" % i, [128, 512]) for i in range(8)]
        X512r = [sb("X512r_%d" % i, [128, 512]) for i in range(6)]
        X256d = [sb("X256d_%d" % i, [128, 256]) for i in range(12)]
        X256r = [sb("X256r_%d" % i, [128, 256]) for i in range(22)]
        X512, X256 = X512d, X256d
        tsm = Ring("tsm", 8, [128, 8])
        tsm_d = Ring("tsmd", 10, [128, 8])
        tsm_r = Ring("tsmr", 10, [128, 8])
        tb256 = Ring("tb256", 2, [128, 256], BF16)

        ID = cst[:, K_ID:K_ID + 128]
        MSL = cst[:, K_MSL:K_MSL + 128]
        MSU = cst[:, K_MSU:K_MSU + 128]
        MUI = cst[:, K_MUI:K_MUI + 128]
        BLK = cst[:, K_BLK:K_BLK + 128]
        GMT = cst[:, K_GMT:K_GMT + 128]

        def b4(m):
            return m.unsqueeze(1).to_broadcast([128, 4, 128])

        def v4(t):
            return t.rearrange("p (h f) -> p h f", h=4)

        def vpar(t, par):
            return t.rearrange("p (a b f) -> p a b f", a=2, b=2)[:, :, par, :]

        def b2(m):
            return m.unsqueeze(1).to_broadcast([128, 2, 128])

        def v2(t):
            return t.rearrange("p (h f) -> p h f", h=2)

        C_ONE, C_E6, C_E5, C_EX, C_ZERO = 0, 1, 2, 3, 4

        def ccol(i):
            return cc[:, i:i + 1]

        def col(l, off, i=0, n=1):
            b = l * LC + off + i
            return cols[:, b:b + n]

        def row(l, off, n):
            b = l * LR + off
            return rows[:, b:b + n]

        def recip(out, in_):
            P.emit("dve", [out], [in_], lambda e: e.reciprocal(out, in_))

        def rsqrt(out, in_, eps_col, scale=1.0):
            P.act(out, in_, AF.Sqrt, bias=ccol(eps_col), scale=scale)
            recip(out, out)

        P.dma(cols[:, 0:NCOL], cols_d)
        P.dma(rows[:], rows_d.to_broadcast([128, NROW]))
        P.dma(mats[:], mats_d.rearrange("l p n -> p l n"))
        P.dma(cst[:], cst_d)
        P.dma(cc[:, 5:6], flag_d)
        P.ts(cc[:, 6:7], cc[:, 5:6], -1.0, ALU.mult, 1.0, ALU.add)
        P.memset(cc[:, 0:1], 1.0)
        P.memset(cc[:, 1:2], 1e-6)
        P.memset(cc[:, 2:3], 1e-5)
        P.memset(cc[:, 3:4], 64e-5)
        P.memset(cc[:, 4:5], 0.0)
        P.memset(ones_bf[:], 1.0)
        P.memset(ones_f[:], 1.0)
        P.memset(o256_f[:], 1.0 / 256.0)
        P.ts(triS[:, 0:128], MUI, -EXPM05, ALU.mult)
        P.ts(triS[:, 128:256], MSU, -EXPM05, ALU.mult)
        for l in range(NL):
            P.tt(wmT[:, l, :].rearrange("p (h f) -> p h f", h=4),
                 mats[:, l, M_WST:M_WST + 512].rearrange("p (h f) -> p h f", h=4), b4(GMT), ALU.mult)
            P.act(nexpA[:, l, :], row(l, R_ALOG, 4), AF.Exp)
            P.ts(nexpA[:, l, :], nexpA[:, l, :], -1.0, ALU.mult)
            P.ts(cols[:, NCOL + 2 * l:NCOL + 2 * l + 2], col(l, C_KA, 0, 2), -1.0, ALU.mult, 1.0, ALU.add)
        for t_ in (Sblk, Zblk, qkvh, zch, cfh, ffh, KbeP, AtP):
            P.memset(t_[:], 0.0)

        hT_flat = hT[:].rearrange("p c t -> p (c t)")
        abuf_flat = abuf[:].rearrange("p c t -> p (c t)")
        zcm_flat = zcm[:].rearrange("p c t -> p (c t)")
        oall_flat = o_all[:].rearrange("p c t -> p (c t)")
        W_ = 4 * TT
        stg32 = [hT_flat[:, 0:W_], hT_flat[:, W_:2 * W_], zcm_flat[:, 0:W_], zcm_flat[:, W_:2 * W_]]
        stg16 = [abuf_flat[:, 0:W_], abuf_flat[:, W_:2 * W_], oall_flat[:, 0:W_], oall_flat[:, W_:2 * W_]]
        ci = 0
        for l in range(n_layers if prologue else 0):
            for k, shp in wshapes.items():
                n_el = shp[0] * shp[1] // 128
                src = w32[k][l].rearrange("(p a) n -> p (a n)", p=128)
                dst = wbf[k][l].rearrange("(p a) n -> p (a n)", p=128)
                off = 0
                while off < n_el:
                    w_ = min(4 * TT, n_el - off)
                    stg = stg32[ci % 4][:, 0:w_]
                    P.dma(stg, src[:, off:off + w_])
                    ob = stg16[ci % 4]
                    eng = ("dve", "act")[ci % 2]
                    P.copy(ob[:, 0:w_], stg, eng=eng)
                    P.dma(dst[:, off:off + w_], ob[:, 0:w_], eng="pool" if ci % 2 else "sp")
                    off += w_
                    ci += 1

        class Stream:
            def __init__(self, name, bank_list, n_g, n_w, n_x):
                self.banks = bank_list
                self.bi = 0
                self.g512 = Ring("g512" + name, n_g, [128, TT])
                self.w515 = Ring("w515" + name, 2, [128, 516])
                self.wring = Ring("wb" + name, n_w, [128, 4096], BF16)
                self.xio = Ring("xio" + name, n_x, [128, 1024])
                self.tr_i = 0

            def pb(self):
                b = self.banks[self.bi % len(self.banks)]
                self.bi += 1
                return b

        SA = Stream("A", banks[0:8], 4, 3, 1)
        SB = SA

        class Pool:
            def __init__(self, bl):
                self.banks = bl
                self.bi = 0
                self.tr_i = 0

            def pb(self):
                b = self.banks[self.bi % len(self.banks)]
                self.bi += 1
                return b

        PD = Pool(banks[0:4])
        PR = Pool(banks[4:8])
        PRest = Pool(banks[0:6])
        PConf = Pool(banks[6:8])

        def transpose_pool(S_, dst_fn, src_aps):
            i = 0
            while i < len(src_aps):
                ps = S_.pb()
                n = min(4, len(src_aps) - i)
                for j in range(n):
                    P.tr(ps[:, j * 128:(j + 1) * 128], src_aps[i + j], ID)
                eng_ = ("act", "dve")[S_.tr_i % 2]
                S_.tr_i += 1
                for j in range(n):
                    P.copy(dst_fn(i + j), ps[:, j * 128:(j + 1) * 128], eng=eng_)
                i += n

        def make_helpers(S_):
            def load_w(ap_dram, kc, ncol):
                wt = S_.wring.next()
                v = wt[:, 0:kc * ncol].rearrange("p (k n) -> p k n", k=kc)
                P.dma(v, ap_dram.rearrange("(k p) n -> p k n", p=128))
                return v

            def rmsnorm(hT, gcol_off_layer, sq, out_bf=None, out_f32=None):
                for c in range(8):
                    P.act(sq[:, c, :], hT[:, c, :], AF.Square)
                ps = S_.pb()
                for c in range(8):
                    P.mm(ps[:, 0:TT], ones_bf[:], sq[:, c, :], start=(c == 0), stop=(c == 7))
                rstd = S_.g512.next()
                rsqrt(rstd[:], ps[:, 0:TT], C_E6, scale=1.0 / D)
                for c in range(8):
                    g = cols[:, gcol_off_layer + c:gcol_off_layer + c + 1]
                    o = out_bf[:, c, :] if out_bf is not None else out_f32[:, c, :]
                    P.stt(o, hT[:, c, :], g, rstd[:], ALU.mult, ALU.mult)

            def transpose_to(dst_fn, src_aps):
                i = 0
                while i < len(src_aps):
                    ps = S_.pb()
                    n = min(4, len(src_aps) - i)
                    for j in range(n):
                        P.tr(ps[:, j * 128:(j + 1) * 128], src_aps[i + j], ID)
                    eng_ = ("act", "dve")[S_.tr_i % 2]
                    S_.tr_i += 1
                    for j in range(n):
                        P.copy(dst_fn(i + j), ps[:, j * 128:(j + 1) * 128], eng=eng_)
                    i += n
            return load_w, rmsnorm, transpose_to

        def tri_inv(Nn, NT, tinv, pb):
            Tt = tinv.next()
            P.tt(v4(Tt[:]), v4(NT[:]), b4(ID), ALU.add)
            cN, cNT = Nn, NT
            for k in range(1, 7):
                yield
                psA = pb()
                for h in range(4):
                    hs = slice(h * 128, (h + 1) * 128)
                    P.mm(psA[:, hs], cNT[:, hs], cN[:, hs])
                nN = tinv.next()
                P.copy(nN[:], psA[:, :], eng="act")
                nNT = None
                if k < 6:
                    yield
                    psB = pb()
                    for h in range(4):
                        hs = slice(h * 128, (h + 1) * 128)
                        P.mm(psB[:, hs], cN[:, hs], cNT[:, hs])
                    nNT = tinv.next()
                    P.copy(nNT[:], psB[:, :], eng="dve")
                yield
                psC = pb()
                for h in range(4):
                    hs = slice(h * 128, (h + 1) * 128)
                    P.mm(psC[:, hs], nN[:, hs], Tt[:, hs])
                nT = tinv.next()
                P.tt(nT[:], Tt[:], psC[:, :], ALU.add)
                Tt = nT
                cN, cNT = nN, nNT
            return Tt

        def chk(name):
            if stop == name:
                raise _Stop()

        def genA(ti, l, par):
            tok0 = ti * TT
            hT = hTs[par]
            hn = hns[par]
            pb = SA.pb
            g512, w515, wring, xio = SA.g512, SA.w515, SA.wring, SA.xio
            load_w, rmsnorm, transpose_to = make_helpers(SA)
            SA.mode = "dense"
            if l == 0:

                if ti == 0:
                    load_x_tile(0)
                for s in range(NSUB):
                    yield
                    transpose_to(lambda i, s=s: hT[:, i, s * 128:(s + 1) * 128],
                                 [X512r[2 * s + c // 4][:, (c % 4) * 128:(c % 4 + 1) * 128] for c in range(8)])

            w_in = wbf["w_in"][l]
            if ti >= 1:
                rcv = X512[0:4]
                for q in range(4):
                    P.dma(rcv[q][:], recv_d[0:128, q * 512:(q + 1) * 512])
                hflat = hT[:].rearrange("p c t -> p (c t)")
                for q in range(4):
                    yield
                    P.stt(hflat[:, q * 512:(q + 1) * 512], rcv[q][:], cc[:, 5:6], hflat[:, q * 512:(q + 1) * 512],
                          ALU.mult, ALU.add)
            chk('x')
            rmsnorm(hT, l * LC + C_NMIX, o_all, out_bf=hn)

            wD = load_w(w_in[:, 2568:3080], 8, 512)
            for c in range(2):
                yield
                ps1 = pb()
                for kc in range(8):
                    yield
                    P.mm(ps1[:, 0:TT], wD[:, kc, c * 128:(c + 1) * 128], hn[:, kc, :], start=(kc == 0), stop=(kc == 7))
                ps2 = pb()
                for kc in range(8):
                    yield
                    P.mm(ps2[:, 0:TT], wD[:, kc, 256 + c * 128:256 + (c + 1) * 128], hn[:, kc, :],
                         start=(kc == 0), stop=(kc == 7))
                sg = g512.next()
                P.act(sg[:], ps2[:, 0:TT], AF.Sigmoid)
                P.copy(hbuf[:, c, 0:30], cfh[:, l, c, :], eng="pool")
                P.tt(hbuf[:, c, 30:30 + TT], ps1[:, 0:TT], sg[:], ALU.mult)
                P.copy(cfh[:, l, c, :], hbuf[:, c, TT:TT + 30], eng="pool")

            def rest_gen():
                pb = PRest.pb
                transpose_to = lambda d_, s_: transpose_pool(PRest, d_, s_)
                wA = load_w(w_in[:, 0:512], 8, 512)
                for s in range(NSUB):
                    yield
                    ss = slice(s * 128, (s + 1) * 128)
                    ps = pb()
                    for kc in range(8):
                        yield
                        P.mm(ps[:, :], hn[:, kc, ss], wA[:, kc, :], start=(kc == 0), stop=(kc == 7))
                    xz = X512r[3]
                    P.copy(xz[:], ps[:, :], eng="act")
                    x2 = X512r[4]
                    P.tt(x2[:], xz[:], xz[:], ALU.mult)
                    P.ts(x2[:], x2[:], 0.044715, ALU.mult, 1.0, ALU.add)
                    P.tt(x2[:], x2[:], xz[:], ALU.mult)
                    P.act(x2[:], x2[:], AF.Sigmoid, scale=1.5957691216057308)
                    gl = X512r[5]
                    P.tt(gl[:], xz[:], x2[:], ALU.mult)
                    st6 = tsm.next()
                    P.bn_stats(st6[:, 0:6], gl[:, 256:512])
                    mv = tsm.next()
                    P.bn_aggr(mv[:, 0:2], st6[:, 0:6])
                    rs = tsm.next()
                    rsqrt(rs[:, 0:1], mv[:, 1:2], C_E5)
                    vn = X256[0]
                    P.ts(vn[:], gl[:, 256:512], mv[:, 0:1], ALU.subtract, rs[:, 0:1], ALU.mult)
                    P.tt(vn[:], vn[:], row(l, R_VG, 256), ALU.mult)
                    vnb = tb256.next()
                    P.tt(vnb[:], vn[:], row(l, R_VB, 256), ALU.add)
                    ps2 = pb()
                    for h in range(4):
                        yield
                        P.mm(ps2[:, h * 64:(h + 1) * 64], wmT[:, l, h * 128:(h + 1) * 128], vnb[:, h * 64:(h + 1) * 64])
                    oa = X256[1]
                    for h in range(4):
                        yield
                        hs = slice(h * 64, (h + 1) * 64)
                        P.stt(oa[:, hs], ps2[:, hs], col(l, C_BST, h), gl[:, hs], ALU.add, ALU.mult)
                    transpose_to(lambda i, s=s: o_all[:, i, s * 128:(s + 1) * 128],
                                 [oa[:, 0:128], oa[:, 128:256]])

                chk('A')
                for half in range(2):
                    yield
                    wB = load_w(w_in[:, 512 + half * 384: 512 + (half + 1) * 384], 8, 384)
                    for cc_ in range(3):
                        yield
                        c = half * 3 + cc_
                        ps = pb()
                        for kc in range(8):
                            yield
                            P.mm(ps[:, 0:TT], wB[:, kc, cc_ * 128:(cc_ + 1) * 128], hn[:, kc, :],
                                 start=(kc == 0), stop=(kc == 7))
                        wk = w515.next()
                        P.copy(wk[:, 3:3 + TT], ps[:, 0:TT], eng="act")
                        P.copy(wk[:, 0:3], qkvh[:, l, c, :], eng="pool")
                        acc = g512.next()
                        P.ts(acc[:], wk[:, 3:3 + TT], col(l, C_DNW, c * 4 + 3), ALU.mult)
                        for k in (2, 1, 0):
                            yield
                            P.stt(acc[:], wk[:, k:k + TT], col(l, C_DNW, c * 4 + k), acc[:], ALU.mult, ALU.add)
                        P.copy(qkvh[:, l, c, :], wk[:, TT:TT + 3], eng="pool")
                        P.act(qkvs[:, c, :], acc[:], AF.Silu)

                chk('B')
                wG = load_w(w_in[:, 1280:1544], 8, 264)
                for s in range(NSUB):
                    yield
                    ss = slice(s * 128, (s + 1) * 128)
                    ps = pb()
                    for kc in range(8):
                        yield
                        P.mm(ps[:, 0:264], hn[:, kc, ss], wG[:, kc, :], start=(kc == 0), stop=(kc == 7))
                    P.act(gateS[:, s, :], ps[:, 0:256], AF.Silu)
                    P.act(betaT[:, s, :], ps[:, 256:260], AF.Sigmoid)
                    P.ts(nbetaT[:, s, :], betaT[:, s, :], -1.0, ALU.mult)
                    sp_ = tsm.next()
                    P.tt(sp_[:, 0:4], ps[:, 260:264], row(l, R_DTB, 4), ALU.add)
                    P.act(sp_[:, 0:4], sp_[:, 0:4], AF.Exp)
                    P.act(sp_[:, 0:4], sp_[:, 0:4], AF.Ln, bias=ccol(C_ONE), scale=1.0)
                    P.tt(gT[:, s, :], sp_[:, 0:4], nexpA[:, l, :], ALU.mult)

                chk('B2')
                for half in range(2):
                    yield
                    wC = load_w(w_in[:, 1544 + half * 512: 1544 + (half + 1) * 512], 8, 512)
                    for cc_ in range(4):
                        yield
                        c = half * 4 + cc_
                        ps = pb()
                        for kc in range(8):
                            yield
                            P.mm(ps[:, 0:TT], wC[:, kc, cc_ * 128:(cc_ + 1) * 128], hn[:, kc, :],
                                 start=(kc == 0), stop=(kc == 7))
                        wk = w515.next()
                        P.copy(wk[:, 1:1 + TT], ps[:, 0:TT], eng="act")
                        P.copy(wk[:, 0:1], zch[:, l, c, :], eng="pool")
                        d_ = g512.next()
                        P.tt(d_[:], wk[:, 0:TT], wk[:, 1:1 + TT], ALU.subtract)
                        P.stt(zcm[:, c, :], d_[:], col(l, C_MU, c), wk[:, 1:1 + TT], ALU.mult, ALU.add)
                        P.copy(zch[:, l, c, :], wk[:, TT:TT + 1], eng="pool")


            def conf_gen():
                pb = PConf.pb
                accs = []
                for c in range(2):
                    yield
                    accA = X512[2 * c][:, 0:TT]
                    P.ts(accA[:], hbuf[:, c, 0:TT], col(l, C_CFW, c * 31 + 0), ALU.mult, col(l, C_CFB, c), ALU.add)
                    for k in range(1, 31):
                        yield
                        P.stt(accA[:], hbuf[:, c, k:k + TT], col(l, C_CFW, c * 31 + k), accA[:], ALU.mult, ALU.add)
                    accs.append(accA)
                psm = pb()
                pss = pb()
                for c in range(2):
                    yield
                    P.mm(psm[:, 0:TT], o256_f[:], accs[c][:], start=(c == 0), stop=(c == 1))
                sqs = []
                for c in range(2):
                    yield
                    sq = X512[2 * c + 1][:, 0:TT]
                    P.act(sq[:], accs[c][:], AF.Square)
                    sqs.append(sq)
                for c in range(2):
                    yield
                    P.mm(pss[:, 0:TT], o256_f[:], sqs[c][:], start=(c == 0), stop=(c == 1))
                mean = X512[4][:, 0:TT]
                P.copy(mean[:], psm[:, 0:TT], eng="act")
                var = X512[5][:, 0:TT]
                P.tt(var[:], mean[:], mean[:], ALU.mult)
                P.tt(var[:], pss[:, 0:TT], var[:], ALU.subtract)
                rsqrt(var[:], var[:], C_E5)
                for c in range(2):
                    yield
                    t1 = X512[6 + c][:, 0:TT]
                    P.tt(t1[:], accs[c][:], mean[:], ALU.subtract)
                    P.tt(t1[:], t1[:], var[:], ALU.mult)
                    P.ts(t1[:], t1[:], col(l, C_CFG, c), ALU.mult, col(l, C_CFLB, c), ALU.add)
                    P.act(o_all[:, 6 + c, :], t1[:], AF.Silu)


            gc_, gr_ = conf_gen(), rest_gen()
            c_alive = r_alive = True
            while c_alive or r_alive:
                for _ in range(4):
                    if r_alive:
                        try:
                            next(gr_)
                        except StopIteration:
                            r_alive = False
                if c_alive:
                    try:
                        next(gc_)
                    except StopIteration:
                        c_alive = False
                yield
            SA.mode = "chain"
            chk('conf')
            def dn_gen():
                pb = PD.pb
                X256, X512, tsm = X256d, X512d, tsm_d
                transpose_to = lambda d_, s_: transpose_pool(PD, d_, s_)
                for c in range(4):
                    yield
                    sq = g512.next()
                    P.act(sq[:], qkvs[:, c, :], AF.Square)
                    ps = pb()
                    P.mm(ps[:, 0:TT], BLK, sq[:])
                    rn = g512.next()
                    rsqrt(rn[:], ps[:, 0:TT], C_E6)
                    if c < 2:
                        P.stt(qkvs[:, c, :], qkvs[:, c, :], 0.125, rn[:], ALU.mult, ALU.mult)
                    else:
                        P.tt(qkvs[:, c, :], qkvs[:, c, :], rn[:], ALU.mult)
                for j in range(NSUB):
                    yield
                    sl = slice(j * 128, (j + 1) * 128)
                    KV = X512[0]
                    transpose_to(lambda i: KV[:, i * 128:(i + 1) * 128], [qkvs[:, 2 + i, sl] for i in range(4)])
                    Ktm = KV[:, 0:256]
                    Vtm = KV[:, 256:512]
                    psg = pb()
                    P.mm(psg[:, 0:4], MUI, gT[:, j, :])
                    gcum = tsm.next()
                    P.copy(gcum[:, 0:4], psg[:, 0:4])
                    dGg, dGb = X512[6], X512[7]
                    for h in range(4):
                        yield
                        P.ts(dGg[:, h * 128:(h + 1) * 128], ID, gcum[:, h:h + 1], ALU.mult)
                        P.ts(dGb[:, h * 128:(h + 1) * 128], ID, betaT[:, j, h:h + 1], ALU.mult)
                    psG = pb()
                    P.mm(psG[:, :], ones_f[:], dGg[:])
                    psB = pb()
                    P.mm(psB[:, :], ones_f[:], dGb[:])
                    Grow = X512[1]
                    P.copy(Grow[:], psG[:, :], eng="act")
                    D1 = X512[2]
                    for h in range(4):
                        yield
                        hs = slice(h * 128, (h + 1) * 128)
                        P.ts(D1[:, hs], Grow[:, hs], -1.0, ALU.mult, gcum[:, h:h + 1], ALU.add)
                    Esl = X512[3]
                    P.tt(v4(Esl[:]), v4(D1[:]), b4(MSL), ALU.mult)
                    P.act(Esl[:], Esl[:], AF.Exp)
                    P.tt(v4(Esl[:]), v4(Esl[:]), b4(MSL), ALU.mult)
                    Eui = X512[4]
                    P.tt(v4(Eui[:]), v4(D1[:]), b4(MUI), ALU.mult)
                    P.act(Eui[:], Eui[:], AF.Exp, scale=-1.0)
                    P.tt(v4(Eui[:]), v4(Eui[:]), b4(MUI), ALU.mult)
                    EB = X512[2]
                    P.tt(v4(EB[:]), v4(Eui[:]), b4(MSU), ALU.mult)
                    P.stt(EB[:], EB[:], -1.0, psB[:, :], ALU.mult, ALU.mult)
                    Erow = X512[5]
                    P.act(Erow[:], Grow[:], AF.Exp)
                    psKK = [pb(), pb()]
                    psKQ = [pb(), pb()]
                    for h in range(4):
                        yield
                        c_, b_ = h // 2, (h % 2) * 64
                        P.mm(psKK[h % 2][:, c_ * 128:(c_ + 1) * 128], qkvs[b_:b_ + 64, 2 + c_, sl], qkvs[b_:b_ + 64, 2 + c_, sl])
                    for h in range(4):
                        yield
                        c_, b_ = h // 2, (h % 2) * 64
                        P.mm(psKQ[h % 2][:, c_ * 128:(c_ + 1) * 128], qkvs[b_:b_ + 64, 2 + c_, sl], qkvs[b_:b_ + 64, c_, sl])
                    Nn = X512[6]
                    for h in range(4):
                        yield
                        hs = slice(h * 128, (h + 1) * 128)
                        P.stt(Nn[:, hs], psKK[h % 2][:, (h // 2) * 128:(h // 2 + 1) * 128], nbetaT[:, j, h:h + 1],
                              Esl[:, hs], ALU.mult, ALU.mult)
                    NT = X512[7]
                    for par in range(2):
                        yield
                        P.tt(vpar(NT[:], par), v2(psKK[par][:, 0:256]), vpar(EB[:], par), ALU.mult)
                    attnT = X512[3]
                    for par in range(2):
                        yield
                        P.tt(vpar(attnT[:], par), v2(psKQ[par][:, 0:256]), vpar(Eui[:], par), ALU.mult)
                    Tt = yield from tri_inv(Nn, NT, tinv_d, pb)
                    eg = tsm.next()
                    P.act(eg[:, 0:4], gcum[:, 0:4], AF.Exp)
                    P.tt(eg[:, 0:4], eg[:, 0:4], betaT[:, j, :], ALU.mult)
                    Vb = X256[0]
                    for h in range(4):
                        yield
                        hs = slice(h * 64, (h + 1) * 64)
                        P.ts(Vb[:, hs], Vtm[:, hs], betaT[:, j, h:h + 1], ALU.mult)
                        P.ts(KbeP[:, h, (h % 2) * 64:(h % 2) * 64 + 64], Ktm[:, hs], eg[:, h:h + 1], ALU.mult)
                    psU = pb()
                    for h in range(4):
                        yield
                        hs = slice(h * 128, (h + 1) * 128)
                        P.mm(psU[:, h * 64:(h + 1) * 64], Tt[:, hs], Vb[:, h * 64:(h + 1) * 64])
                    Usb = X256[1]
                    P.copy(Usb[:], psU[:, 0:256], eng="act")
                    psW = pb()
                    for pr in range(2):
                        yield
                        for hh in range(2):
                            yield
                            h = pr * 2 + hh
                            P.mm(psW[:, pr * 128:(pr + 1) * 128], KbeP[:, h, :], Tt[:, h * 128:(h + 1) * 128],
                                 start=(hh == 0), stop=(hh == 1))
                    WT = X256[2]
                    P.copy(WT[:], psW[:, 0:256])
                    Qd = X256[3]
                    for pr in range(2):
                        yield
                        for hh in range(2):
                            yield
                            h = pr * 2 + hh
                            b_ = hh * 64
                            P.tt(Qd[b_:b_ + 64, pr * 128:(pr + 1) * 128], qkvs[b_:b_ + 64, pr, sl],
                                 Erow[b_:b_ + 64, h * 128:(h + 1) * 128], ALU.mult)
                    glast = Grow[:].rearrange("p (h f) -> p h f", h=4)[:, :, 127]
                    kds = tsm.next()
                    P.tt(kds[:, 0:4], glast, gcum[:, 0:4], ALU.subtract)
                    P.act(kds[:, 0:4], kds[:, 0:4], AF.Exp)
                    egl = tsm.next()
                    P.act(egl[:, 0:4], glast, AF.Exp)
                    Kd = X256[4]
                    for h in range(4):
                        yield
                        hs = slice(h * 64, (h + 1) * 64)
                        P.ts(Kd[:, hs], Ktm[:, hs], kds[:, h:h + 1], ALU.mult)
                    otm = X256[5]
                    for pr in range(2):
                        yield
                        prs = slice(pr * 128, (pr + 1) * 128)
                        ps1 = pb()
                        P.mm(ps1[:, 0:128], WT[:, prs], Sblk[:, l, pr, :])
                        vnew = X256[6 + pr]
                        P.tt(vnew[:, 0:128], Usb[:, prs], ps1[:, 0:128], ALU.subtract)
                        ps2 = pb()
                        P.mm(ps2[:, 0:128], Qd[:, prs], Sblk[:, l, pr, :], start=True, stop=False)
                        for hh in range(2):
                            yield
                            h = pr * 2 + hh
                            P.mm(ps2[:, hh * 64:(hh + 1) * 64], attnT[:, h * 128:(h + 1) * 128],
                                 vnew[:, hh * 64:(hh + 1) * 64], start=False, stop=(hh == 1))
                        P.copy(otm[:, prs], ps2[:, 0:128], eng="act")
                        ps3 = pb()
                        P.mm(ps3[:, 0:128], Kd[:, prs], vnew[:, 0:128])
                        tm = X256[8 + pr]
                        P.tt(tm[:, 0:128], ps3[:, 0:128], BLK, ALU.mult)
                        for hh in range(2):
                            yield
                            h = pr * 2 + hh
                            b_ = hh * 64
                            P.stt(Sblk[b_:b_ + 64, l, pr, :], Sblk[b_:b_ + 64, l, pr, :], egl[b_:b_ + 64, h:h + 1],
                                  tm[b_:b_ + 64, 0:128], ALU.mult, ALU.add)
                    sq = X256[10]
                    P.tt(sq[:], otm[:], otm[:], ALU.mult)
                    ssq = tsm.next()
                    P.reduce(ssq[:, 0:4], sq[:].rearrange("p (h d) -> p h d", h=4), ALU.add)
                    rsqrt(ssq[:, 0:4], ssq[:, 0:4], C_E6, scale=1.0 / 64)
                    ob = X256[11]
                    for h in range(4):
                        yield
                        hs = slice(h * 64, (h + 1) * 64)
                        P.stt(ob[:, hs], otm[:, hs], ssq[:, h:h + 1], row(l, R_OG, 64), ALU.mult, ALU.mult)
                    P.tt(ob[:], ob[:], gateS[:, j, :], ALU.mult)
                    transpose_to(lambda i, j=j: o_all[:, 2 + i, j * 128:(j + 1) * 128],
                                 [ob[:, 0:128], ob[:, 128:256]])


            def rw_gen():
                pb = PR.pb
                X256, X512, tsm = X256r, X512r, tsm_r
                transpose_to = lambda d_, s_: transpose_pool(PR, d_, s_)
                wa2 = mats[:, l, M_WA2:M_WA2 + 256]
                g2 = mats[:, l, M_G2:M_G2 + 256]
                P.act(thx[0:64, :], zcm[0:64, 6, :], AF.Tanh)
                sgx = X512r[0][:, 0:TT]
                P.act(sgx[:], zcm[:, 7, :], AF.Sigmoid)
                for c in range(2):
                    yield
                    ps = pb()
                    P.mm(ps[:, 0:TT], wa2[64:128, c * 128:(c + 1) * 128], zcm[64:128, 6, :])
                    P.act(asig[:, c, :], ps[:, 0:TT], AF.Sigmoid, bias=col(l, C_A0, c), scale=1.0)
                    ps = pb()
                    P.mm(ps[:, 0:TT], g2[:, c * 128:(c + 1) * 128], sgx[:])
                    P.copy(gateC[:, c, :], ps[:, 0:TT])
                for j in range(NSUB):
                    yield
                    sl = slice(j * 128, (j + 1) * 128)
                    def fm(cbase):
                        return zcm[:, cbase:cbase + 2, sl]

                    def t2(i):
                        t = X256[i]
                        return t, t[:].rearrange("p (c t) -> p c t", c=2)
                    kkt, kk3 = t2(0)
                    for c in range(2):
                        yield
                        P.ts(kk3[:, c, :], zcm[:, 2 + c, sl], col(l, C_KK, c), ALU.mult)
                    sq = X256[1]
                    P.act(sq[:], kkt[:], AF.Square)
                    ps = pb()
                    P.mm(ps[:, 0:256], BLK, sq[:])
                    rn = X256[2]
                    rsqrt(rn[:], ps[:, 0:256], C_E6)
                    P.tt(kkt[:], kkt[:], rn[:], ALU.mult)
                    k2t, k23 = t2(3)
                    for c in range(2):
                        yield
                        P.ts(k23[:, c, :], asig[:, c, sl], col(l, C_KA, c), ALU.mult,
                             cols[:, NCOL + 2 * l + c:NCOL + 2 * l + c + 1], ALU.add)
                    P.tt(k23, k23, fm(2), ALU.mult)
                    bvt, bv3 = t2(4)
                    P.tt(bv3, kk3, asig[:, :, sl], ALU.mult)
                    rkt, rk3 = t2(5)
                    for c in range(2):
                        yield
                        P.stt(rk3[:, c, :], zcm[:, c, sl], col(l, C_RK, c), k23[:, c, :], ALU.mult, ALU.mult)
                    psb = pb()
                    P.mm(psb[:, 0:256], BLK, rkt[:])
                    bon, bon3 = t2(11)
                    P.tt(bon3, psb[:, 0:256].rearrange("p (c t) -> p c t", c=2), fm(4), ALU.mult)
                    psl = pb()
                    P.mm(psl[:, 0:256], thx[0:64, sl], wa2[0:64, :])
                    sgT = X256[6]
                    P.tt(sgT[:], psl[:, 0:256], row(l, R_W0, 256), ALU.add)
                    P.act(sgT[:], sgT[:], AF.Sigmoid)
                    psc = pb()
                    for c in range(2):
                        yield
                        P.mm(psc[:, c * 128:(c + 1) * 128], sgT[:, c * 128:(c + 1) * 128], triS[:, 0:128])
                    for c in range(2):
                        yield
                        P.mm(psc[:, 256 + c * 128:256 + (c + 1) * 128], sgT[:, c * 128:(c + 1) * 128], triS[:, 128:256])
                    cum = X512[0]
                    P.copy(cum[:], psc[:, :], eng="act")
                    cum3 = cum[:, 0:256].rearrange("p (c t) -> p c t", c=2)
                    tot = cum3[:, :, 127]
                    gam, gam3 = t2(7)
                    P.act(gam[:], cum[:, 0:256], AF.Exp)
                    igam, igam3 = t2(8)
                    P.act(igam[:], cum[:, 0:256], AF.Exp, scale=-1.0)
                    gamx, gamx3 = t2(9)
                    P.act(gamx[:], cum[:, 256:512], AF.Exp)
                    ghat, ghat3 = t2(10)
                    for c in range(2):
                        yield
                        P.act(ghat3[:, c, :], cum3[:, c, :], AF.Exp, bias=cum3[:, c, 127:128], scale=-1.0)
                    etot = tsm.next()
                    P.act(etot[:, 0:2], tot, AF.Exp)
                    At, At3 = t2(12)
                    P.stt(At[:], kkt[:], -1.0, gamx[:], ALU.mult, ALU.mult)
                    Bt, Bt3 = t2(13)
                    P.tt(Bt[:], bvt[:], igam[:], ALU.mult)
                    Kt, Kt3 = t2(14)
                    P.tt(Kt[:], k2t[:], igam[:], ALU.mult)
                    Rt, Rt3 = t2(15)
                    P.tt(Rt3, fm(0), gam3, ALU.mult)
                    Bh, Bh3 = t2(16)
                    P.tt(Bh[:], bvt[:], ghat[:], ALU.mult)
                    Kh, Kh3 = t2(17)
                    P.tt(Kh[:], k2t[:], ghat[:], ALU.mult)
                    BhT = X256[0]
                    KhT = X256[1]
                    VT = X256[2]
                    transpose_to(lambda i: (BhT, BhT, KhT, KhT)[i][:, (i % 2) * 128:(i % 2) * 128 + 128],
                                 [Bh3[:, 0, :], Bh3[:, 1, :], Kh3[:, 0, :], Kh3[:, 1, :]])
                    AtT = X256[3]
                    transpose_to(lambda i: (VT, VT, AtT, AtT)[i][:, (i % 2) * 128:(i % 2) * 128 + 128],
                                 [zcm[:, 4, sl], zcm[:, 5, sl], At3[:, 0, :], At3[:, 1, :]])
                    for h in range(4):
                        yield
                        b_ = (h % 2) * 64
                        P.copy(AtP[:, h, b_:b_ + 64], AtT[:, h * 64:(h + 1) * 64])
                    def hm(lhs3, rhs3, mask, slot):
                        ps = [pb(), pb()]
                        for h in range(4):
                            c_, b_ = h // 2, (h % 2) * 64
                            P.mm(ps[h % 2][:, c_ * 128:(c_ + 1) * 128], lhs3[b_:b_ + 64, c_, :], rhs3[b_:b_ + 64, c_, :])
                        o = X512[slot]
                        for par in range(2):
                            P.tt(vpar(o[:], par), v2(ps[par][:, 0:256]), b2(mask), ALU.mult)
                        return o
                    Nn = hm(At3, Bt3, MSL, 1)
                    NT = hm(Bt3, At3, MSU, 2)
                    AakT = hm(Kt3, At3, MSU, 3)
                    ArbT = hm(Bt3, Rt3, MUI, 4)
                    ArkT = hm(Kt3, Rt3, MUI, 5)
                    Tt = yield from tri_inv(Nn, NT, tinv_r, pb)
                    psM = pb()
                    for h in range(4):
                        yield
                        P.mm(psM[:, h * 64:(h + 1) * 64], AakT[:, h * 128:(h + 1) * 128], VT[:, h * 64:(h + 1) * 64])
                    M1 = X256[4]
                    P.copy(M1[:], psM[:, 0:256], eng="act")
                    psU = pb()
                    for h in range(4):
                        yield
                        P.mm(psU[:, h * 64:(h + 1) * 64], Tt[:, h * 128:(h + 1) * 128], M1[:, h * 64:(h + 1) * 64])
                    Usb = X256[5]
                    P.copy(Usb[:], psU[:, 0:256])
                    psW = pb()
                    for pr in range(2):
                        yield
                        for hh in range(2):
                            yield
                            h = pr * 2 + hh
                            P.mm(psW[:, pr * 128:(pr + 1) * 128], AtP[:, h, :], Tt[:, h * 128:(h + 1) * 128],
                                 start=(hh == 0), stop=(hh == 1))
                    WT = X256[6]
                    P.copy(WT[:], psW[:, 0:256], eng="act")
                    ytm = X256[7]
                    for pr in range(2):
                        yield
                        prs = slice(pr * 128, (pr + 1) * 128)
                        ps1 = pb()
                        P.mm(ps1[:, 0:128], WT[:, prs], Zblk[:, l, pr, :])
                        Pm = X256[8 + pr]
                        P.tt(Pm[:, 0:128], Usb[:, prs], ps1[:, 0:128], ALU.add)
                        ps2 = pb()
                        P.mm(ps2[:, 0:128], Rt3[:, pr, :], Zblk[:, l, pr, :], start=True, stop=False)
                        for hh in range(2):
                            yield
                            h = pr * 2 + hh
                            P.mm(ps2[:, hh * 64:(hh + 1) * 64], ArbT[:, h * 128:(h + 1) * 128],
                                 Pm[:, hh * 64:(hh + 1) * 64], start=False, stop=False)
                            P.mm(ps2[:, hh * 64:(hh + 1) * 64], ArkT[:, h * 128:(h + 1) * 128],
                                 VT[:, h * 64:(h + 1) * 64], start=False, stop=(hh == 1))
                        P.copy(ytm[:, prs], ps2[:, 0:128], eng="act")
                        ps3 = pb()
                        P.mm(ps3[:, 0:128], BhT[:, prs], Pm[:, 0:128], start=True, stop=False)
                        P.mm(ps3[:, 0:128], KhT[:, prs], VT[:, prs], start=False, stop=True)
                        tm = X256[(10, 18)[pr]]
                        P.tt(tm[:, 0:128], ps3[:, 0:128], BLK, ALU.mult)
                        P.stt(Zblk[:, l, pr, :], Zblk[:, l, pr, :], etot[:, pr:pr + 1], tm[:, 0:128], ALU.mult, ALU.add)
                    y4 = ytm[:].rearrange("p (h d) -> p h d", h=4)
                    s1 = tsm.next()
                    P.reduce(s1[:, 0:4], y4, ALU.add)
                    sq = X256[19]
                    P.tt(sq[:], ytm[:], ytm[:], ALU.mult)
                    s2 = tsm.next()
                    P.reduce(s2[:, 0:4], sq[:].rearrange("p (h d) -> p h d", h=4), ALU.add)
                    mean = tsm.next()
                    P.ts(mean[:, 0:4], s1[:, 0:4], 1.0 / 64, ALU.mult)
                    m2 = tsm.next()
                    P.tt(m2[:, 0:4], mean[:, 0:4], mean[:, 0:4], ALU.mult)
                    var = tsm.next()
                    P.stt(var[:, 0:4], s2[:, 0:4], 1.0 / 64, m2[:, 0:4], ALU.mult, ALU.subtract)
                    rsqrt(var[:, 0:4], var[:, 0:4], C_EX)
                    yn = X256[20]
                    for h in range(4):
                        yield
                        hs = slice(h * 64, (h + 1) * 64)
                        P.ts(yn[:, hs], ytm[:, hs], mean[:, h:h + 1], ALU.subtract, var[:, h:h + 1], ALU.mult)
                    P.tt(yn[:], yn[:], row(l, R_LNG, 256), ALU.mult)
                    P.tt(yn[:], yn[:], row(l, R_LNB, 256), ALU.add)
                    psT = pb()
                    for c in range(2):
                        yield
                        P.tr(psT[:, c * 128:(c + 1) * 128], yn[:, c * 128:(c + 1) * 128], ID)
                    yo = X256[21]
                    P.tt(yo[:], psT[:, 0:256], bon[:], ALU.add)
                    P.tt(o_all[:, 4:6, sl], yo[:].rearrange("p (c t) -> p c t", c=2), gateC[:, :, sl], ALU.mult)


            gd, gr = dn_gen(), rw_gen()
            alive = [gd, gr]
            while alive:
                for g_ in list(alive):
                    try:
                        next(g_)
                    except StopIteration:
                        alive.remove(g_)
                yield
            chk('dn')
            chk('rw')
            if dbg is not None and dbg_d is not None and ti == 0 and l == 0 and dbg[0] == 1024:
                for c in range(8):
                    yield
                    tmp = g512.next()
                    P.copy(tmp[:], o_all[:, c, :])
                    P.dma(dbg_d[c * 128:(c + 1) * 128, :], tmp[:], is_output=True)

            SA.mode = "dense"
            for half in range(2):
                yield
                wO = load_w(wbf["w_out"][l][:, half * 512:(half + 1) * 512], 8, 512)
                for dc in range(4):
                    yield
                    ps = pb()
                    for kc in range(8):
                        yield
                        P.mm(ps[:, 0:TT], wO[:, kc, dc * 128:(dc + 1) * 128], o_all[:, kc, :],
                             start=(kc == 0), stop=(kc == 7))
                    d = half * 4 + dc
                    P.tt(hT[:, d, :], hT[:, d, :], ps[:, 0:TT], ALU.add)


        groups = [[2 * i, 2 * i + 1] for i in range(n_cores // 2)]

        def ccfn(e):
            return e.collective_compute("AllGather", ALU.bypass, replica_groups=groups, ins=[send_d], outs=[recv_d])

        def load_x_tile(step_):
            for s in range(NSUB):
                r0 = step_ * TT + s * 128
                for hf in range(2):
                    P.dma(X512r[2 * s + hf][:], x_d[r0:r0 + 128, hf * 512:(hf + 1) * 512])

        def genB(ti, l, par):
            tok0 = ti * TT
            hT = hTs[par]
            hn = hns[par]
            pb = SB.pb
            g512, w515, wring, xio = SB.g512, SB.w515, SB.wring, SB.xio
            load_w, rmsnorm, transpose_to = make_helpers(SB)

            chk('oproj')
            if ti + 1 <= n_tiles:
                load_x_tile(ti + 1)
            rmsnorm(hT, l * LC + C_NFFN, abuf, out_bf=hn)
            for part in range(2):
                for g in range(6):
                    yield
                    ncl = 2 if g < 5 else 1
                    c0 = part * 11 + g * 2
                    wt = wring.next()
                    vG = wt[:, 0:8 * 128 * ncl].rearrange("p (k n) -> p k n", k=8)
                    vU = wt[:, 2048:2048 + 8 * 128 * ncl].rearrange("p (k n) -> p k n", k=8)
                    P.dma(vG, wbf["w_fg"][l][:, c0 * 128:(c0 + ncl) * 128].rearrange("(k p) n -> p k n", p=128))
                    P.dma(vU, wbf["w_fu"][l][:, c0 * 128:(c0 + ncl) * 128].rearrange("(k p) n -> p k n", p=128))
                    for ci_ in range(ncl):
                        yield
                        c = c0 + ci_
                        psg = pb()
                        for kc in range(8):
                            yield
                            P.mm(psg[:, 0:TT], vG[:, kc, ci_ * 128:(ci_ + 1) * 128], hn[:, kc, :],
                                 start=(kc == 0), stop=(kc == 7))
                        psu = pb()
                        for kc in range(8):
                            yield
                            P.mm(psu[:, 0:TT], vU[:, kc, ci_ * 128:(ci_ + 1) * 128], hn[:, kc, :],
                                 start=(kc == 0), stop=(kc == 7))
                        wk = w515.next()
                        P.copy(wk[:, 2:2 + TT], psg[:, 0:TT], eng="act")
                        P.copy(wk[:, 0:2], ffh[:, l, c, :], eng="pool")
                        acc = g512.next()
                        P.ts(acc[:], wk[:, 2:2 + TT], col(l, C_FFW, c * 3 + 2), ALU.mult)
                        P.stt(acc[:], wk[:, 1:1 + TT], col(l, C_FFW, c * 3 + 1), acc[:], ALU.mult, ALU.add)
                        P.stt(acc[:], wk[:, 0:TT], col(l, C_FFW, c * 3 + 0), acc[:], ALU.mult, ALU.add)
                        P.copy(ffh[:, l, c, :], wk[:, TT:TT + 2], eng="pool")
                        P.act(acc[:], acc[:], AF.Silu)
                        P.tt(abuf[:, c - part * 11, :], acc[:], psu[:, 0:TT], ALU.mult)
                for dg in range(4):
                    yield
                    wd = wring.next()
                    vD = wd[:, 0:11 * 256].rearrange("p (k n) -> p k n", k=11)
                    P.dma(vD, wbf["w_fd"][l][part * 1408:(part + 1) * 1408, dg * 256:(dg + 1) * 256]
                          .rearrange("(k p) n -> p k n", p=128))
                    for dc in range(2):
                        yield
                        ps = pb()
                        for kc in range(11):
                            yield
                            P.mm(ps[:, 0:TT], vD[:, kc, dc * 128:(dc + 1) * 128], abuf[:, kc, :],
                                 start=(kc == 0), stop=(kc == 10))
                        d = dg * 2 + dc
                        P.tt(hT[:, d, :], hT[:, d, :], ps[:, 0:TT], ALU.add)

            chk('ffn')
            rmsnorm(hT, l * LC + C_NPLE, abuf, out_bf=hn)
            pT = abuf
            for s in range(NSUB):
                yield
                pt = xio.next()
                P.dma(pt[:, 0:256], p_d[l, tok0 + s * 128: tok0 + (s + 1) * 128, :])
                transpose_to(lambda i, s=s: pT[:, i, s * 128:(s + 1) * 128], [pt[:, 0:128], pt[:, 128:256]])
            for half in range(2):
                yield
                wGt = load_w(wbf["w_pg"][l][:, half * 512:(half + 1) * 512], 8, 512)
                wP = load_w(wbf["w_pp"][l][:, half * 512:(half + 1) * 512], 2, 512)
                for dc in range(4):
                    yield
                    d = half * 4 + dc
                    psg = pb()
                    for kc in range(8):
                        yield
                        P.mm(psg[:, 0:TT], wGt[:, kc, dc * 128:(dc + 1) * 128], hn[:, kc, :],
                             start=(kc == 0), stop=(kc == 7))
                    psp = pb()
                    for kc in range(2):
                        yield
                        P.mm(psp[:, 0:TT], wP[:, kc, dc * 128:(dc + 1) * 128], pT[:, kc, :], start=(kc == 0), stop=(kc == 1))
                    sg = g512.next()
                    P.act(sg[:], psg[:, 0:TT], AF.Sigmoid)
                    P.tt(sg[:], sg[:], psp[:, 0:TT], ALU.mult)
                    P.tt(hT[:, d, :], hT[:, d, :], sg[:], ALU.add)

            if ti < n_tiles:
                P.dma(send_d, hT[:].rearrange("p c t -> p (c t)"))
                P.emit_cc([recv_d], [send_d], ccfn, cc_sem, 1)
            if ti >= 1:
                otok = (ti - 1) * TT
                if not (dbg is not None and dbg[0] == 1025):
                    rmsnorm(hT, C_FINAL, abuf, out_f32=zcm)
                    fin = zcm
                else:
                    fin = hT
                for s in range(NSUB):
                    yield
                    ot = xio.next()
                    transpose_to(lambda i: ot[:, i * 128:(i + 1) * 128], [fin[:, c, s * 128:(s + 1) * 128] for c in range(8)])
                    P.dma(out_d[otok + s * 128: otok + (s + 1) * 128, :], ot[:], eng="pool", is_output=True)

        try:
            for step in range(n_tiles + 1):
                for _ in genA(step, 0, 0):
                    pass
                for _ in genB(step, 0, 0):
                    pass
                if step == 0:
                    P.ts(ffh[:, 0, :, :], ffh[:, 0, :, :], cc[:, 6:7], ALU.mult)
        except _Stop:
            pass
        P.finish()
    return nc, P


def _colvec(v):
    v = np.asarray(v, np.float32).reshape(-1)
    return v.reshape(-1, 128).T


def pack_shared(inp, role):
    cols = np.zeros((128, NCOL), np.float32)
    rows = np.zeros((1, NROW), np.float32)
    mats = np.zeros((NL, 128, 1024), np.float32)
    l = role
    b = 0
    cols[:, b + C_NMIX:b + C_NMIX + 8] = _colvec(inp["norm_mix_g"][l])
    cols[:, b + C_NFFN:b + C_NFFN + 8] = _colvec(inp["norm_ffn_g"][l])
    cols[:, b + C_NPLE:b + C_NPLE + 8] = _colvec(inp["norm_ple_g"][l])
    cols[:, b + C_DNW:b + C_DNW + 24] = np.asarray(inp["dn_conv_w"][l]).reshape(4, 6, 128).transpose(2, 1, 0).reshape(128, 24)
    cols[:, b + C_MU:b + C_MU + 8] = _colvec(inp["rw_mu"][l])
    cols[:, b + C_A0:b + C_A0 + 2] = _colvec(inp["rw_a0"][l])
    cols[:, b + C_KK:b + C_KK + 2] = _colvec(inp["rw_k_k"][l])
    cols[:, b + C_KA:b + C_KA + 2] = _colvec(inp["rw_k_a"][l])
    cols[:, b + C_RK:b + C_RK + 2] = _colvec(inp["rw_r_k"][l])
    cols[:, b + C_CFW:b + C_CFW + 62] = np.asarray(inp["cf_conv_w"][l]).reshape(31, 2, 128).transpose(2, 1, 0).reshape(128, 62)
    cols[:, b + C_CFB:b + C_CFB + 2] = _colvec(inp["cf_conv_b"][l])
    cols[:, b + C_CFG:b + C_CFG + 2] = _colvec(inp["cf_ln_g"][l])
    cols[:, b + C_CFLB:b + C_CFLB + 2] = _colvec(inp["cf_ln_b"][l])
    cols[:, b + C_FFW:b + C_FFW + 66] = np.asarray(inp["ffn_conv_w"][l]).reshape(3, 22, 128).transpose(2, 1, 0).reshape(128, 66)
    cols[:, b + C_BST:b + C_BST + 4] = np.asarray(inp["gmlp_b_s"][l]).T
    r = 0
    rows[0, r + R_VG:r + R_VG + 256] = inp["gmlp_v_g"][l]
    rows[0, r + R_VB:r + R_VB + 256] = inp["gmlp_v_b"][l]
    rows[0, r + R_ALOG:r + R_ALOG + 4] = inp["dn_a_log"][l]
    rows[0, r + R_DTB:r + R_DTB + 4] = inp["dn_dt_bias"][l]
    rows[0, r + R_OG:r + R_OG + 64] = inp["dn_o_g"][l]
    rows[0, r + R_W0:r + R_W0 + 256] = inp["rw_w0"][l]
    rows[0, r + R_LNG:r + R_LNG + 256] = inp["rw_lnx_g"][l]
    rows[0, r + R_LNB:r + R_LNB + 256] = inp["rw_lnx_b"][l]
    mats[0, 0:64, M_WA2:M_WA2 + 256] = inp["rw_w2"][l]
    mats[0, 64:128, M_WA2:M_WA2 + 256] = inp["rw_a2"][l]
    mats[0, :, M_G2:M_G2 + 256] = inp["rw_g2"][l]
    mats[0, :, M_WST:M_WST + 512] = np.asarray(inp["gmlp_w_s"][l]).transpose(2, 0, 1).reshape(128, 512)
    cols[:, C_FINAL:C_FINAL + 8] = _colvec(inp["final_norm_g"])
    pi = np.arange(128)[:, None]
    fi = np.arange(128)[None, :]
    cst = np.concatenate([(pi == fi), (pi > fi), (pi < fi), (pi <= fi), (pi // 64 == fi // 64),
                          (pi // 64 <= fi // 64)], axis=1).astype(np.float32)
    sh = {"cols_d": cols, "rows_d": rows, "mats_d": mats, "cst_d": np.ascontiguousarray(cst),
          "flag_d": np.full((128, 1), float(role), np.float32)}
    names = {"w_in": "w_in", "w_out": "w_out", "w_fg": "w_ffn_gate", "w_fu": "w_ffn_up", "w_fd": "w_ffn_down",
             "w_pg": "w_ple_gate", "w_pp": "w_ple_proj"}
    for k, src in names.items():
        sh[k] = np.ascontiguousarray(np.asarray(inp[src], np.float32)[l:l + 1])
    return sh


def core_inputs(inputs, shs, c):
    b, role = c // 2, c % 2
    x = np.asarray(inputs["x"], np.float32)
    p = np.asarray(inputs["p"], np.float32)
    m = dict(shs[role])
    if role == 0:
        m["x"] = np.concatenate([x[b], np.zeros((TT, D), np.float32)], axis=0)
        m["p"] = np.concatenate([p[0, b], np.zeros((TT, 256), np.float32)], axis=0)[None]
    else:
        m["x"] = np.zeros((S + TT, D), np.float32)
        m["p"] = np.concatenate([np.zeros((TT, 256), np.float32), p[1, b]], axis=0)[None]
    return m


_CACHE = {}


def kernel(**inputs):
    inputs = {k: np.asarray(v) for k, v in inputs.items()}
    shs = [pack_shared(inputs, 0), pack_shared(inputs, 1)]
    if "nc" not in _CACHE:
        _CACHE["nc"] = build_program()[0]
    nc = _CACHE["nc"]
    in_maps = [core_inputs(inputs, shs, c) for c in range(8)]
    res = run_bass_kernel_spmd(nc, in_maps, core_ids=list(range(8)))
    out = np.stack([res.results[2 * b + 1]["out"] for b in range(4)], axis=0)
    return out.astype(np.float32)
```

```python
import numpy as np
import concourse.bass as bass
import concourse.mybir as mybir

F32 = mybir.dt.float32
BF16 = mybir.dt.bfloat16
ALU = mybir.AluOpType
AF = mybir.ActivationFunctionType
AX = mybir.AxisListType

ENGS = ("pe", "act", "dve", "pool", "sp")
N_DMA_SEM = 24


class Prog:
    def __init__(self, nc):
        self.nc = nc
        self.q = {e: [] for e in ENGS}
        self.cnt = {e: 0 for e in ENGS}
        self.seen = {e: {} for e in ENGS}
        self.acc = {}
        self.esem = {}
        self.dsem = {"sp": [], "pool": [], "act": []}
        self.dma_i = 0
        self.dma_k = {"sp": 0, "pool": 0, "act": 0}
        self.dma_cnt = {}
        self.out_tokens = []
        self.n_wait = 0
        self.track_dram = set()

    def setup_sems(self, stack):
        for e in ("pe", "act", "dve", "pool"):
            self.esem[e] = stack.enter_context(self.nc.semaphore("s_" + e))
        for q, n in (("sp", 16), ("pool", 8)):
            for i in range(n):
                self.dsem[q].append(stack.enter_context(self.nc.semaphore("s_dma_%s%d" % (q, i))))

    @staticmethod
    def _region(ap):
        t = ap.tensor
        shp = list(t.shape)
        rowlen = 1
        for s in shp[1:]:
            rowlen *= s
        off = int(ap.offset)
        apl = ap.ap
        p0 = off // rowlen
        f0 = off % rowlen
        pstep, pcnt = apl[0]
        if pstep == 0:
            p1 = p0 + 1
        else:
            p1 = p0 + (pcnt - 1) * (pstep // rowlen) + 1
        ext = 0
        for st, c in apl[1:]:
            ext += (c - 1) * abs(st)
        if str(ap.space) == "PSUM":
            return (ap.name, p0, p1, 0, rowlen)
        return (ap.name, p0, p1, f0, f0 + ext + 1)

    def _reg2(self, ap):
        if str(ap.space) == "DRAM":
            if ap.name in self.track_dram:
                return (ap.name, 0, 1, 0, 1)
            return None
        return self._region(ap)

    def _need(self, eng, tok, needs):
        if tok is None:
            return
        sem, val, teng, tidx = tok
        if teng == eng:
            if eng == "pe":
                return
        key = id(sem)
        if self.seen[eng].get(key, 0) >= val:
            return
        cur = needs.get(key)
        if cur is None or cur[1] < val:
            needs[key] = (sem, val)

    def emit(self, eng, outs, ins, fn, is_dma=False, is_output=False):
        needs = {}
        regs_in = [r for r in (self._reg2(a) for a in ins if a is not None) if r is not None]
        regs_out = [r for r in (self._reg2(a) for a in outs if a is not None) if r is not None]
        psum_in = [r for r, a in zip(regs_in, [a for a in ins if a is not None and self._reg2(a) is not None])
                   if str(a.space) == "PSUM"]
        if psum_in:
            regs_in = [r for r in regs_in if r not in psum_in]
            regs_out = regs_out + psum_in
        for (name, p0, p1, f0, f1) in regs_in:
            for r in self.acc.get(name, ()):
                if r[5] and r[0] < p1 and p0 < r[1] and r[2] < f1 and f0 < r[3]:
                    self._need(eng, r[4], needs)
        for (name, p0, p1, f0, f1) in regs_out:
            for r in self.acc.get(name, ()):
                if r[0] < p1 and p0 < r[1] and r[2] < f1 and f0 < r[3]:
                    self._need(eng, r[4], needs)
        if is_dma:
            j = self.dma_k[eng] % len(self.dsem[eng])
            self.dma_k[eng] += 1
            self.dma_i += 1
            sem = self.dsem[eng][j]
            prev = self.dma_cnt.get((eng, j), 0)
            if prev > 0:
                if self.seen[eng].get(id(sem), 0) < 16 * prev:
                    needs[id(sem)] = (sem, 16 * prev)
            self.dma_cnt[(eng, j)] = prev + 1
            tok = (sem, 16 * (prev + 1), "dma%s%d" % (eng, j), 0)
            inc = (sem, 16)
        else:
            idx = self.cnt[eng]
            sem = self.esem[eng]
            tok = (sem, idx + 1, eng, idx)
            inc = (sem, 1)
            self.cnt[eng] = idx + 1
        for key, (s, v) in needs.items():
            self.q[eng].append(("w", s, v))
            self.seen[eng][key] = v
            self.n_wait += 1
        self.q[eng].append(("op", fn, inc))
        if is_output:
            self.out_tokens.append(tok)
        for (name, p0, p1, f0, f1) in regs_out:
            lst = self.acc.setdefault(name, [])
            lst[:] = [r for r in lst if not (p0 <= r[0] and r[1] <= p1 and f0 <= r[2] and r[3] <= f1)]
            lst.append((p0, p1, f0, f1, tok, True, eng))
        for (name, p0, p1, f0, f1) in regs_in:
            lst = self.acc.setdefault(name, [])
            if not is_dma:
                lst[:] = [r for r in lst if not ((not r[5]) and r[6] == eng and p0 <= r[0] and r[1] <= p1
                                                 and f0 <= r[2] and r[3] <= f1)]
            lst.append((p0, p1, f0, f1, tok, False, eng if not is_dma else "dma"))

    def mm(self, out, lhsT, rhs, start=True, stop=True):
        self.emit("pe", [out], [lhsT, rhs],
                  lambda e: e.matmul(out, lhsT, rhs, start=start, stop=stop))

    def tr(self, out, in_, ident):
        self.emit("pe", [out], [in_, ident], lambda e: e.transpose(out, in_, ident))

    def act(self, out, in_, func, bias=None, scale=None, accum_out=None, eng="act"):
        kw = {}
        ins = [in_]
        if bias is not None:
            kw["bias"] = bias
            if not isinstance(bias, (int, float)):
                ins.append(bias)
        if scale is not None:
            kw["scale"] = scale
            if not isinstance(scale, (int, float)):
                ins.append(scale)
        outs = [out]
        if accum_out is not None:
            kw["accum_out"] = accum_out
            outs.append(accum_out)
        self.emit("act", outs, ins, lambda e: e.activation(out, in_, func, **kw))

    def tt(self, out, in0, in1, op, eng="dve"):
        self.emit(eng, [out], [in0, in1], lambda e: e.tensor_tensor(out, in0, in1, op))

    def ts(self, out, in0, s1, op0, s2=None, op1=None, eng="dve", accum_out=None):
        ins = [in0] + [s for s in (s1, s2) if s is not None and not isinstance(s, (int, float))]
        outs = [out] + ([accum_out] if accum_out is not None else [])
        kw = {}
        if op1 is not None:
            kw["op1"] = op1
        if accum_out is not None:
            kw["accum_out"] = accum_out
        self.emit(eng, outs, ins, lambda e: e.tensor_scalar(out, in0, s1, s2, op0, **kw))

    def stt(self, out, in0, scalar, in1, op0, op1, eng="dve"):
        eng = "dve"
        ins = [in0, in1] + ([scalar] if not isinstance(scalar, (int, float)) else [])
        self.emit(eng, [out], ins, lambda e: e.scalar_tensor_tensor(out, in0, scalar, in1, op0, op1))

    def copy(self, out, in_, eng="dve"):
        if eng == "act":
            self.emit("act", [out], [in_], lambda e: e.copy(out, in_))
        else:
            self.emit(eng, [out], [in_], lambda e: e.tensor_copy(out, in_))

    def memset(self, ap, val, eng="dve"):
        self.emit(eng, [ap], [], lambda e: e.memset(ap, val))

    def reduce(self, out, in_, op, axis=AX.X, eng="dve"):
        self.emit(eng, [out], [in_], lambda e: e.tensor_reduce(out, in_, axis, op))

    def bn_stats(self, out, in_):
        self.emit("dve", [out], [in_], lambda e: e.bn_stats(out, in_))

    def bn_aggr(self, out, in_):
        self.emit("dve", [out], [in_], lambda e: e.bn_aggr(out, in_))

    def dma(self, out, in_, eng="sp", is_output=False):
        self.emit(eng, [out], [in_], lambda e: e.dma_start(out=out, in_=in_), is_dma=True, is_output=is_output)

    def dma_cast(self, out, in_):
        self.emit("pool", [out], [in_], lambda e: e.dma_start(out=out, in_=in_), is_dma=True)

    def emit_cc(self, outs, ins, fn, sem, inc):
        eng = "pool"
        needs = {}
        regs_in = [r for r in (self._reg2(a) for a in ins) if r is not None]
        regs_out = [r for r in (self._reg2(a) for a in outs) if r is not None]
        for (name, p0, p1, f0, f1) in regs_in:
            for r in self.acc.get(name, ()):
                if r[5] and r[0] < p1 and p0 < r[1] and r[2] < f1 and f0 < r[3]:
                    self._need(eng, r[4], needs)
        for (name, p0, p1, f0, f1) in regs_out:
            for r in self.acc.get(name, ()):
                if r[0] < p1 and p0 < r[1] and r[2] < f1 and f0 < r[3]:
                    self._need(eng, r[4], needs)
        self.cc_n = getattr(self, "cc_n", 0) + 1
        tok = (sem, inc * self.cc_n, "cc", 0)
        for key, (s_, v) in needs.items():
            self.q[eng].append(("w", s_, v))
            self.seen[eng][key] = v
        self.q[eng].append(("op", fn, (sem, inc)))
        for (name, p0, p1, f0, f1) in regs_out:
            lst = self.acc.setdefault(name, [])
            lst[:] = [r for r in lst if not (p0 <= r[0] and r[1] <= p1 and f0 <= r[2] and r[3] <= f1)]
            lst.append((p0, p1, f0, f1, tok, True, "cc"))
        for (name, p0, p1, f0, f1) in regs_in:
            self.acc.setdefault(name, []).append((p0, p1, f0, f1, tok, False, "cc"))

    def wait_token(self, eng, tok):
        needs = {}
        self._need(eng, tok, needs)
        for key, (s, v) in needs.items():
            self.q[eng].append(("w", s, v))
            self.seen[eng][key] = v

    def finish(self):
        for tok in self.out_tokens:
            self.wait_token("sp", tok)
        for e in ("pe", "act", "dve", "pool"):
            if self.cnt[e] > 0:
                self.q["sp"].append(("w", self.esem[e], self.cnt[e]))
        nc = self.nc
        q = self.q

        def replay(name, eng):
            for it in q[name]:
                if it[0] == "w":
                    eng.wait_ge(it[1], it[2])
                else:
                    ins = it[1](eng)
                    ins.then_inc(it[2][0], it[2][1])

        with nc.Block() as block:
            @block.tensor
            def _(e):
                replay("pe", e)

            @block.scalar
            def _(e):
                replay("act", e)

            @block.vector
            def _(e):
                replay("dve", e)

            @block.gpsimd
            def _(e):
                replay("pool", e)

            @block.sync
            def _(e):
                replay("sp", e)

from contextlib import ExitStack
from concourse.bass_utils import run_bass_kernel_spmd

D = 1024
S = 4096
TT = 256
NSUB = TT // 128
DIN = 3080
DFF = 2816
NKF = 22
NL = 2
LC = 202
LR = 1352
C_NMIX, C_NFFN, C_NPLE, C_DNW, C_MU, C_A0, C_KK, C_KA, C_RK = 0, 8, 16, 24, 48, 56, 58, 60, 62
C_CFW, C_CFB, C_CFG, C_CFLB, C_FFW, C_BST = 64, 126, 128, 130, 132, 198
C_FINAL = NL * LC
NCOL = C_FINAL + 8
R_VG, R_VB, R_ALOG, R_DTB, R_OG, R_W0, R_LNG, R_LNB = 0, 256, 512, 516, 520, 584, 840, 1096
NROW = NL * LR
M_WA2, M_G2, M_WST = 0, 256, 512
K_ID, K_MSL, K_MSU, K_MUI, K_BLK, K_GMT = 0, 128, 256, 384, 512, 640
EXPM05 = 0.6065306597126334


class _Stop(Exception):
    pass


def build_program(n_tiles=S // TT, n_layers=1, dbg=None, stop=None, prologue=True, n_cores=8):
    nc = bass.Bass("TRN2", target_bir_lowering=False)
    dt = nc.dram_tensor
    n_layers = 1
    x_d = dt("x", [S + TT, D], F32, kind="ExternalInput").ap()
    p_d = dt("p", [1, S + TT, 256], F32, kind="ExternalInput").ap()
    flag_d = dt("flag_d", [128, 1], F32, kind="ExternalInput").ap()
    send_d = dt("send_d", [128, 8 * TT], F32, kind="Internal").ap()
    recv_d = dt("recv_d", [256, 8 * TT], F32, kind="Internal").ap()
    cols_d = dt("cols_d", [128, NCOL], F32, kind="ExternalInput").ap()
    rows_d = dt("rows_d", [1, NROW], F32, kind="ExternalInput").ap()
    mats_d = dt("mats_d", [NL, 128, 1024], F32, kind="ExternalInput").ap()
    cst_d = dt("cst_d", [128, 768], F32, kind="ExternalInput").ap()
    wshapes = {"w_in": [D, DIN], "w_out": [D, D], "w_fg": [D, DFF], "w_fu": [D, DFF],
               "w_fd": [DFF, D], "w_pg": [D, D], "w_pp": [256, D]}
    w32 = {k: dt(k, [1] + v, F32, kind="ExternalInput").ap() for k, v in wshapes.items()}
    wbf = {k: dt(k + "_bf", [1] + v, BF16, kind="Internal").ap() for k, v in wshapes.items()}
    out_d = dt("out", [S, D], F32, kind="ExternalOutput").ap()
    dbg_d = None
    if dbg is not None:
        dbg_d = dt("dbg", list(dbg), F32, kind="ExternalOutput").ap()

    P = Prog(nc)
    for k in wshapes:
        P.track_dram.add(k + "_bf")
    P.track_dram.update(["send_d", "recv_d"])

    with ExitStack() as st:
        P.setup_sems(st)
        cc_sem = st.enter_context(nc.semaphore("s_cc"))
        sb = lambda n, s, d=F32: st.enter_context(nc.sbuf_tensor(n, s, d))
        banks = [st.enter_context(nc.psum_tensor("ps%d" % i, [128, 512], F32)) for i in range(8)]
        bank_i = [0]

        def pb():
            b = banks[bank_i[0] % 8]
            bank_i[0] += 1
            return b

        class Ring:
            def __init__(self, name, n, shape, d=F32):
                self.t = [sb("%s%d" % (name, i), shape, d) for i in range(n)]
                self.i = 0

            def next(self):
                t = self.t[self.i % len(self.t)]
                self.i += 1
                return t

        cols = sb("cols", [128, NCOL + 8])
        rows = sb("rows", [128, NROW])
        mats = sb("mats", [128, NL, 1024])
        cst = sb("cst", [128, 768])
        cc = sb("cc", [128, 8])
        ones_bf = sb("ones_bf", [128, 128], BF16)
        ones_f = sb("ones_f", [128, 128])
        o256_f = sb("o256_f", [128, 128])
        triS = sb("triS", [128, 256])
        wmT = sb("wmT", [128, NL, 512], BF16)
        nexpA = sb("nexpA", [128, NL, 4])
        hTs = [sb("hT%d" % i, [128, 8, TT]) for i in range(1)]
        hns = [sb("hn%d" % i, [128, 8, TT], BF16) for i in range(1)]
        hT = hTs[0]
        abuf = sb("abuf", [128, 11, TT], BF16)
        o_all = sb("o_all", [128, 8, TT], BF16)
        Sblk = sb("Sblk", [128, NL, 2, 128])
        Zblk = sb("Zblk", [128, NL, 2, 128])
        qkvh = sb("qkvh", [128, NL, 6, 3])
        zch = sb("zch", [128, NL, 8, 1])
        cfh = sb("cfh", [128, NL, 2, 30])
        ffh = sb("ffh", [128, NL, NKF, 2])
        qkvs = sb("qkvs", [128, 6, TT])
        gateS = sb("gateS", [128, NSUB, 256])
        betaT = sb("betaT", [128, NSUB, 4])
        nbetaT = sb("nbetaT", [128, NSUB, 4])
        gT = sb("gT", [128, NSUB, 4])
        zcm = sb("zcm", [128, 8, TT])
        asig = sb("asig", [128, 2, TT])
        gateC = sb("gateC", [128, 2, TT])
        thx = sb("thx", [128, TT])
        hbuf = sb("hbuf", [128, 2, 30 + TT])
        KbeP = sb("KbeP", [128, 4, 128])
        AtP = sb("AtP", [128, 4, 128])
        tinv_d = Ring("tinvd", 6, [128, 512])
        tinv_r = Ring("tinvr", 6, [128, 512])
        X512d = [sb("X512d_%d" % i, [128, 512]) for i in range(8)]
        X512r = [sb("X512r_%d" % i, [128, 512]) for i in range(6)]
        X256d = [sb("X256d_%d" % i, [128, 256]) for i in range(12)]
        X256r = [sb("X256r_%d" % i, [128, 256]) for i in range(22)]
        X512, X256 = X512d, X256d
        tsm = Ring("tsm", 8, [128, 8])
        tsm_d = Ring("tsmd", 10, [128, 8])
        tsm_r = Ring("tsmr", 10, [128, 8])
        tb256 = Ring("tb256", 2, [128, 256], BF16)

        ID = cst[:, K_ID:K_ID + 128]
        MSL = cst[:, K_MSL:K_MSL + 128]
        MSU = cst[:, K_MSU:K_MSU + 128]
        MUI = cst[:, K_MUI:K_MUI + 128]
        BLK = cst[:, K_BLK:K_BLK + 128]
        GMT = cst[:, K_GMT:K_GMT + 128]

        def b4(m):
            return m.unsqueeze(1).to_broadcast([128, 4, 128])

        def v4(t):
            return t.rearrange("p (h f) -> p h f", h=4)

        def vpar(t, par):
            return t.rearrange("p (a b f) -> p a b f", a=2, b=2)[:, :, par, :]

        def b2(m):
            return m.unsqueeze(1).to_broadcast([128, 2, 128])

        def v2(t):
            return t.rearrange("p (h f) -> p h f", h=2)

        C_ONE, C_E6, C_E5, C_EX, C_ZERO = 0, 1, 2, 3, 4

        def ccol(i):
            return cc[:, i:i + 1]

        def col(l, off, i=0, n=1):
            b = l * LC + off + i
            return cols[:, b:b + n]

        def row(l, off, n):
            b = l * LR + off
            return rows[:, b:b + n]

        def recip(out, in_):
            P.emit("dve", [out], [in_], lambda e: e.reciprocal(out, in_))

        def rsqrt(out, in_, eps_col, scale=1.0):
            P.act(out, in_, AF.Sqrt, bias=ccol(eps_col), scale=scale)
            recip(out, out)

        P.dma(cols[:, 0:NCOL], cols_d)
        P.dma(rows[:], rows_d.to_broadcast([128, NROW]))
        P.dma(mats[:], mats_d.rearrange("l p n -> p l n"))
        P.dma(cst[:], cst_d)
        P.dma(cc[:, 5:6], flag_d)
        P.ts(cc[:, 6:7], cc[:, 5:6], -1.0, ALU.mult, 1.0, ALU.add)
        P.memset(cc[:, 0:1], 1.0)
        P.memset(cc[:, 1:2], 1e-6)
        P.memset(cc[:, 2:3], 1e-5)
        P.memset(cc[:, 3:4], 64e-5)
        P.memset(cc[:, 4:5], 0.0)
        P.memset(ones_bf[:], 1.0)
        P.memset(ones_f[:], 1.0)
        P.memset(o256_f[:], 1.0 / 256.0)
        P.ts(triS[:, 0:128], MUI, -EXPM05, ALU.mult)
        P.ts(triS[:, 128:256], MSU, -EXPM05, ALU.mult)
        for l in range(NL):
            P.tt(wmT[:, l, :].rearrange("p (h f) -> p h f", h=4),
                 mats[:, l, M_WST:M_WST + 512].rearrange("p (h f) -> p h f", h=4), b4(GMT), ALU.mult)
            P.act(nexpA[:, l, :], row(l, R_ALOG, 4), AF.Exp)
            P.ts(nexpA[:, l, :], nexpA[:, l, :], -1.0, ALU.mult)
            P.ts(cols[:, NCOL + 2 * l:NCOL + 2 * l + 2], col(l, C_KA, 0, 2), -1.0, ALU.mult, 1.0, ALU.add)
        for t_ in (Sblk, Zblk, qkvh, zch, cfh, ffh, KbeP, AtP):
            P.memset(t_[:], 0.0)

        hT_flat = hT[:].rearrange("p c t -> p (c t)")
        abuf_flat = abuf[:].rearrange("p c t -> p (c t)")
        zcm_flat = zcm[:].rearrange("p c t -> p (c t)")
        oall_flat = o_all[:].rearrange("p c t -> p (c t)")
        W_ = 4 * TT
        stg32 = [hT_flat[:, 0:W_], hT_flat[:, W_:2 * W_], zcm_flat[:, 0:W_], zcm_flat[:, W_:2 * W_]]
        stg16 = [abuf_flat[:, 0:W_], abuf_flat[:, W_:2 * W_], oall_flat[:, 0:W_], oall_flat[:, W_:2 * W_]]
        ci = 0
        for l in range(n_layers if prologue else 0):
            for k, shp in wshapes.items():
                n_el = shp[0] * shp[1] // 128
                src = w32[k][l].rearrange("(p a) n -> p (a n)", p=128)
                dst = wbf[k][l].rearrange("(p a) n -> p (a n)", p=128)
                off = 0
                while off < n_el:
                    w_ = min(4 * TT, n_el - off)
                    stg = stg32[ci % 4][:, 0:w_]
                    P.dma(stg, src[:, off:off + w_])
                    ob = stg16[ci % 4]
                    eng = ("dve", "act")[ci % 2]
                    P.copy(ob[:, 0:w_], stg, eng=eng)
                    P.dma(dst[:, off:off + w_], ob[:, 0:w_], eng="pool")
                    off += w_
                    ci += 1

        class Stream:
            def __init__(self, name, bank_list, n_g, n_w, n_x):
                self.banks = bank_list
                self.bi = 0
                self.g512 = Ring("g512" + name, n_g, [128, TT])
                self.w515 = Ring("w515" + name, 2, [128, 516])
                self.wring = Ring("wb" + name, n_w, [128, 4096], BF16)
                self.xio = Ring("xio" + name, n_x, [128, 1024])
                self.tr_i = 0

            def pb(self):
                b = self.banks[self.bi % len(self.banks)]
                self.bi += 1
                return b

        SA = Stream("A", banks[0:8], 4, 3, 1)
        SB = SA

        class Pool:
            def __init__(self, bl):
                self.banks = bl
                self.bi = 0
                self.tr_i = 0

            def pb(self):
                b = self.banks[self.bi % len(self.banks)]
                self.bi += 1
                return b

        PD = Pool(banks[0:4])
        PR = Pool(banks[4:8])
        PRest = Pool(banks[0:6])
        PConf = Pool(banks[6:8])

        def transpose_pool(S_, dst_fn, src_aps):
            i = 0
            while i < len(src_aps):
                ps = S_.pb()
                n = min(4, len(src_aps) - i)
                for j in range(n):
                    P.tr(ps[:, j * 128:(j + 1) * 128], src_aps[i + j], ID)
                eng_ = ("act", "dve")[S_.tr_i % 2]
                S_.tr_i += 1
                for j in range(n):
                    P.copy(dst_fn(i + j), ps[:, j * 128:(j + 1) * 128], eng=eng_)
                i += n

        def make_helpers(S_):
            def load_w(ap_dram, kc, ncol):
                wt = S_.wring.next()
                v = wt[:, 0:kc * ncol].rearrange("p (k n) -> p k n", k=kc)
                P.dma(v, ap_dram.rearrange("(k p) n -> p k n", p=128))
                return v

            def rmsnorm(hT, gcol_off_layer, sq, out_bf=None, out_f32=None):
                for c in range(8):
                    P.act(sq[:, c, :], hT[:, c, :], AF.Square)
                ps = S_.pb()
                for c in range(8):
                    P.mm(ps[:, 0:TT], ones_bf[:], sq[:, c, :], start=(c == 0), stop=(c == 7))
                rstd = S_.g512.next()
                rsqrt(rstd[:], ps[:, 0:TT], C_E6, scale=1.0 / D)
                for c in range(8):
                    g = cols[:, gcol_off_layer + c:gcol_off_layer + c + 1]
                    o = out_bf[:, c, :] if out_bf is not None else out_f32[:, c, :]
                    P.stt(o, hT[:, c, :], g, rstd[:], ALU.mult, ALU.mult)

            def transpose_to(dst_fn, src_aps):
                i = 0
                while i < len(src_aps):
                    ps = S_.pb()
                    n = min(4, len(src_aps) - i)
                    for j in range(n):
                        P.tr(ps[:, j * 128:(j + 1) * 128], src_aps[i + j], ID)
                    eng_ = ("act", "dve")[S_.tr_i % 2]
                    S_.tr_i += 1
                    for j in range(n):
                        P.copy(dst_fn(i + j), ps[:, j * 128:(j + 1) * 128], eng=eng_)
                    i += n
            return load_w, rmsnorm, transpose_to

        def tri_inv(Nn, NT, tinv, pb):
            Tt = tinv.next()
            P.tt(v4(Tt[:]), v4(NT[:]), b4(ID), ALU.add)
            cN, cNT = Nn, NT
            for k in range(1, 7):
                yield
                psA = pb()
                for h in range(4):
                    hs = slice(h * 128, (h + 1) * 128)
                    P.mm(psA[:, hs], cNT[:, hs], cN[:, hs])
                nN = tinv.next()
                P.copy(nN[:], psA[:, :], eng="act")
                nNT = None
                if k < 6:
                    yield
                    psB = pb()
                    for h in range(4):
                        hs = slice(h * 128, (h + 1) * 128)
                        P.mm(psB[:, hs], cN[:, hs], cNT[:, hs])
                    nNT = tinv.next()
                    P.copy(nNT[:], psB[:, :], eng="dve")
                yield
                psC = pb()
                for h in range(4):
                    hs = slice(h * 128, (h + 1) * 128)
                    P.mm(psC[:, hs], nN[:, hs], Tt[:, hs])
                nT = tinv.next()
                P.tt(nT[:], Tt[:], psC[:, :], ALU.add)
                Tt = nT
                cN, cNT = nN, nNT
            return Tt

        def chk(name):
            if stop == name:
                raise _Stop()

        def genA(ti, l, par):
            tok0 = ti * TT
            hT = hTs[par]
            hn = hns[par]
            pb = SA.pb
            g512, w515, wring, xio = SA.g512, SA.w515, SA.wring, SA.xio
            load_w, rmsnorm, transpose_to = make_helpers(SA)
            SA.mode = "dense"
            if l == 0:

                if ti == 0:
                    load_x_tile(0)
                for s in range(NSUB):
                    yield
                    transpose_to(lambda i, s=s: hT[:, i, s * 128:(s + 1) * 128],
                                 [X512r[2 * s + c // 4][:, (c % 4) * 128:(c % 4 + 1) * 128] for c in range(8)])

            w_in = wbf["w_in"][l]
            if ti >= 1:
                rcv = X512[0:4]
                for q in range(4):
                    P.dma(rcv[q][:], recv_d[0:128, q * 512:(q + 1) * 512])
                hflat = hT[:].rearrange("p c t -> p (c t)")
                for q in range(4):
                    yield
                    P.stt(hflat[:, q * 512:(q + 1) * 512], rcv[q][:], cc[:, 5:6], hflat[:, q * 512:(q + 1) * 512],
                          ALU.mult, ALU.add)
            chk('x')
            rmsnorm(hT, l * LC + C_NMIX, o_all, out_bf=hn)

            wD = load_w(w_in[:, 2568:3080], 8, 512)
            for c in range(2):
                yield
                ps1 = pb()
                for kc in range(8):
                    yield
                    P.mm(ps1[:, 0:TT], wD[:, kc, c * 128:(c + 1) * 128], hn[:, kc, :], start=(kc == 0), stop=(kc == 7))
                ps2 = pb()
                for kc in range(8):
                    yield
                    P.mm(ps2[:, 0:TT], wD[:, kc, 256 + c * 128:256 + (c + 1) * 128], hn[:, kc, :],
                         start=(kc == 0), stop=(kc == 7))
                sg = g512.next()
                P.act(sg[:], ps2[:, 0:TT], AF.Sigmoid)
                P.copy(hbuf[:, c, 0:30], cfh[:, l, c, :], eng="pool")
                P.tt(hbuf[:, c, 30:30 + TT], ps1[:, 0:TT], sg[:], ALU.mult)
                P.copy(cfh[:, l, c, :], hbuf[:, c, TT:TT + 30], eng="pool")

            def rest_gen():
                pb = PRest.pb
                transpose_to = lambda d_, s_: transpose_pool(PRest, d_, s_)
                wA = load_w(w_in[:, 0:512], 8, 512)
                for s in range(NSUB):
                    yield
                    ss = slice(s * 128, (s + 1) * 128)
                    ps = pb()
                    for kc in range(8):
                        yield
                        P.mm(ps[:, :], hn[:, kc, ss], wA[:, kc, :], start=(kc == 0), stop=(kc == 7))
                    xz = X512r[3]
                    P.copy(xz[:], ps[:, :], eng="act")
                    x2 = X512r[4]
                    P.tt(x2[:], xz[:], xz[:], ALU.mult)
                    P.ts(x2[:], x2[:], 0.044715, ALU.mult, 1.0, ALU.add)
                    P.tt(x2[:], x2[:], xz[:], ALU.mult)
                    P.act(x2[:], x2[:], AF.Sigmoid, scale=1.5957691216057308)
                    gl = X512r[5]
                    P.tt(gl[:], xz[:], x2[:], ALU.mult)
                    st6 = tsm.next()
                    P.bn_stats(st6[:, 0:6], gl[:, 256:512])
                    mv = tsm.next()
                    P.bn_aggr(mv[:, 0:2], st6[:, 0:6])
                    rs = tsm.next()
                    rsqrt(rs[:, 0:1], mv[:, 1:2], C_E5)
                    vn = X256[0]
                    P.ts(vn[:], gl[:, 256:512], mv[:, 0:1], ALU.subtract, rs[:, 0:1], ALU.mult)
                    P.tt(vn[:], vn[:], row(l, R_VG, 256), ALU.mult)
                    vnb = tb256.next()
                    P.tt(vnb[:], vn[:], row(l, R_VB, 256), ALU.add)
                    ps2 = pb()
                    for h in range(4):
                        yield
                        P.mm(ps2[:, h * 64:(h + 1) * 64], wmT[:, l, h * 128:(h + 1) * 128], vnb[:, h * 64:(h + 1) * 64])
                    oa = X256[1]
                    for h in range(4):
                        yield
                        hs = slice(h * 64, (h + 1) * 64)
                        P.stt(oa[:, hs], ps2[:, hs], col(l, C_BST, h), gl[:, hs], ALU.add, ALU.mult)
                    transpose_to(lambda i, s=s: o_all[:, i, s * 128:(s + 1) * 128],
                                 [oa[:, 0:128], oa[:, 128:256]])

                chk('A')
                for half in range(2):
                    yield
                    wB = load_w(w_in[:, 512 + half * 384: 512 + (half + 1) * 384], 8, 384)
                    for cc_ in range(3):
                        yield
                        c = half * 3 + cc_
                        ps = pb()
                        for kc in range(8):
                            yield
                            P.mm(ps[:, 0:TT], wB[:, kc, cc_ * 128:(cc_ + 1) * 128], hn[:, kc, :],
                                 start=(kc == 0), stop=(kc == 7))
                        wk = w515.next()
                        P.copy(wk[:, 3:3 + TT], ps[:, 0:TT], eng="act")
                        P.copy(wk[:, 0:3], qkvh[:, l, c, :], eng="pool")
                        acc = g512.next()
                        P.ts(acc[:], wk[:, 3:3 + TT], col(l, C_DNW, c * 4 + 3), ALU.mult)
                        for k in (2, 1, 0):
                            yield
                            P.stt(acc[:], wk[:, k:k + TT], col(l, C_DNW, c * 4 + k), acc[:], ALU.mult, ALU.add)
                        P.copy(qkvh[:, l, c, :], wk[:, TT:TT + 3], eng="pool")
                        P.act(qkvs[:, c, :], acc[:], AF.Silu)

                chk('B')
                wG = load_w(w_in[:, 1280:1544], 8, 264)
                for s in range(NSUB):
                    yield
                    ss = slice(s * 128, (s + 1) * 128)
                    ps = pb()
                    for kc in range(8):
                        yield
                        P.mm(ps[:, 0:264], hn[:, kc, ss], wG[:, kc, :], start=(kc == 0), stop=(kc == 7))
                    P.act(gateS[:, s, :], ps[:, 0:256], AF.Silu)
                    P.act(betaT[:, s, :], ps[:, 256:260], AF.Sigmoid)
                    P.ts(nbetaT[:, s, :], betaT[:, s, :], -1.0, ALU.mult)
                    sp_ = tsm.next()
                    P.tt(sp_[:, 0:4], ps[:, 260:264], row(l, R_DTB, 4), ALU.add)
                    P.act(sp_[:, 0:4], sp_[:, 0:4], AF.Exp)
                    P.act(sp_[:, 0:4], sp_[:, 0:4], AF.Ln, bias=ccol(C_ONE), scale=1.0)
                    P.tt(gT[:, s, :], sp_[:, 0:4], nexpA[:, l, :], ALU.mult)

                chk('B2')
                for half in range(2):
                    yield
                    wC = load_w(w_in[:, 1544 + half * 512: 1544 + (half + 1) * 512], 8, 512)
                    for cc_ in range(4):
                        yield
                        c = half * 4 + cc_
                        ps = pb()
                        for kc in range(8):
                            yield
                            P.mm(ps[:, 0:TT], wC[:, kc, cc_ * 128:(cc_ + 1) * 128], hn[:, kc, :],
                                 start=(kc == 0), stop=(kc == 7))
                        wk = w515.next()
                        P.copy(wk[:, 1:1 + TT], ps[:, 0:TT], eng="act")
                        P.copy(wk[:, 0:1], zch[:, l, c, :], eng="pool")
                        d_ = g512.next()
                        P.tt(d_[:], wk[:, 0:TT], wk[:, 1:1 + TT], ALU.subtract)
                        P.stt(zcm[:, c, :], d_[:], col(l, C_MU, c), wk[:, 1:1 + TT], ALU.mult, ALU.add)
                        P.copy(zch[:, l, c, :], wk[:, TT:TT + 1], eng="pool")


            def conf_gen():
                pb = PConf.pb
                accs = []
                for c in range(2):
                    yield
                    accA = X512[2 * c][:, 0:TT]
                    P.ts(accA[:], hbuf[:, c, 0:TT], col(l, C_CFW, c * 31 + 0), ALU.mult, col(l, C_CFB, c), ALU.add)
                    for k in range(1, 31):
                        yield
                        P.stt(accA[:], hbuf[:, c, k:k + TT], col(l, C_CFW, c * 31 + k), accA[:], ALU.mult, ALU.add)
                    accs.append(accA)
                psm = pb()
                pss = pb()
                for c in range(2):
                    yield
                    P.mm(psm[:, 0:TT], o256_f[:], accs[c][:], start=(c == 0), stop=(c == 1))
                sqs = []
                for c in range(2):
                    yield
                    sq = X512[2 * c + 1][:, 0:TT]
                    P.act(sq[:], accs[c][:], AF.Square)
                    sqs.append(sq)
                for c in range(2):
                    yield
                    P.mm(pss[:, 0:TT], o256_f[:], sqs[c][:], start=(c == 0), stop=(c == 1))
                mean = X512[4][:, 0:TT]
                P.copy(mean[:], psm[:, 0:TT], eng="act")
                var = X512[5][:, 0:TT]
                P.tt(var[:], mean[:], mean[:], ALU.mult)
                P.tt(var[:], pss[:, 0:TT], var[:], ALU.subtract)
                rsqrt(var[:], var[:], C_E5)
                for c in range(2):
                    yield
                    t1 = X512[6 + c][:, 0:TT]
                    P.tt(t1[:], accs[c][:], mean[:], ALU.subtract)
                    P.tt(t1[:], t1[:], var[:], ALU.mult)
                    P.ts(t1[:], t1[:], col(l, C_CFG, c), ALU.mult, col(l, C_CFLB, c), ALU.add)
                    P.act(o_all[:, 6 + c, :], t1[:], AF.Silu)


            gc_, gr_ = conf_gen(), rest_gen()
            c_alive = r_alive = True
            while c_alive or r_alive:
                for _ in range(4):
                    if r_alive:
                        try:
                            next(gr_)
                        except StopIteration:
                            r_alive = False
                if c_alive:
                    try:
                        next(gc_)
                    except StopIteration:
                        c_alive = False
                yield
            SA.mode = "chain"
            chk('conf')
            def dn_gen():
                pb = PD.pb
                X256, X512, tsm = X256d, X512d, tsm_d
                transpose_to = lambda d_, s_: transpose_pool(PD, d_, s_)
                for c in range(4):
                    yield
                    sq = g512.next()
                    P.act(sq[:], qkvs[:, c, :], AF.Square)
                    ps = pb()
                    P.mm(ps[:, 0:TT], BLK, sq[:])
                    rn = g512.next()
                    rsqrt(rn[:], ps[:, 0:TT], C_E6)
                    if c < 2:
                        P.stt(qkvs[:, c, :], qkvs[:, c, :], 0.125, rn[:], ALU.mult, ALU.mult)
                    else:
                        P.tt(qkvs[:, c, :], qkvs[:, c, :], rn[:], ALU.mult)
                for j in range(NSUB):
                    yield
                    sl = slice(j * 128, (j + 1) * 128)
                    KV = X512[0]
                    transpose_to(lambda i: KV[:, i * 128:(i + 1) * 128], [qkvs[:, 2 + i, sl] for i in range(4)])
                    Ktm = KV[:, 0:256]
                    Vtm = KV[:, 256:512]
                    psg = pb()
                    P.mm(psg[:, 0:4], MUI, gT[:, j, :])
                    gcum = tsm.next()
                    P.copy(gcum[:, 0:4], psg[:, 0:4])
                    dGg, dGb = X512[6], X512[7]
                    for h in range(4):
                        yield
                        P.ts(dGg[:, h * 128:(h + 1) * 128], ID, gcum[:, h:h + 1], ALU.mult)
                        P.ts(dGb[:, h * 128:(h + 1) * 128], ID, betaT[:, j, h:h + 1], ALU.mult)
                    psG = pb()
                    P.mm(psG[:, :], ones_f[:], dGg[:])
                    psB = pb()
                    P.mm(psB[:, :], ones_f[:], dGb[:])
                    Grow = X512[1]
                    P.copy(Grow[:], psG[:, :], eng="act")
                    D1 = X512[2]
                    for h in range(4):
                        yield
                        hs = slice(h * 128, (h + 1) * 128)
                        P.ts(D1[:, hs], Grow[:, hs], -1.0, ALU.mult, gcum[:, h:h + 1], ALU.add)
                    Esl = X512[3]
                    P.tt(v4(Esl[:]), v4(D1[:]), b4(MSL), ALU.mult)
                    P.act(Esl[:], Esl[:], AF.Exp)
                    P.tt(v4(Esl[:]), v4(Esl[:]), b4(MSL), ALU.mult)
                    Eui = X512[4]
                    P.tt(v4(Eui[:]), v4(D1[:]), b4(MUI), ALU.mult)
                    P.act(Eui[:], Eui[:], AF.Exp, scale=-1.0)
                    P.tt(v4(Eui[:]), v4(Eui[:]), b4(MUI), ALU.mult)
                    EB = X512[2]
                    P.tt(v4(EB[:]), v4(Eui[:]), b4(MSU), ALU.mult)
                    P.stt(EB[:], EB[:], -1.0, psB[:, :], ALU.mult, ALU.mult)
                    Erow = X512[5]
                    P.act(Erow[:], Grow[:], AF.Exp)
                    psKK = [pb(), pb()]
                    psKQ = [pb(), pb()]
                    for h in range(4):
                        yield
                        c_, b_ = h // 2, (h % 2) * 64
                        P.mm(psKK[h % 2][:, c_ * 128:(c_ + 1) * 128], qkvs[b_:b_ + 64, 2 + c_, sl], qkvs[b_:b_ + 64, 2 + c_, sl])
                    for h in range(4):
                        yield
                        c_, b_ = h // 2, (h % 2) * 64
                        P.mm(psKQ[h % 2][:, c_ * 128:(c_ + 1) * 128], qkvs[b_:b_ + 64, 2 + c_, sl], qkvs[b_:b_ + 64, c_, sl])
                    Nn = X512[6]
                    for h in range(4):
                        yield
                        hs = slice(h * 128, (h + 1) * 128)
                        P.stt(Nn[:, hs], psKK[h % 2][:, (h // 2) * 128:(h // 2 + 1) * 128], nbetaT[:, j, h:h + 1],
                              Esl[:, hs], ALU.mult, ALU.mult)
                    NT = X512[7]
                    for par in range(2):
                        yield
                        P.tt(vpar(NT[:], par), v2(psKK[par][:, 0:256]), vpar(EB[:], par), ALU.mult)
                    attnT = X512[3]
                    for par in range(2):
                        yield
                        P.tt(vpar(attnT[:], par), v2(psKQ[par][:, 0:256]), vpar(Eui[:], par), ALU.mult)
                    Tt = yield from tri_inv(Nn, NT, tinv_d, pb)
                    eg = tsm.next()
                    P.act(eg[:, 0:4], gcum[:, 0:4], AF.Exp)
                    P.tt(eg[:, 0:4], eg[:, 0:4], betaT[:, j, :], ALU.mult)
                    Vb = X256[0]
                    for h in range(4):
                        yield
                        hs = slice(h * 64, (h + 1) * 64)
                        P.ts(Vb[:, hs], Vtm[:, hs], betaT[:, j, h:h + 1], ALU.mult)
                        P.ts(KbeP[:, h, (h % 2) * 64:(h % 2) * 64 + 64], Ktm[:, hs], eg[:, h:h + 1], ALU.mult)
                    psU = pb()
                    for h in range(4):
                        yield
                        hs = slice(h * 128, (h + 1) * 128)
                        P.mm(psU[:, h * 64:(h + 1) * 64], Tt[:, hs], Vb[:, h * 64:(h + 1) * 64])
                    Usb = X256[1]
                    P.copy(Usb[:], psU[:, 0:256], eng="act")
                    psW = pb()
                    for pr in range(2):
                        yield
                        for hh in range(2):
                            yield
                            h = pr * 2 + hh
                            P.mm(psW[:, pr * 128:(pr + 1) * 128], KbeP[:, h, :], Tt[:, h * 128:(h + 1) * 128],
                                 start=(hh == 0), stop=(hh == 1))
                    WT = X256[2]
                    P.copy(WT[:], psW[:, 0:256])
                    Qd = X256[3]
                    for pr in range(2):
                        yield
                        for hh in range(2):
                            yield
                            h = pr * 2 + hh
                            b_ = hh * 64
                            P.tt(Qd[b_:b_ + 64, pr * 128:(pr + 1) * 128], qkvs[b_:b_ + 64, pr, sl],
                                 Erow[b_:b_ + 64, h * 128:(h + 1) * 128], ALU.mult)
                    glast = Grow[:].rearrange("p (h f) -> p h f", h=4)[:, :, 127]
                    kds = tsm.next()
                    P.tt(kds[:, 0:4], glast, gcum[:, 0:4], ALU.subtract)
                    P.act(kds[:, 0:4], kds[:, 0:4], AF.Exp)
                    egl = tsm.next()
                    P.act(egl[:, 0:4], glast, AF.Exp)
                    Kd = X256[4]
                    for h in range(4):
                        yield
                        hs = slice(h * 64, (h + 1) * 64)
                        P.ts(Kd[:, hs], Ktm[:, hs], kds[:, h:h + 1], ALU.mult)
                    otm = X256[5]
                    for pr in range(2):
                        yield
                        prs = slice(pr * 128, (pr + 1) * 128)
                        ps1 = pb()
                        P.mm(ps1[:, 0:128], WT[:, prs], Sblk[:, l, pr, :])
                        vnew = X256[6 + pr]
                        P.tt(vnew[:, 0:128], Usb[:, prs], ps1[:, 0:128], ALU.subtract)
                        ps2 = pb()
                        P.mm(ps2[:, 0:128], Qd[:, prs], Sblk[:, l, pr, :], start=True, stop=False)
                        for hh in range(2):
                            yield
                            h = pr * 2 + hh
                            P.mm(ps2[:, hh * 64:(hh + 1) * 64], attnT[:, h * 128:(h + 1) * 128],
                                 vnew[:, hh * 64:(hh + 1) * 64], start=False, stop=(hh == 1))
                        P.copy(otm[:, prs], ps2[:, 0:128], eng="act")
                        ps3 = pb()
                        P.mm(ps3[:, 0:128], Kd[:, prs], vnew[:, 0:128])
                        tm = X256[8 + pr]
                        P.tt(tm[:, 0:128], ps3[:, 0:128], BLK, ALU.mult)
                        for hh in range(2):
                            yield
                            h = pr * 2 + hh
                            b_ = hh * 64
                            P.stt(Sblk[b_:b_ + 64, l, pr, :], Sblk[b_:b_ + 64, l, pr, :], egl[b_:b_ + 64, h:h + 1],
                                  tm[b_:b_ + 64, 0:128], ALU.mult, ALU.add)
                    sq = X256[10]
                    P.tt(sq[:], otm[:], otm[:], ALU.mult)
                    ssq = tsm.next()
                    P.reduce(ssq[:, 0:4], sq[:].rearrange("p (h d) -> p h d", h=4), ALU.add)
                    rsqrt(ssq[:, 0:4], ssq[:, 0:4], C_E6, scale=1.0 / 64)
                    ob = X256[11]
                    for h in range(4):
                        yield
                        hs = slice(h * 64, (h + 1) * 64)
                        P.stt(ob[:, hs], otm[:, hs], ssq[:, h:h + 1], row(l, R_OG, 64), ALU.mult, ALU.mult)
                    P.tt(ob[:], ob[:], gateS[:, j, :], ALU.mult)
                    transpose_to(lambda i, j=j: o_all[:, 2 + i, j * 128:(j + 1) * 128],
                                 [ob[:, 0:128], ob[:, 128:256]])


            def rw_gen():
                pb = PR.pb
                X256, X512, tsm = X256r, X512r, tsm_r
                transpose_to = lambda d_, s_: transpose_pool(PR, d_, s_)
                wa2 = mats[:, l, M_WA2:M_WA2 + 256]
                g2 = mats[:, l, M_G2:M_G2 + 256]
                P.act(thx[0:64, :], zcm[0:64, 6, :], AF.Tanh)
                sgx = X512r[0][:, 0:TT]
                P.act(sgx[:], zcm[:, 7, :], AF.Sigmoid)
                for c in range(2):
                    yield
                    ps = pb()
                    P.mm(ps[:, 0:TT], wa2[64:128, c * 128:(c + 1) * 128], zcm[64:128, 6, :])
                    P.act(asig[:, c, :], ps[:, 0:TT], AF.Sigmoid, bias=col(l, C_A0, c), scale=1.0)
                    ps = pb()
                    P.mm(ps[:, 0:TT], g2[:, c * 128:(c + 1) * 128], sgx[:])
                    P.copy(gateC[:, c, :], ps[:, 0:TT])
                for j in range(NSUB):
                    yield
                    sl = slice(j * 128, (j + 1) * 128)
                    def fm(cbase):
                        return zcm[:, cbase:cbase + 2, sl]

                    def t2(i):
                        t = X256[i]
                        return t, t[:].rearrange("p (c t) -> p c t", c=2)
                    kkt, kk3 = t2(0)
                    for c in range(2):
                        yield
                        P.ts(kk3[:, c, :], zcm[:, 2 + c, sl], col(l, C_KK, c), ALU.mult)
                    sq = X256[1]
                    P.act(sq[:], kkt[:], AF.Square)
                    ps = pb()
                    P.mm(ps[:, 0:256], BLK, sq[:])
                    rn = X256[2]
                    rsqrt(rn[:], ps[:, 0:256], C_E6)
                    P.tt(kkt[:], kkt[:], rn[:], ALU.mult)
                    k2t, k23 = t2(3)
                    for c in range(2):
                        yield
                        P.ts(k23[:, c, :], asig[:, c, sl], col(l, C_KA, c), ALU.mult,
                             cols[:, NCOL + 2 * l + c:NCOL + 2 * l + c + 1], ALU.add)
                    P.tt(k23, k23, fm(2), ALU.mult)
                    bvt, bv3 = t2(4)
                    P.tt(bv3, kk3, asig[:, :, sl], ALU.mult)
                    rkt, rk3 = t2(5)
                    for c in range(2):
                        yield
                        P.stt(rk3[:, c, :], zcm[:, c, sl], col(l, C_RK, c), k23[:, c, :], ALU.mult, ALU.mult)
                    psb = pb()
                    P.mm(psb[:, 0:256], BLK, rkt[:])
                    bon, bon3 = t2(11)
                    P.tt(bon3, psb[:, 0:256].rearrange("p (c t) -> p c t", c=2), fm(4), ALU.mult)
                    psl = pb()
                    P.mm(psl[:, 0:256], thx[0:64, sl], wa2[0:64, :])
                    sgT = X256[6]
                    P.tt(sgT[:], psl[:, 0:256], row(l, R_W0, 256), ALU.add)
                    P.act(sgT[:], sgT[:], AF.Sigmoid)
                    psc = pb()
                    for c in range(2):
                        yield
                        P.mm(psc[:, c * 128:(c + 1) * 128], sgT[:, c * 128:(c + 1) * 128], triS[:, 0:128])
                    for c in range(2):
                        yield
                        P.mm(psc[:, 256 + c * 128:256 + (c + 1) * 128], sgT[:, c * 128:(c + 1) * 128], triS[:, 128:256])
                    cum = X512[0]
                    P.copy(cum[:], psc[:, :], eng="act")
                    cum3 = cum[:, 0:256].rearrange("p (c t) -> p c t", c=2)
                    tot = cum3[:, :, 127]
                    gam, gam3 = t2(7)
                    P.act(gam[:], cum[:, 0:256], AF.Exp)
                    igam, igam3 = t2(8)
                    P.act(igam[:], cum[:, 0:256], AF.Exp, scale=-1.0)
                    gamx, gamx3 = t2(9)
                    P.act(gamx[:], cum[:, 256:512], AF.Exp)
                    ghat, ghat3 = t2(10)
                    for c in range(2):
                        yield
                        P.act(ghat3[:, c, :], cum3[:, c, :], AF.Exp, bias=cum3[:, c, 127:128], scale=-1.0)
                    etot = tsm.next()
                    P.act(etot[:, 0:2], tot, AF.Exp)
                    At, At3 = t2(12)
                    P.stt(At[:], kkt[:], -1.0, gamx[:], ALU.mult, ALU.mult)
                    Bt, Bt3 = t2(13)
                    P.tt(Bt[:], bvt[:], igam[:], ALU.mult)
                    Kt, Kt3 = t2(14)
                    P.tt(Kt[:], k2t[:], igam[:], ALU.mult)
                    Rt, Rt3 = t2(15)
                    P.tt(Rt3, fm(0), gam3, ALU.mult)
                    Bh, Bh3 = t2(16)
                    P.tt(Bh[:], bvt[:], ghat[:], ALU.mult)
                    Kh, Kh3 = t2(17)
                    P.tt(Kh[:], k2t[:], ghat[:], ALU.mult)
                    BhT = X256[0]
                    KhT = X256[1]
                    VT = X256[2]
                    transpose_to(lambda i: (BhT, BhT, KhT, KhT)[i][:, (i % 2) * 128:(i % 2) * 128 + 128],
                                 [Bh3[:, 0, :], Bh3[:, 1, :], Kh3[:, 0, :], Kh3[:, 1, :]])
                    AtT = X256[3]
                    transpose_to(lambda i: (VT, VT, AtT, AtT)[i][:, (i % 2) * 128:(i % 2) * 128 + 128],
                                 [zcm[:, 4, sl], zcm[:, 5, sl], At3[:, 0, :], At3[:, 1, :]])
                    for h in range(4):
                        yield
                        b_ = (h % 2) * 64
                        P.copy(AtP[:, h, b_:b_ + 64], AtT[:, h * 64:(h + 1) * 64])
                    def hm(lhs3, rhs3, mask, slot):
                        ps = [pb(), pb()]
                        for h in range(4):
                            c_, b_ = h // 2, (h % 2) * 64
                            P.mm(ps[h % 2][:, c_ * 128:(c_ + 1) * 128], lhs3[b_:b_ + 64, c_, :], rhs3[b_:b_ + 64, c_, :])
                        o = X512[slot]
                        for par in range(2):
                            P.tt(vpar(o[:], par), v2(ps[par][:, 0:256]), b2(mask), ALU.mult)
                        return o
                    Nn = hm(At3, Bt3, MSL, 1)
                    NT = hm(Bt3, At3, MSU, 2)
                    AakT = hm(Kt3, At3, MSU, 3)
                    ArbT = hm(Bt3, Rt3, MUI, 4)
                    ArkT = hm(Kt3, Rt3, MUI, 5)
                    Tt = yield from tri_inv(Nn, NT, tinv_r, pb)
                    psM = pb()
                    for h in range(4):
                        yield
                        P.mm(psM[:, h * 64:(h + 1) * 64], AakT[:, h * 128:(h + 1) * 128], VT[:, h * 64:(h + 1) * 64])
                    M1 = X256[4]
                    P.copy(M1[:], psM[:, 0:256], eng="act")
                    psU = pb()
                    for h in range(4):
                        yield
                        P.mm(psU[:, h * 64:(h + 1) * 64], Tt[:, h * 128:(h + 1) * 128], M1[:, h * 64:(h + 1) * 64])
                    Usb = X256[5]
                    P.copy(Usb[:], psU[:, 0:256])
                    psW = pb()
                    for pr in range(2):
                        yield
                        for hh in range(2):
                            yield
                            h = pr * 2 + hh
                            P.mm(psW[:, pr * 128:(pr + 1) * 128], AtP[:, h, :], Tt[:, h * 128:(h + 1) * 128],
                                 start=(hh == 0), stop=(hh == 1))
                    WT = X256[6]
                    P.copy(WT[:], psW[:, 0:256], eng="act")
                    ytm = X256[7]
                    for pr in range(2):
                        yield
                        prs = slice(pr * 128, (pr + 1) * 128)
                        ps1 = pb()
                        P.mm(ps1[:, 0:128], WT[:, prs], Zblk[:, l, pr, :])
                        Pm = X256[8 + pr]
                        P.tt(Pm[:, 0:128], Usb[:, prs], ps1[:, 0:128], ALU.add)
                        ps2 = pb()
                        P.mm(ps2[:, 0:128], Rt3[:, pr, :], Zblk[:, l, pr, :], start=True, stop=False)
                        for hh in range(2):
                            yield
                            h = pr * 2 + hh
                            P.mm(ps2[:, hh * 64:(hh + 1) * 64], ArbT[:, h * 128:(h + 1) * 128],
                                 Pm[:, hh * 64:(hh + 1) * 64], start=False, stop=False)
                            P.mm(ps2[:, hh * 64:(hh + 1) * 64], ArkT[:, h * 128:(h + 1) * 128],
                                 VT[:, h * 64:(h + 1) * 64], start=False, stop=(hh == 1))
                        P.copy(ytm[:, prs], ps2[:, 0:128], eng="act")
                        ps3 = pb()
                        P.mm(ps3[:, 0:128], BhT[:, prs], Pm[:, 0:128], start=True, stop=False)
                        P.mm(ps3[:, 0:128], KhT[:, prs], VT[:, prs], start=False, stop=True)
                        tm = X256[(10, 18)[pr]]
                        P.tt(tm[:, 0:128], ps3[:, 0:128], BLK, ALU.mult)
                        P.stt(Zblk[:, l, pr, :], Zblk[:, l, pr, :], etot[:, pr:pr + 1], tm[:, 0:128], ALU.mult, ALU.add)
                    y4 = ytm[:].rearrange("p (h d) -> p h d", h=4)
                    s1 = tsm.next()
                    P.reduce(s1[:, 0:4], y4, ALU.add)
                    sq = X256[19]
                    P.tt(sq[:], ytm[:], ytm[:], ALU.mult)
                    s2 = tsm.next()
                    P.reduce(s2[:, 0:4], sq[:].rearrange("p (h d) -> p h d", h=4), ALU.add)
                    mean = tsm.next()
                    P.ts(mean[:, 0:4], s1[:, 0:4], 1.0 / 64, ALU.mult)
                    m2 = tsm.next()
                    P.tt(m2[:, 0:4], mean[:, 0:4], mean[:, 0:4], ALU.mult)
                    var = tsm.next()
                    P.stt(var[:, 0:4], s2[:, 0:4], 1.0 / 64, m2[:, 0:4], ALU.mult, ALU.subtract)
                    rsqrt(var[:, 0:4], var[:, 0:4], C_EX)
                    yn = X256[20]
                    for h in range(4):
                        yield
                        hs = slice(h * 64, (h + 1) * 64)
                        P.ts(yn[:, hs], ytm[:, hs], mean[:, h:h + 1], ALU.subtract, var[:, h:h + 1], ALU.mult)
                    P.tt(yn[:], yn[:], row(l, R_LNG, 256), ALU.mult)
                    P.tt(yn[:], yn[:], row(l, R_LNB, 256), ALU.add)
                    psT = pb()
                    for c in range(2):
                        yield
                        P.tr(psT[:, c * 128:(c + 1) * 128], yn[:, c * 128:(c + 1) * 128], ID)
                    yo = X256[21]
                    P.tt(yo[:], psT[:, 0:256], bon[:], ALU.add)
                    P.tt(o_all[:, 4:6, sl], yo[:].rearrange("p (c t) -> p c t", c=2), gateC[:, :, sl], ALU.mult)


            gd, gr = dn_gen(), rw_gen()
            alive = [gd, gr]
            while alive:
                for g_ in list(alive):
                    try:
                        next(g_)
                    except StopIteration:
                        alive.remove(g_)
                yield
            chk('dn')
            chk('rw')
            if dbg is not None and dbg_d is not None and ti == 0 and l == 0 and dbg[0] == 1024:
                for c in range(8):
                    yield
                    tmp = g512.next()
                    P.copy(tmp[:], o_all[:, c, :])
                    P.dma(dbg_d[c * 128:(c + 1) * 128, :], tmp[:], is_output=True)

            SA.mode = "dense"
            for half in range(2):
                yield
                wO = load_w(wbf["w_out"][l][:, half * 512:(half + 1) * 512], 8, 512)
                for dc in range(4):
                    yield
                    ps = pb()
                    for kc in range(8):
                        yield
                        P.mm(ps[:, 0:TT], wO[:, kc, dc * 128:(dc + 1) * 128], o_all[:, kc, :],
                             start=(kc == 0), stop=(kc == 7))
                    d = half * 4 + dc
                    P.tt(hT[:, d, :], hT[:, d, :], ps[:, 0:TT], ALU.add)


        groups = [[2 * i, 2 * i + 1] for i in range(n_cores // 2)]

        def ccfn(e):
            return e.collective_compute("AllGather", ALU.bypass, replica_groups=groups, ins=[send_d], outs=[recv_d])

        def load_x_tile(step_):
            for s in range(NSUB):
                r0 = step_ * TT + s * 128
                for hf in range(2):
                    P.dma(X512r[2 * s + hf][:], x_d[r0:r0 + 128, hf * 512:(hf + 1) * 512])

        def genB(ti, l, par):
            tok0 = ti * TT
            hT = hTs[par]
            hn = hns[par]
            pb = SB.pb
            g512, w515, wring, xio = SB.g512, SB.w515, SB.wring, SB.xio
            load_w, rmsnorm, transpose_to = make_helpers(SB)

            chk('oproj')
            if ti + 1 <= n_tiles:
                load_x_tile(ti + 1)
            rmsnorm(hT, l * LC + C_NFFN, abuf, out_bf=hn)
            for part in range(2):
                for g in range(6):
                    yield
                    ncl = 2 if g < 5 else 1
                    c0 = part * 11 + g * 2
                    wt = wring.next()
                    vG = wt[:, 0:8 * 128 * ncl].rearrange("p (k n) -> p k n", k=8)
                    vU = wt[:, 2048:2048 + 8 * 128 * ncl].rearrange("p (k n) -> p k n", k=8)
                    P.dma(vG, wbf["w_fg"][l][:, c0 * 128:(c0 + ncl) * 128].rearrange("(k p) n -> p k n", p=128))
                    P.dma(vU, wbf["w_fu"][l][:, c0 * 128:(c0 + ncl) * 128].rearrange("(k p) n -> p k n", p=128))
                    for ci_ in range(ncl):
                        yield
                        c = c0 + ci_
                        psg = pb()
                        for kc in range(8):
                            yield
                            P.mm(psg[:, 0:TT], vG[:, kc, ci_ * 128:(ci_ + 1) * 128], hn[:, kc, :],
                                 start=(kc == 0), stop=(kc == 7))
                        psu = pb()
                        for kc in range(8):
                            yield
                            P.mm(psu[:, 0:TT], vU[:, kc, ci_ * 128:(ci_ + 1) * 128], hn[:, kc, :],
                                 start=(kc == 0), stop=(kc == 7))
                        wk = w515.next()
                        P.copy(wk[:, 2:2 + TT], psg[:, 0:TT], eng="act")
                        P.copy(wk[:, 0:2], ffh[:, l, c, :], eng="pool")
                        acc = g512.next()
                        P.ts(acc[:], wk[:, 2:2 + TT], col(l, C_FFW, c * 3 + 2), ALU.mult)
                        P.stt(acc[:], wk[:, 1:1 + TT], col(l, C_FFW, c * 3 + 1), acc[:], ALU.mult, ALU.add)
                        P.stt(acc[:], wk[:, 0:TT], col(l, C_FFW, c * 3 + 0), acc[:], ALU.mult, ALU.add)
                        P.copy(ffh[:, l, c, :], wk[:, TT:TT + 2], eng="pool")
                        P.act(acc[:], acc[:], AF.Silu)
                        P.tt(abuf[:, c - part * 11, :], acc[:], psu[:, 0:TT], ALU.mult)
                for dg in range(4):
                    yield
                    wd = wring.next()
                    vD = wd[:, 0:11 * 256].rearrange("p (k n) -> p k n", k=11)
                    P.dma(vD, wbf["w_fd"][l][part * 1408:(part + 1) * 1408, dg * 256:(dg + 1) * 256]
                          .rearrange("(k p) n -> p k n", p=128))
                    for dc in range(2):
                        yield
                        ps = pb()
                        for kc in range(11):
                            yield
                            P.mm(ps[:, 0:TT], vD[:, kc, dc * 128:(dc + 1) * 128], abuf[:, kc, :],
                                 start=(kc == 0), stop=(kc == 10))
                        d = dg * 2 + dc
                        P.tt(hT[:, d, :], hT[:, d, :], ps[:, 0:TT], ALU.add)

            chk('ffn')
            rmsnorm(hT, l * LC + C_NPLE, abuf, out_bf=hn)
            pT = abuf
            for s in range(NSUB):
                yield
                pt = xio.next()
                P.dma(pt[:, 0:256], p_d[l, tok0 + s * 128: tok0 + (s + 1) * 128, :])
                transpose_to(lambda i, s=s: pT[:, i, s * 128:(s + 1) * 128], [pt[:, 0:128], pt[:, 128:256]])
            for half in range(2):
                yield
                wGt = load_w(wbf["w_pg"][l][:, half * 512:(half + 1) * 512], 8, 512)
                wP = load_w(wbf["w_pp"][l][:, half * 512:(half + 1) * 512], 2, 512)
                for dc in range(4):
                    yield
                    d = half * 4 + dc
                    psg = pb()
                    for kc in range(8):
                        yield
                        P.mm(psg[:, 0:TT], wGt[:, kc, dc * 128:(dc + 1) * 128], hn[:, kc, :],
                             start=(kc == 0), stop=(kc == 7))
                    psp = pb()
                    for kc in range(2):
                        yield
                        P.mm(psp[:, 0:TT], wP[:, kc, dc * 128:(dc + 1) * 128], pT[:, kc, :], start=(kc == 0), stop=(kc == 1))
                    sg = g512.next()
                    P.act(sg[:], psg[:, 0:TT], AF.Sigmoid)
                    P.tt(sg[:], sg[:], psp[:, 0:TT], ALU.mult)
                    P.tt(hT[:, d, :], hT[:, d, :], sg[:], ALU.add)

            if ti < n_tiles:
                P.dma(send_d, hT[:].rearrange("p c t -> p (c t)"))
                P.emit_cc([recv_d], [send_d], ccfn, cc_sem, 1)
            if ti >= 1:
                otok = (ti - 1) * TT
                if not (dbg is not None and dbg[0] == 1025):
                    rmsnorm(hT, C_FINAL, abuf, out_f32=zcm)
                    fin = zcm
                else:
                    fin = hT
                for s in range(NSUB):
                    yield
                    ot = xio.next()
                    transpose_to(lambda i: ot[:, i * 128:(i + 1) * 128], [fin[:, c, s * 128:(s + 1) * 128] for c in range(8)])
                    P.dma(out_d[otok + s * 128: otok + (s + 1) * 128, :], ot[:], eng="pool", is_output=True)

        try:
            for step in range(n_tiles + 1):
                for _ in genA(step, 0, 0):
                    pass
                for _ in genB(step, 0, 0):
                    pass
                if step == 0:
                    P.ts(ffh[:, 0, :, :], ffh[:, 0, :, :], cc[:, 6:7], ALU.mult)
        except _Stop:
            pass
        P.finish()
    return nc, P


def _colvec(v):
    v = np.asarray(v, np.float32).reshape(-1)
    return v.reshape(-1, 128).T


def pack_shared(inp, role):
    cols = np.zeros((128, NCOL), np.float32)
    rows = np.zeros((1, NROW), np.float32)
    mats = np.zeros((NL, 128, 1024), np.float32)
    l = role
    b = 0
    cols[:, b + C_NMIX:b + C_NMIX + 8] = _colvec(inp["norm_mix_g"][l])
    cols[:, b + C_NFFN:b + C_NFFN + 8] = _colvec(inp["norm_ffn_g"][l])
    cols[:, b + C_NPLE:b + C_NPLE + 8] = _colvec(inp["norm_ple_g"][l])
    cols[:, b + C_DNW:b + C_DNW + 24] = np.asarray(inp["dn_conv_w"][l]).reshape(4, 6, 128).transpose(2, 1, 0).reshape(128, 24)
    cols[:, b + C_MU:b + C_MU + 8] = _colvec(inp["rw_mu"][l])
    cols[:, b + C_A0:b + C_A0 + 2] = _colvec(inp["rw_a0"][l])
    cols[:, b + C_KK:b + C_KK + 2] = _colvec(inp["rw_k_k"][l])
    cols[:, b + C_KA:b + C_KA + 2] = _colvec(inp["rw_k_a"][l])
    cols[:, b + C_RK:b + C_RK + 2] = _colvec(inp["rw_r_k"][l])
    cols[:, b + C_CFW:b + C_CFW + 62] = np.asarray(inp["cf_conv_w"][l]).reshape(31, 2, 128).transpose(2, 1, 0).reshape(128, 62)
    cols[:, b + C_CFB:b + C_CFB + 2] = _colvec(inp["cf_conv_b"][l])
    cols[:, b + C_CFG:b + C_CFG + 2] = _colvec(inp["cf_ln_g"][l])
    cols[:, b + C_CFLB:b + C_CFLB + 2] = _colvec(inp["cf_ln_b"][l])
    cols[:, b + C_FFW:b + C_FFW + 66] = np.asarray(inp["ffn_conv_w"][l]).reshape(3, 22, 128).transpose(2, 1, 0).reshape(128, 66)
    cols[:, b + C_BST:b + C_BST + 4] = np.asarray(inp["gmlp_b_s"][l]).T
    r = 0
    rows[0, r + R_VG:r + R_VG + 256] = inp["gmlp_v_g"][l]
    rows[0, r + R_VB:r + R_VB + 256] = inp["gmlp_v_b"][l]
    rows[0, r + R_ALOG:r + R_ALOG + 4] = inp["dn_a_log"][l]
    rows[0, r + R_DTB:r + R_DTB + 4] = inp["dn_dt_bias"][l]
    rows[0, r + R_OG:r + R_OG + 64] = inp["dn_o_g"][l]
    rows[0, r + R_W0:r + R_W0 + 256] = inp["rw_w0"][l]
    rows[0, r + R_LNG:r + R_LNG + 256] = inp["rw_lnx_g"][l]
    rows[0, r + R_LNB:r + R_LNB + 256] = inp["rw_lnx_b"][l]
    mats[0, 0:64, M_WA2:M_WA2 + 256] = inp["rw_w2"][l]
    mats[0, 64:128, M_WA2:M_WA2 + 256] = inp["rw_a2"][l]
    mats[0, :, M_G2:M_G2 + 256] = inp["rw_g2"][l]
    mats[0, :, M_WST:M_WST + 512] = np.asarray(inp["gmlp_w_s"][l]).transpose(2, 0, 1).reshape(128, 512)
    cols[:, C_FINAL:C_FINAL + 8] = _colvec(inp["final_norm_g"])
    pi = np.arange(128)[:, None]
    fi = np.arange(128)[None, :]
    cst = np.concatenate([(pi == fi), (pi > fi), (pi < fi), (pi <= fi), (pi // 64 == fi // 64),
                          (pi // 64 <= fi // 64)], axis=1).astype(np.float32)
    sh = {"cols_d": cols, "rows_d": rows, "mats_d": mats, "cst_d": np.ascontiguousarray(cst),
          "flag_d": np.full((128, 1), float(role), np.float32)}
    names = {"w_in": "w_in", "w_out": "w_out", "w_fg": "w_ffn_gate", "w_fu": "w_ffn_up", "w_fd": "w_ffn_down",
             "w_pg": "w_ple_gate", "w_pp": "w_ple_proj"}
    for k, src in names.items():
        sh[k] = np.ascontiguousarray(np.asarray(inp[src], np.float32)[l:l + 1])
    return sh


def core_inputs(inputs, shs, c):
    b, role = c // 2, c % 2
    x = np.asarray(inputs["x"], np.float32)
    p = np.asarray(inputs["p"], np.float32)
    m = dict(shs[role])
    if role == 0:
        m["x"] = np.concatenate([x[b], np.zeros((TT, D), np.float32)], axis=0)
        m["p"] = np.concatenate([p[0, b], np.zeros((TT, 256), np.float32)], axis=0)[None]
    else:
        m["x"] = np.zeros((S + TT, D), np.float32)
        m["p"] = np.concatenate([np.zeros((TT, 256), np.float32), p[1, b]], axis=0)[None]
    return m


_CACHE = {}


def kernel(**inputs):
    inputs = {k: np.asarray(v) for k, v in inputs.items()}
    shs = [pack_shared(inputs, 0), pack_shared(inputs, 1)]
    if "nc" not in _CACHE:
        _CACHE["nc"] = build_program()[0]
    nc = _CACHE["nc"]
    in_maps = [core_inputs(inputs, shs, c) for c in range(8)]
    res = run_bass_kernel_spmd(nc, in_maps, core_ids=list(range(8)))
    out = np.stack([res.results[2 * b + 1]["out"] for b in range(4)], axis=0)
    return out.astype(np.float32)
```
